# Optimizing a Trainium2 kernel written in Bass

```python
import math
import jax, jax.numpy as jnp
from jax import lax
import numpy as np

D_MODEL = 2048
BATCH = 8
SEQ = 4096
DEPTH = 4

MIX_WIDTH = D_MODEL
HEAD_DIM = 128
DIFF_WIDTH = MIX_WIDTH // 2
CONV_WIDTH = MIX_WIDTH // 4
MEM_WIDTH = MIX_WIDTH - DIFF_WIDTH - CONV_WIDTH
DIFF_HEADS = DIFF_WIDTH // HEAD_DIM
DIFF_QK_DIM = HEAD_DIM // 2
CONV_GROUPS = CONV_WIDTH // HEAD_DIM
CONV_TAPS = 3
MEM_HEADS = 4
MEM_HEAD_DIM = MEM_WIDTH // MEM_HEADS
MEM_TOKENS = 256
D_FF = 4 * D_MODEL
Q_BLOCK = 128
IN_WIDTH = 3 * DIFF_WIDTH + 3 * CONV_WIDTH + MEM_WIDTH
EPS = 1e-6

kernel_name = "hybrid_diffattn_shortconv_memxattn_encoder"


def rms_norm(x, g):
    xf = x.astype(jnp.float32)
    y = xf * lax.rsqrt(jnp.mean(xf * xf, axis=-1, keepdims=True) + EPS)
    return (y * g.astype(jnp.float32)).astype(x.dtype)


def alibi_slopes(n_heads):
    return jnp.asarray([2.0 ** (-8.0 * (i + 1) / n_heads) for i in range(n_heads)], dtype=jnp.float32)


def diff_attention(q, k, v, slopes, lam, lambda_init, g_sub):
    bsz, seq = q.shape[0], q.shape[1]
    n_blk = seq // Q_BLOCK
    scale = DIFF_QK_DIM ** -0.5
    q_blocks = q.reshape(bsz, n_blk, Q_BLOCK, DIFF_HEADS, 2, DIFF_QK_DIM).transpose(1, 0, 2, 3, 4, 5)
    starts = jnp.arange(n_blk, dtype=jnp.int32) * Q_BLOCK
    k_pos = jnp.arange(seq, dtype=jnp.float32)

    def one_block(args):
        q_blk, start = args
        q_pos = (start + jnp.arange(Q_BLOCK, dtype=jnp.int32)).astype(jnp.float32)
        dist = jnp.abs(q_pos[:, None] - k_pos[None, :])
        s = jnp.einsum('bqhcd,bkhcd->bhcqk', q_blk, k).astype(jnp.float32) * scale
        s = s - slopes[None, :, None, None, None] * dist[None, None, None]
        p = jax.nn.softmax(s, axis=-1)
        a = p[:, :, 0] - lam * p[:, :, 1]
        return jnp.einsum('bhqk,bkhd->bqhd', a.astype(v.dtype), v)

    o = lax.map(one_block, (q_blocks, starts))
    o = o.transpose(1, 0, 2, 3, 4).reshape(bsz, seq, DIFF_HEADS, 2 * DIFF_QK_DIM)
    o = rms_norm(o, g_sub) * (1.0 - lambda_init)
    return o.reshape(bsz, seq, DIFF_WIDTH)


def short_conv(u, w):
    return lax.conv_general_dilated(
        u, w[:, None, :].astype(u.dtype), window_strides=(1,), padding=[(1, 1)],
        dimension_numbers=('NWC', 'WIO', 'NWC'), feature_group_count=u.shape[-1])


def memory_attention(q, mem_n, w_kv, g_q, g_k):
    bsz, seq = q.shape[0], q.shape[1]
    kv = mem_n @ w_kv
    km, vm = jnp.split(kv, 2, axis=-1)
    km = rms_norm(km.reshape(bsz, -1, MEM_HEADS, MEM_HEAD_DIM), g_k)
    vm = vm.reshape(bsz, -1, MEM_HEADS, MEM_HEAD_DIM)
    qh = rms_norm(q.reshape(bsz, seq, MEM_HEADS, MEM_HEAD_DIM), g_q)
    s = jnp.einsum('bqhd,bmhd->bhqm', qh, km).astype(jnp.float32) * (MEM_HEAD_DIM ** -0.5)
    p = jax.nn.softmax(s, axis=-1)
    o = jnp.einsum('bhqm,bmhd->bqhd', p.astype(vm.dtype), vm)
    return o.reshape(bsz, seq, MEM_WIDTH)


def setup_inputs(seed: int = 0) -> dict:
    key = jax.random.key(seed)
    ks = jax.random.split(key, 24)
    f32 = jnp.float32

    def nrm(k, shape, std):
        return jax.random.normal(k, shape, dtype=f32) * std

    def gain(k, shape):
        return 1.0 + 0.02 * jax.random.normal(k, shape, dtype=f32)

    return {
        "x": nrm(ks[0], (BATCH, SEQ, D_MODEL), 1.0),
        "mem": nrm(ks[1], (BATCH, MEM_TOKENS, D_MODEL), 1.0),
        "g_mix": gain(ks[2], (DEPTH, D_MODEL)),
        "w_in": nrm(ks[3], (DEPTH, D_MODEL, IN_WIDTH), D_MODEL ** -0.5),
        "g_q_diff": gain(ks[4], (DEPTH, DIFF_QK_DIM)),
        "g_k_diff": gain(ks[5], (DEPTH, DIFF_QK_DIM)),
        "lam_q1": nrm(ks[6], (DEPTH, DIFF_QK_DIM), 0.1),
        "lam_k1": nrm(ks[7], (DEPTH, DIFF_QK_DIM), 0.1),
        "lam_q2": nrm(ks[8], (DEPTH, DIFF_QK_DIM), 0.1),
        "lam_k2": nrm(ks[9], (DEPTH, DIFF_QK_DIM), 0.1),
        "g_subln": gain(ks[10], (DEPTH, 2 * DIFF_QK_DIM)),
        "conv_w": nrm(ks[11], (DEPTH, CONV_TAPS, CONV_WIDTH), CONV_TAPS ** -0.5),
        "g_conv_out": gain(ks[12], (DEPTH, CONV_WIDTH)),
        "g_mem": gain(ks[13], (DEPTH, D_MODEL)),
        "w_mem_kv": nrm(ks[14], (DEPTH, D_MODEL, 2 * MEM_WIDTH), D_MODEL ** -0.5),
        "g_q_mem": gain(ks[15], (DEPTH, MEM_HEAD_DIM)),
        "g_k_mem": gain(ks[16], (DEPTH, MEM_HEAD_DIM)),
        "w_out": nrm(ks[17], (DEPTH, MIX_WIDTH, D_MODEL), MIX_WIDTH ** -0.5),
        "g_mlp": gain(ks[18], (DEPTH, D_MODEL)),
        "w_up": nrm(ks[19], (DEPTH, D_MODEL, D_FF), D_MODEL ** -0.5),
        "w_down": nrm(ks[20], (DEPTH, D_FF, D_MODEL), D_FF ** -0.5),
    }


def reference(x, mem, g_mix, w_in, g_q_diff, g_k_diff, lam_q1, lam_k1, lam_q2, lam_k2,
              g_subln, conv_w, g_conv_out, g_mem, w_mem_kv, g_q_mem, g_k_mem,
              w_out, g_mlp, w_up, w_down):
    bsz, seq = x.shape[0], x.shape[1]
    slopes = alibi_slopes(DIFF_HEADS)
    splits = [DIFF_WIDTH, 2 * DIFF_WIDTH, 3 * DIFF_WIDTH,
              3 * DIFF_WIDTH + CONV_WIDTH, 3 * DIFF_WIDTH + 2 * CONV_WIDTH,
              3 * DIFF_WIDTH + 3 * CONV_WIDTH]
    for l in range(DEPTH):
        h = rms_norm(x, g_mix[l])
        proj = h @ w_in[l]
        q_d, k_d, v_d, gate_b, gate_c, u, q_m = jnp.split(proj, splits, axis=-1)

        q_d = rms_norm(q_d.reshape(bsz, seq, DIFF_HEADS, 2, DIFF_QK_DIM), g_q_diff[l])
        k_d = rms_norm(k_d.reshape(bsz, seq, DIFF_HEADS, 2, DIFF_QK_DIM), g_k_diff[l])
        v_d = v_d.reshape(bsz, seq, DIFF_HEADS, 2 * DIFF_QK_DIM)
        lambda_init = 0.8 - 0.6 * math.exp(-0.3 * l)
        lam = (jnp.exp(jnp.sum(lam_q1[l].astype(jnp.float32) * lam_k1[l].astype(jnp.float32)))
               - jnp.exp(jnp.sum(lam_q2[l].astype(jnp.float32) * lam_k2[l].astype(jnp.float32)))
               + lambda_init)
        a_out = diff_attention(q_d, k_d, v_d, slopes, lam, lambda_init, g_subln[l])

        c_out = gate_b * short_conv(gate_c * u, conv_w[l])
        c_out = rms_norm(c_out.reshape(bsz, seq, CONV_GROUPS, HEAD_DIM),
                         g_conv_out[l].reshape(CONV_GROUPS, HEAD_DIM)).reshape(bsz, seq, CONV_WIDTH)

        mem_n = rms_norm(mem, g_mem[l])
        m_out = memory_attention(q_m, mem_n, w_mem_kv[l], g_q_mem[l], g_k_mem[l])

        mixed = jnp.concatenate([a_out, c_out, m_out], axis=-1)
        x = x + mixed @ w_out[l]

        h = rms_norm(x, g_mlp[l])
        x = x + jnp.square(jax.nn.relu(h @ w_up[l])) @ w_down[l]
    return x
```

```python
import math
import numpy as np
import concourse.bass as bass
import concourse.mybir as mybir
from concourse.bass_utils import run_bass_kernel_spmd

F32 = mybir.dt.float32
BF16 = mybir.dt.bfloat16
AF = mybir.ActivationFunctionType
ALU = mybir.AluOpType
AX = mybir.AxisListType

P = 128
D = 2048
DFF = 8192
INW = 5120
NH = 8
T = 512
MEM = 256
KC = D // P
EPS = 1e-6
ENGS = ("pe", "act", "dve", "pool", "sp")


class Ev:
    __slots__ = ("eng", "is_dma", "needs", "sem", "val")

    def __init__(self, eng, is_dma):
        self.eng = eng
        self.is_dma = is_dma
        self.needs = False
        self.sem = None
        self.val = 0


class Tracker:
    def __init__(self):
        self.ops = {e: [] for e in ENGS}
        self.state = {}
        self.last = {e: None for e in ENGS}
        self.pending_dma = {}
        self.dma_cnt = {}

    def _deps(self, eng, reads, writes, is_dma):
        deps = []
        for k in reads:
            st = self.state.get(k)
            if st is not None and st[0] is not None:
                deps.append(st[0])
        for k in writes:
            st = self.state.get(k)
            if st is not None:
                if st[0] is not None:
                    deps.append(st[0])
                deps.extend(st[1].values())
                deps.extend(st[2])
        out = []
        seen = set()
        for d in deps:
            if id(d) in seen:
                continue
            seen.add(id(d))
            if d.is_dma or is_dma or d.eng != eng:
                d.needs = True
                out.append(d)
        return out

    def _update(self, ev, reads, writes):
        for k in reads:
            st = self.state.get(k)
            if st is None:
                st = [None, {}, []]
                self.state[k] = st
            if ev.is_dma:
                st[2].append(ev)
            else:
                st[1][ev.eng] = ev
        for k in writes:
            self.state[k] = [ev, {}, []]

    def emit(self, eng, fn, reads=(), writes=()):
        deps = self._deps(eng, reads, writes, False)
        ev = Ev(eng, False)
        self.ops[eng].append((deps, fn, ev))
        self._update(ev, reads, writes)
        self.last[eng] = ev
        return ev

    def dma(self, eng, pairs, sem, reads=(), writes=(), **kw):
        deps = self._deps(eng, reads, writes, True)
        ev = Ev(eng, True)
        ev.sem = sem
        self.dma_cnt[sem] = self.dma_cnt.get(sem, 0) + 16 * len(pairs)
        ev.val = self.dma_cnt[sem]

        def fn(e, pairs=pairs, kw=kw):
            return [e.dma_start(out=o, in_=i, **kw) for (o, i) in pairs]

        self.ops[eng].append((deps, fn, ev))
        self._update(ev, reads, writes)
        self.pending_dma[sem] = ev
        return ev

    def barrier(self):
        evs = [e for e in self.last.values() if e is not None] + list(self.pending_dma.values())
        for eng in ENGS:
            deps = []
            for d in evs:
                if d.is_dma or d.eng != eng:
                    d.needs = True
                    deps.append(d)
            self.ops[eng].append((deps, None, None))
        self.pending_dma = {}
        self.state = {}

    def finalize(self, engsem):
        for eng in ENGS:
            cnt = 0
            for (_, _, ev) in self.ops[eng]:
                if ev is not None and not ev.is_dma:
                    ev.sem = engsem[eng]
                    if ev.needs:
                        cnt += 1
                        ev.val = cnt

    def replay(self, engname, eng):
        waited = {}
        for (deps, fn, ev) in self.ops[engname]:
            for d in deps:
                key = id(d.sem)
                if waited.get(key, 0) < d.val:
                    eng.wait_ge(d.sem, d.val)
                    waited[key] = d.val
            if fn is None:
                continue
            ins = fn(eng)
            if ev.is_dma:
                for i in ins:
                    i.then_inc(ev.sem, 16)
            elif ev.needs:
                ins.then_inc(ev.sem, 1)


class Arena:
    def __init__(self, nc, lo, hi):
        self.nc = nc
        self.lo = lo
        self.hi = hi
        self.ptr = lo
        self.n = 0
        self.off = {}

    def alloc(self, name, shape, dtype):
        esz = 4 if dtype == F32 else 2
        nbytes = esz
        for s in shape[1:]:
            nbytes *= s
        off = (self.ptr + 31) // 32 * 32
        assert off + nbytes <= self.hi, f"SBUF overflow allocating {name}: {off}+{nbytes} > {self.hi}"
        self.ptr = off + nbytes
        self.n += 1
        self.off[name] = off
        return self.nc.alloc_sbuf_tensor_at(f"{name}_{self.n}", list(shape), dtype, offset=off)

    def mark(self):
        return self.ptr

    def release(self, m):
        self.ptr = m


def lambda_init(l):
    return 0.8 - 0.6 * math.exp(-0.3 * l)


def build_program(S, L, skip_thresh=None, debug=False):
    NT = S // T
    NKT = S // P
    NQB = S // 512
    nc = bass.Bass("TRN2", target_bir_lowering=False)
    tr = Tracker()

    def din(name, shape):
        return nc.dram_tensor(name, list(shape), F32, kind="ExternalInput").ap()

    x_d = din("x", [S, D])
    mem_d = din("mem", [MEM, D])
    w_in_d = din("w_in", [L, D, INW])
    w_kv_d = din("w_mem_kv", [L, D, 1024])
    w_out_d = din("w_out", [L, D, D])
    w_up_d = din("w_up", [L, D, DFF])
    w_down_d = din("w_down", [L, DFF, D])
    vecs_d = din("vecs", [P, 3 * L * KC + 5 * L + L * 12 + L * 4])
    lam_d = [din(n, [1, L * 64]) for n in ("lam_q1", "lam_k1", "lam_q2", "lam_k2")]
    ident_d = din("c_ident", [P, P])
    kaug_d = din("c_kaug", [4, S])
    qaug_d = din("c_qaug", [NH, 2, 4, S])
    diagb_d = din("c_diagb", [NH, P, P])
    out_d = nc.dram_tensor("out", [S, D], F32, kind="ExternalOutput").ap()

    def scr(name, shape, dt):
        kind = "ExternalOutput" if (debug and not name.startswith("wb_")) else "Internal"
        return nc.dram_tensor(name, list(shape), dt, kind=kind).ap()

    wb = {
        "in": scr("wb_in", [L, 10, P, 8192], BF16),
        "kv": scr("wb_kv", [L, 2, P, 8192], BF16),
        "out": scr("wb_out", [L, 4, P, 8192], BF16),
        "up": scr("wb_up", [L, 16, P, 8192], BF16),
        "down": scr("wb_down", [L, 16, P, 8192], BF16),
    }
    xres = scr("xres", [P, KC, S], F32)
    qT_s = scr("qT_s", [NH, P, S], BF16)
    kT_s = scr("kT_s", [NH, P, S], BF16)
    v_s = scr("v_s", [NH, P, NKT, P], BF16)
    zT_s = scr("zT_s", [4, P, S], BF16)
    gbT_s = scr("gbT_s", [4, P, S], BF16)
    mixT_s = scr("mixT_s", [16, P, S], BF16)
    qaug_b = scr("qaug_b", [NH, 2, 4, S], BF16)

    sem_n = [0]
    sem_objs = []

    from contextlib import ExitStack
    stack = ExitStack()

    def mksem(name):
        sem_n[0] += 1
        return stack.enter_context(nc.semaphore(f"{name}{sem_n[0]}"))

    engsem = {e: mksem("e_" + e) for e in ENGS if e != "sp"}
    engsem["sp"] = mksem("e_sp")
    semcache = {}

    def dsem(key):
        if key not in semcache:
            semcache[key] = mksem("d")
        return semcache[key]

    lo = (nc.sbuf_base + 31) // 32 * 32
    ar = Arena(nc, lo, nc.sbuf_top)
    wbuf = [ar.alloc(f"wbuf{i}", [P, 8192], BF16) for i in range(3)]
    ident_f = ar.alloc("ident_f", [P, P], F32)
    ident_b = ar.alloc("ident_b", [P, P], BF16)
    ones_b = ar.alloc("ones_b", [P, P], BF16)
    bd_b = ar.alloc("bd_b", [P, P], BF16)
    diagB = ar.alloc("diagB", [P, NH, P], BF16)
    eps_t = ar.alloc("eps_t", [P, 1], F32)
    NV = 3 * L * KC + 5 * L + L * 12 + L * 4
    vecs_t = ar.alloc("vecs_t", [P, NV], F32)
    _o = [0]

    def vview(n):
        a = _o[0]
        _o[0] += n
        return vecs_t[:, a:a + n]
    gmix_t = vview(L * KC).rearrange("p (l k) -> p l k", k=KC)
    gmlp_t = vview(L * KC).rearrange("p (l k) -> p l k", k=KC)
    gmem_t = vview(L * KC).rearrange("p (l k) -> p l k", k=KC)
    gq_t = vview(L)
    gk_t = vview(L)
    gsub_t = vview(L)
    gqm_t = vview(L)
    gkm_t = vview(L)
    convw_t = vview(L * 12).rearrange("p (l t j) -> p l t j", t=3, j=4)
    gconv_t = vview(L * 4).rearrange("p (l j) -> p l j", j=4)
    lams_t = ar.alloc("lams_t", [P, 2, L], F32)
    neglam_t = ar.alloc("neglam_t", [P, L], F32)
    kmT = ar.alloc("kmT", [P, 4, MEM], BF16)
    vm = ar.alloc("vm", [P, 2, 512], BF16)
    base_mark = ar.mark()
    lamv_t = [ar.alloc(f"lamv{i}", [P, 1, L * 64], F32) for i in range(4)]
    lamp_t = ar.alloc("lamp_t", [P, L, 64], F32)
    ar.release(base_mark)

    pst = [nc.alloc_psum_tensor(f"ps{i}", [P, 1024], F32) for i in range(4)]

    def bank(b):
        return pst[b // 2][:, (b % 2) * 512:(b % 2) * 512 + 512]

    def bkey(b):
        return ("ps", b)

    dbg_n = [0]

    def dbg_dump(name, ap, shape, dt, reads):
        if not debug:
            return
        dbg_n[0] += 1
        t = nc.dram_tensor("dbg_" + name, list(shape), dt, kind="ExternalOutput").ap()
        tr.dma("sp", [(t, ap)], dsem(("dbg", dbg_n[0])), reads=reads)

    def mm(out, lhsT, rhs, start, stop, reads, writes):
        return tr.emit("pe", lambda e, o=out, l=lhsT, r=rhs, s=start, t=stop:
                       e.matmul(o, lhsT=l, rhs=r, start=s, stop=t), reads, writes)

    def act(out, in_, func, reads, writes, scale=1.0, bias=None):
        if bias is None:
            return tr.emit("act", lambda e, o=out, i=in_, f=func, s=scale:
                           e.activation(out=o, in_=i, func=f, scale=s), reads, writes)
        return tr.emit("act", lambda e, o=out, i=in_, f=func, s=scale, b=bias:
                       e.activation(out=o, in_=i, func=f, scale=s, bias=b), reads, writes)

    def stt(eng, out, in0, scalar, in1, op0, op1, reads, writes):
        return tr.emit(eng, lambda e, o=out, a=in0, s=scalar, b=in1, p0=op0, p1=op1:
                       e.scalar_tensor_tensor(out=o, in0=a, scalar=s, in1=b, op0=p0, op1=p1), reads, writes)

    def tt(eng, out, in0, in1, op, reads, writes):
        return tr.emit(eng, lambda e, o=out, a=in0, b=in1, p=op:
                       e.tensor_tensor(out=o, in0=a, in1=b, op=p), reads, writes)

    def ts(eng, out, in0, s1, op0, reads, writes, s2=None, op1=None):
        if op1 is None:
            return tr.emit(eng, lambda e, o=out, a=in0, s=s1, p=op0:
                           e.tensor_scalar(out=o, in0=a, scalar1=s, scalar2=None, op0=p), reads, writes)
        return tr.emit(eng, lambda e, o=out, a=in0, s=s1, p=op0, s2=s2, p1=op1:
                       e.tensor_scalar(out=o, in0=a, scalar1=s, scalar2=s2, op0=p, op1=p1), reads, writes)

    def cp(eng, out, in_, reads, writes):
        if eng == "act":
            return tr.emit("act", lambda e, o=out, i=in_: e.copy(out=o, in_=i), reads, writes)
        return tr.emit(eng, lambda e, o=out, i=in_: e.tensor_copy(out=o, in_=i), reads, writes)

    def recip(out, in_, reads, writes):
        return tr.emit("dve", lambda e, o=out, i=in_: e.reciprocal(out=o, in_=i), reads, writes)

    def memset(eng, ap, val, writes):
        return tr.emit(eng, lambda e, a=ap, v=val: e.memset(a, v), (), writes)

    bank_rr = [0]

    def next_bank(n=6):
        b = bank_rr[0] % n
        bank_rr[0] += 1
        return b

    rr = {}

    def rot(name, n):
        i = rr.get(name, 0)
        rr[name] = i + 1
        return i % n

    wsrc = {"in": w_in_d, "kv": w_kv_d, "out": w_out_d, "up": w_up_d, "down": w_down_d}
    wnblk = {"in": 10, "kv": 2, "out": 4, "up": 16, "down": 16}

    def convert(name, l, blocks=None, pace=()):
        pairs = []
        for b in (range(wnblk[name]) if blocks is None else blocks):
            if name == "down":
                half, cb = b // 8, b % 8
                src = w_down_d[l, half * 4096:(half + 1) * 4096, cb * 256:(cb + 1) * 256].rearrange(
                    "(kc p) n -> p kc n", p=P)
                dst = wb[name][l, b].rearrange("p (kc n) -> p kc n", n=256)
            else:
                src = wsrc[name][l, :, b * 512:(b + 1) * 512].rearrange("(kc p) n -> p kc n", p=P)
                dst = wb[name][l, b].rearrange("p (kc n) -> p kc n", n=512)
            pairs.append((dst, src))
        tr.dma("pool", pairs, dsem(("cv", name, l)), reads=pace, writes=[("wb", name, l)])

    wseq = []
    for l in range(L + 1):
        if l < L:
            wseq += [("kv", l, 0), ("kv", l, 1)]
        for t in range(NT):
            if l > 0:
                wseq += [("out", l - 1, b) for b in range(4)]
                for half in range(2):
                    wseq += [("up", l - 1, half * 8 + b) for b in range(8)]
                    wseq += [("down", l - 1, half * 8 + b) for b in range(8)]
            if l < L:
                wseq += [("in", l, b) for b in range(10)]
    wpos = [0]
    wld = [0]

    def wnext(expect):
        i = wpos[0]
        assert wseq[i] == expect, (wseq[i], expect)
        while wld[0] < len(wseq) and wld[0] <= i + 2:
            j = wld[0]
            name, l, b = wseq[j]
            tr.dma("sp", [(wbuf[j % 3][:, :], wb[name][l, b])], dsem(("wl", j % 3)),
                   reads=[("wb", name, l)], writes=[("wbuf", j % 3)])
            wld[0] += 1
        wpos[0] += 1
        return wbuf[i % 3], ("wbuf", i % 3)

    tr.dma("sp", [(ident_f[:, :], ident_d)], dsem("c0"), writes=["ident_f"])
    tr.dma("pool", [(ident_b[:, :], ident_d)], dsem("c1"), writes=["ident_b"])
    tr.dma("pool", [(diagB[:, :, :], diagb_d.rearrange("h i j -> i h j"))], dsem("c2"), writes=["diagB"])
    tr.dma("pool", [(qaug_b, qaug_d)], dsem("c3"), writes=["qaug_b"])
    convert("in", 0)
    convert("kv", 0)
    memset("dve", ones_b[:, :], 1.0, ["ones_b"])
    memset("dve", bd_b[:, :], 0.0, ["bd_b"])
    memset("dve", bd_b[0:64, 0:64], 1.0, ["bd_b"])
    memset("dve", bd_b[64:128, 64:128], 1.0, ["bd_b"])
    memset("dve", eps_t[:, :], EPS, ["eps_t"])
    tr.dma("sp", [(vecs_t[:, :], vecs_d)], dsem("c4"), writes=["small"])
    tr.dma("sp", [(lamv_t[i][:, :, :], lam_d[i].partition_broadcast(P)) for i in range(4)], dsem("c5"),
           writes=["lamv"])
    for j in range(2):
        tt("dve", lamp_t[:, :, :], lamv_t[2 * j][:, 0, :].rearrange("p (l d) -> p l d", d=64),
           lamv_t[2 * j + 1][:, 0, :].rearrange("p (l d) -> p l d", d=64), ALU.mult, ["lamv"], ["lamp"])
        tr.emit("dve", lambda e, o=lams_t[:, j, :], i=lamp_t[:, :, :]:
                e.tensor_reduce(out=o, in_=i, axis=AX.X, op=ALU.add), ["lamp"], ["lams"])
    act(lams_t[:, :, :], lams_t[:, :, :], AF.Exp, ["lams"], ["lams"])
    for l in range(L):
        stt("dve", neglam_t[:, l:l + 1], lams_t[:, 1, l:l + 1], -lambda_init(l), lams_t[:, 0, l:l + 1],
            ALU.add, ALU.subtract, ["lams"], ["neglam"])
        ts("dve", gsub_t[:, l:l + 1], gsub_t[:, l:l + 1], 1.0 - lambda_init(l), ALU.mult, ["small"], ["small"])
    dbg_dump("neglam", neglam_t[:, :], [P, L], F32, ["neglam"])
    dbg_dump("lams", lams_t[:, :, :], [P, 2, L], F32, ["lams"])
    dbg_dump("lamp", lamp_t[:, :, :], [P, L, 64], F32, ["lamp"])
    dbg_dump("vecs", vecs_t[:, :], [P, NV], F32, ["small"])
    tr.barrier()

    def rms_stats_to_rs(psb, pkey, n, lnv, rsv, lkey, width):
        act(lnv, psb, AF.Ln, [pkey], [lkey + "_ln"], scale=1.0 / n, bias=eps_t[:, 0:1])
        act(rsv, lnv, AF.Exp, [lkey + "_ln"], [lkey], scale=-0.5)

    def phase_kv(l):
        m0 = ar.mark()
        memtok = ar.alloc("memtok", [P, 2, D], F32)
        memT = ar.alloc("memT", [P, KC, MEM], F32)
        memsq = ar.alloc("memsq", [P, KC, MEM], BF16)
        memn = ar.alloc("memn", [P, KC, MEM], BF16)
        lnv = ar.alloc("kv_ln", [P, MEM], F32)
        rsv = ar.alloc("kv_rs", [P, MEM], F32)
        sqk = ar.alloc("kv_sqk", [P, MEM], BF16)
        tr.dma("sp", [(memtok[:, :, :], mem_d.rearrange("(mt p) d -> p mt d", p=P))], dsem("memtok"),
               writes=["memtok"])
        for kc in range(KC):
            b = next_bank()
            for mt in range(2):
                tr.emit("pe", lambda e, o=bank(b)[:, mt * P:(mt + 1) * P], i=memtok[:, mt, kc * P:(kc + 1) * P]:
                        e.transpose(o, i, ident_f[:, :]), ["memtok", "ident_f"], [bkey(b)])
            cp("act" if kc % 2 else "dve", memT[:, kc, :], bank(b)[:, 0:MEM], [bkey(b)], [("memT", kc)])
            act(memsq[:, kc, :], memT[:, kc, :], AF.Square, [("memT", kc)], [("memsq", kc)])
        b = next_bank()
        for kc in range(KC):
            mm(bank(b)[:, 0:MEM], ones_b[:, :], memsq[:, kc, :], kc == 0, kc == KC - 1,
               [("memsq", kc), "ones_b"], [bkey(b)])
        rms_stats_to_rs(bank(b)[:, 0:MEM], bkey(b), D, lnv[:, :], rsv[:, :], "kvrs", MEM)
        for kc in range(KC):
            stt("dve", memn[:, kc, :], memT[:, kc, :], gmem_t[:, l, kc:kc + 1], rsv[:, :], ALU.mult, ALU.mult,
                [("memT", kc), "kvrs"], [("memn", kc)])
        w, wk = wnext(("kv", l, 0))
        w3 = w[:, :].rearrange("p (kc n) -> p kc n", n=512)
        for h in range(4):
            b = next_bank()
            for kc in range(KC):
                mm(bank(b)[:, 0:MEM], w3[:, kc, h * P:(h + 1) * P], memn[:, kc, :], kc == 0, kc == KC - 1,
                   [wk, ("memn", kc)], [bkey(b)])
            act(sqk[:, :], bank(b)[:, 0:MEM], AF.Square, [bkey(b)], ["sqk"])
            b2 = next_bank()
            mm(bank(b2)[:, 0:MEM], ones_b[:, :], sqk[:, :], True, True, ["sqk", "ones_b"], [bkey(b2)])
            rms_stats_to_rs(bank(b2)[:, 0:MEM], bkey(b2), P, lnv[:, :], rsv[:, :], "kvrs2", MEM)
            stt("dve", kmT[:, h, :], bank(b)[:, 0:MEM], gkm_t[:, l:l + 1], rsv[:, :], ALU.mult, ALU.mult,
                [bkey(b), "kvrs2"], ["kmT"])
        w, wk = wnext(("kv", l, 1))
        w3 = w[:, :].rearrange("p (kc n) -> p kc n", n=512)
        for mt in range(2):
            b = next_bank()
            for kc in range(KC):
                mm(bank(b), memn[:, kc, mt * P:(mt + 1) * P], w3[:, kc, :], kc == 0, kc == KC - 1,
                   [wk, ("memn", kc)], [bkey(b)])
            cp("dve", vm[:, mt, :], bank(b), [bkey(b)], ["vm"])
        tr.barrier()
        ar.release(m0)

    def phase_p(l):
        m0 = ar.mark()
        xT = ar.alloc("xT", [P, KC, T], F32)
        hT = ar.alloc("hT", [P, KC, T], BF16)
        mixT = ar.alloc("mixT", [P, KC, T], BF16)
        aT = ar.alloc("aT", [P, 32, T], BF16)
        gcb = ar.alloc("gcb", [P, 4, T], F32)
        sq = [ar.alloc(f"sq{i}", [P, T], BF16) for i in range(4)]
        lnv = [ar.alloc(f"lnv{i}", [P, T], F32) for i in range(2)]
        rsv = [ar.alloc(f"rsv{i}", [P, T], F32) for i in range(2)]
        rl = [ar.alloc(f"rl{i}", [P, T], F32) for i in range(2)]
        qst = [ar.alloc(f"qst{i}", [P, T], BF16) for i in range(2)]
        vst = [ar.alloc(f"vst{i}", [P, T], BF16) for i in range(2)]
        gst = [ar.alloc(f"gst{i}", [P, T], BF16) for i in range(2)]
        most = [ar.alloc(f"most{i}", [P, T], BF16) for i in range(2)]
        qmn = [ar.alloc(f"qmn{i}", [P, T], BF16) for i in range(2)]
        pm = [ar.alloc(f"pm{i}", [P, 2 * T], BF16) for i in range(2)]
        rinv = [ar.alloc(f"rinv{i}", [P, T], F32) for i in range(2)]
        zc = ar.alloc("zc", [P, 4, T + 2], BF16)
        gbc = ar.alloc("gbc", [P, 4, T], BF16)
        cy = ar.alloc("cy", [P, T], F32)
        xin = nc.alloc_sbuf_tensor_at(f"xin_{l}", [P, 4, D], F32, offset=ar.off["aT"])
        xout = xin

        SB = 6

        sq_pending = []

        def sq_accum(c):
            while sq_pending:
                sq_pending.pop(0)()
            i = rot("sq", 4)
            act(sq[i][:, :], xT[:, c, :], AF.Square, [("xT", c)], [("sq", i)])
            sq_pending.append(lambda c=c, i=i: mm(bank(SB), ones_b[:, :], sq[i][:, :], c == 0, c == KC - 1,
                                                  [("sq", i), "ones_b"], [bkey(SB)]))

        def norm_finish(g_t, tagl):
            while sq_pending:
                sq_pending.pop(0)()
            i = rot("lnv", 2)
            act(lnv[i][:, :], bank(SB), AF.Ln, [bkey(SB)], [("lnv", i)], scale=1.0 / D, bias=eps_t[:, 0:1])
            act(rsv[i][:, :], lnv[i][:, :], AF.Exp, [("lnv", i)], [("rsv", i)], scale=-0.5)
            for kc in range(KC):
                stt("dve", hT[:, kc, :], xT[:, kc, :], g_t[:, tagl, kc:kc + 1],
                    rsv[i][:, :], ALU.mult, ALU.mult, [("xT", kc), ("rsv", i)], [("hT", kc)])

        def load_x(tt_):
            s1 = tt_ * T
            if l == 0:
                tr.dma("sp", [(xin[:, :, :], x_d[s1:s1 + T, :].rearrange("(j p) d -> p j d", p=P))], dsem("xin"),
                       writes=["xin"])
            else:
                tr.dma("sp", [(xT[:, :, :], xres[:, :, s1:s1 + T])], dsem("xT"),
                       writes=[("xT", kc) for kc in range(KC)])

        def load_mix(tt_):
            s1 = tt_ * T
            tr.dma("sp", [(mixT[:, 0:8, :], mixT_s[0:8, :, s1:s1 + T].rearrange("c p s -> p c s")),
                          (mixT[:, 12:16, :], mixT_s[12:16, :, s1:s1 + T].rearrange("c p s -> p c s"))],
                   dsem("mixTl"), writes=[("mixT", kc) for kc in range(8)] + [("mixT", kc) for kc in range(12, 16)])
            a = max(s1 - 1, 0)
            b_ = min(s1 + T + 1, S)
            if s1 == 0:
                memset("dve", zc[:, :, 0:1], 0.0, ["zc"])
            if s1 + T == S:
                memset("dve", zc[:, :, T + 1:T + 2], 0.0, ["zc"])
            tr.dma("sp", [(zc[:, :, a - (s1 - 1):b_ - (s1 - 1)], zT_s[:, :, a:b_].rearrange("j p s -> p j s")),
                          (gbc[:, :, :], gbT_s[:, :, s1:s1 + T].rearrange("j p s -> p j s"))],
                   dsem("zcl"), writes=["zc", "gbc"])

        def prep_steps():
            lp = l - 1
            steps = []
            for j in range(4):
                def conv_step(j=j):
                    ts("dve", cy[:, :], zc[:, j, 1:T + 1], convw_t[:, lp, 1, j:j + 1], ALU.mult, ["zc"], ["cy"])
                    stt("dve", cy[:, :], zc[:, j, 0:T], convw_t[:, lp, 0, j:j + 1], cy[:, :], ALU.mult, ALU.add,
                        ["zc", "cy"], ["cy"])
                    stt("dve", cy[:, :], zc[:, j, 2:T + 2], convw_t[:, lp, 2, j:j + 1], cy[:, :], ALU.mult, ALU.add,
                        ["zc", "cy"], ["cy"])
                    tt("dve", mixT[:, 8 + j, :], cy[:, :], gbc[:, j, :], ALU.mult, ["cy", "gbc"], [("mixT", 8 + j)])
                steps.append(conv_step)
            for kc in range(12):
                def norm_step(kc=kc):
                    i = rot("sq", 4)
                    act(sq[i][:, :], mixT[:, kc, :], AF.Square, [("mixT", kc)], [("sq", i)])
                    b = next_bank()
                    mm(bank(b), ones_b[:, :], sq[i][:, :], True, True, [("sq", i), "ones_b"], [bkey(b)])
                    k = rot("lnv", 2)
                    act(lnv[k][:, :], bank(b), AF.Ln, [bkey(b)], [("lnv", k)], scale=1.0 / P, bias=eps_t[:, 0:1])
                    act(rsv[k][:, :], lnv[k][:, :], AF.Exp, [("lnv", k)], [("rsv", k)], scale=-0.5)
                    g = gsub_t[:, lp:lp + 1] if kc < 8 else gconv_t[:, lp, kc - 8:kc - 7]
                    stt("dve", mixT[:, kc, :], mixT[:, kc, :], g, rsv[k][:, :], ALU.mult, ALU.mult,
                        [("mixT", kc), ("rsv", k)], [("mixT", kc)])
                steps.append(norm_step)
            return steps

        cpieces = conv_pieces(l) if l < L else []

        for t in range(NT):
            s0 = t * T
            for pc in cpieces[t::NT]:
                convert(pc[0], pc[1], pc[2], pace=[("hT", 0)])
            if t == 0:
                load_x(0)
                if l > 0:
                    load_mix(0)
                    for st_ in prep_steps():
                        st_()
            if l == 0:
                for kc in range(KC):
                    b = next_bank()
                    for j in range(4):
                        tr.emit("pe", lambda e, o=bank(b)[:, j * P:(j + 1) * P], i=xin[:, j, kc * P:(kc + 1) * P]:
                                e.transpose(o, i, ident_f[:, :]), ["xin", "ident_f"], [bkey(b)])
                    cp("act" if kc % 2 else "dve", xT[:, kc, :], bank(b), [bkey(b)], [("xT", kc)])
                    sq_accum(kc)
                if t + 1 < NT:
                    load_x(t + 1)
            else:
                lp = l - 1
                for g in range(4):
                    w, wk = wnext(("out", lp, g))
                    w3 = w[:, :].rearrange("p (kc n) -> p kc n", n=512)
                    for dc in range(4):
                        b = next_bank()
                        for kc in range(KC):
                            mm(bank(b), w3[:, kc, dc * P:(dc + 1) * P], mixT[:, kc, :], kc == 0, kc == KC - 1,
                               [wk, ("mixT", kc)], [bkey(b)])
                        c = 4 * g + dc
                        tt("dve", xT[:, c, :], xT[:, c, :], bank(b), ALU.add, [("xT", c), bkey(b)], [("xT", c)])
                        sq_accum(c)
                norm_finish(gmlp_t, lp)
                pending = []
                if t + 1 < NT:
                    load_mix(t + 1)
                    pending = prep_steps()
                for half in range(2):
                    for g in range(8):
                        if pending:
                            pending.pop(0)()
                        w, wk = wnext(("up", lp, half * 8 + g))
                        w3 = w[:, :].rearrange("p (kc n) -> p kc n", n=512)
                        for hc in range(4):
                            b = next_bank()
                            for kc in range(KC):
                                mm(bank(b), w3[:, kc, hc * P:(hc + 1) * P], hT[:, kc, :], kc == 0, kc == KC - 1,
                                   [wk, ("hT", kc)], [bkey(b)])
                            i = rot("rl", 2)
                            a = g * 4 + hc
                            act(rl[i][:, :], bank(b), AF.Relu, [bkey(b)], [("rl", i)])
                            tt("dve", aT[:, a, :], rl[i][:, :], rl[i][:, :], ALU.mult,
                               [("rl", i)], [("aT", a)])
                    for g in range(8):
                        w, wk = wnext(("down", lp, half * 8 + g))
                        w3 = w[:, :].rearrange("p (kc n) -> p kc n", n=256)
                        for dc in range(2):
                            b = next_bank()
                            for a in range(32):
                                mm(bank(b), w3[:, a, dc * P:(dc + 1) * P], aT[:, a, :], a == 0, a == 31,
                                   [wk, ("aT", a)], [bkey(b)])
                            c = 2 * g + dc
                            tt("dve", xT[:, c, :], xT[:, c, :], bank(b), ALU.add, [("xT", c), bkey(b)], [("xT", c)])
                            if half == 1 and l < L:
                                sq_accum(c)
            if l == L:
                for j in range(4):
                    for c4 in range(4):
                        b = next_bank()
                        for k in range(4):
                            kc = c4 * 4 + k
                            tr.emit("pe", lambda e, o=bank(b)[:, k * P:(k + 1) * P], i=xT[:, kc, j * P:(j + 1) * P]:
                                    e.transpose(o, i, ident_f[:, :]), [("xT", kc), "ident_f"], [bkey(b)])
                        cp("act" if c4 % 2 else "dve", xout[:, j, c4 * 512:(c4 + 1) * 512], bank(b), [bkey(b)],
                           [("aT", a_) for a_ in range(32)])
                tr.dma("sp", [(out_d[s0:s0 + T, :].rearrange("(j p) d -> p j d", p=P), xout[:, :, :])],
                       dsem("xout"), reads=[("aT", a_) for a_ in range(32)])
                if t + 1 < NT:
                    load_x(t + 1)
                continue
            if l > 0:
                tr.dma("sp", [(xres[:, :, s0:s0 + T], xT[:, :, :])], dsem("xres_st"),
                       reads=[("xT", kc) for kc in range(KC)])
            else:
                tr.dma("sp", [(xres[:, :, s0:s0 + T], xT[:, :, :])], dsem("xres_st"),
                       reads=[("xT", kc) for kc in range(KC)])
            norm_finish(gmix_t, l)
            if l > 0 and t + 1 < NT:
                load_x(t + 1)
            pend = None

            def finish_qk(item):
                b, which, h = item
                i = rot("sq", 4)
                act(sq[i][:, :], bank(b), AF.Square, [bkey(b)], [("sq", i)])
                b2 = next_bank()
                mm(bank(b2), bd_b[:, :], sq[i][:, :], True, True, [("sq", i), "bd_b"], [bkey(b2)])
                k = rot("lnv", 2)
                act(lnv[k][:, :], bank(b2), AF.Ln, [bkey(b2)], [("lnv", k)], scale=1.0 / 64, bias=eps_t[:, 0:1])
                act(rsv[k][:, :], lnv[k][:, :], AF.Exp, [("lnv", k)], [("rsv", k)], scale=-0.5)
                s = rot("qst", 2)
                g = gq_t if which == 0 else gk_t
                stt("dve", qst[s][:, :], bank(b), g[:, l:l + 1], rsv[k][:, :], ALU.mult, ALU.mult,
                    [bkey(b), ("rsv", k)], [("qst", s)])
                dst = qT_s if which == 0 else kT_s
                tr.dma("sp", [(dst[h, :, s0:s0 + T], qst[s][:, :])], dsem(("qst", s)), reads=[("qst", s)])

            for blk in range(4):
                w, wk = wnext(("in", l, blk))
                w3 = w[:, :].rearrange("p (kc n) -> p kc n", n=512)
                for oc in range(4):
                    b = next_bank()
                    for kc in range(KC):
                        mm(bank(b), w3[:, kc, oc * P:(oc + 1) * P], hT[:, kc, :], kc == 0, kc == KC - 1,
                           [wk, ("hT", kc)], [bkey(b)])
                    if pend is not None:
                        finish_qk(pend)
                    pend = (b, blk // 2, (blk % 2) * 4 + oc)
            for blk in range(4, 6):
                w, wk = wnext(("in", l, blk))
                w3 = w[:, :].rearrange("p (kc n) -> p kc n", n=512)
                for j in range(4):
                    b = next_bank()
                    for kc in range(KC):
                        mm(bank(b), hT[:, kc, j * P:(j + 1) * P], w3[:, kc, :], kc == 0, kc == KC - 1,
                           [wk, ("hT", kc)], [bkey(b)])
                    if pend is not None:
                        finish_qk(pend)
                        pend = None
                    s = rot("vst", 2)
                    cp("act", vst[s][:, :], bank(b), [bkey(b)], [("vst", s)])
                    hb = (blk - 4) * 4
                    tr.dma("sp", [(v_s[hb:hb + 4, :, t * 4 + j, :].rearrange("h p d -> p h d"),
                                   vst[s][:, :].rearrange("p (h d) -> p h d", d=P))],
                           dsem(("vst", s)), reads=[("vst", s)])
            for blk in range(6, 9):
                w, wk = wnext(("in", l, blk))
                w3 = w[:, :].rearrange("p (kc n) -> p kc n", n=512)
                for oc in range(4):
                    b = next_bank()
                    for kc in range(KC):
                        mm(bank(b), w3[:, kc, oc * P:(oc + 1) * P], hT[:, kc, :], kc == 0, kc == KC - 1,
                           [wk, ("hT", kc)], [bkey(b)])
                    if blk == 6:
                        s = rot("gst", 2)
                        cp("act", gst[s][:, :], bank(b), [bkey(b)], [("gst", s)])
                        tr.dma("sp", [(gbT_s[oc, :, s0:s0 + T], gst[s][:, :])], dsem(("gst", s)), reads=[("gst", s)])
                    elif blk == 7:
                        cp("act", gcb[:, oc, :], bank(b), [bkey(b)], [("gcb", oc)])
                    else:
                        s = rot("gst", 2)
                        tt("dve", gst[s][:, :], bank(b), gcb[:, oc, :], ALU.mult, [bkey(b), ("gcb", oc)],
                           [("gst", s)])
                        tr.dma("sp", [(zT_s[oc, :, s0:s0 + T], gst[s][:, :])], dsem(("gst", s)), reads=[("gst", s)])
            w, wk = wnext(("in", l, 9))
            w3 = w[:, :].rearrange("p (kc n) -> p kc n", n=512)
            st8 = [dict() for _ in range(4)]

            def stA(h):
                b = next_bank()
                for kc in range(KC):
                    mm(bank(b), w3[:, kc, h * P:(h + 1) * P], hT[:, kc, :], kc == 0, kc == KC - 1,
                       [wk, ("hT", kc)], [bkey(b)])
                i = rot("sq", 4)
                act(sq[i][:, :], bank(b), AF.Square, [bkey(b)], [("sq", i)])
                st8[h].update(b=b, i=i)

            def stB(h):
                b, i = st8[h]["b"], st8[h]["i"]
                b2 = next_bank()
                mm(bank(b2), ones_b[:, :], sq[i][:, :], True, True, [("sq", i), "ones_b"], [bkey(b2)])
                k = rot("lnv", 2)
                act(lnv[k][:, :], bank(b2), AF.Ln, [bkey(b2)], [("lnv", k)], scale=1.0 / P, bias=eps_t[:, 0:1])
                act(rsv[k][:, :], lnv[k][:, :], AF.Exp, [("lnv", k)], [("rsv", k)], scale=-0.5)
                qi = rot("qmn", 2)
                stt("dve", qmn[qi][:, :], bank(b), gqm_t[:, l:l + 1], rsv[k][:, :], ALU.mult, ALU.mult,
                    [bkey(b), ("rsv", k)], [("qmn", qi)])
                st8[h].update(qi=qi)

            def stC(h):
                qi = st8[h]["qi"]
                for mt in range(2):
                    mm(bank(6 + mt), kmT[:, h, mt * P:(mt + 1) * P], qmn[qi][:, :], True, True,
                       [("qmn", qi), "kmT"], [bkey(6 + mt)])
                pi = rot("pm", 2)
                act(pm[pi][:, :], pst[3][:, :], AF.Exp, [bkey(6), bkey(7)], [("pm", pi)], scale=float(P) ** -0.5)
                st8[h].update(pi=pi)

            def stD(h):
                pi = st8[h]["pi"]
                bo = next_bank()
                for mt in range(2):
                    mm(bank(bo), vm[:, mt, h * P:(h + 1) * P], pm[pi][:, mt * T:(mt + 1) * T], mt == 0, mt == 1,
                       [("pm", pi), "vm"], [bkey(bo)])
                bs = next_bank()
                for mt in range(2):
                    mm(bank(bs), ones_b[:, :], pm[pi][:, mt * T:(mt + 1) * T], mt == 0, mt == 1,
                       [("pm", pi), "ones_b"], [bkey(bs)])
                ri = rot("rinv", 2)
                recip(rinv[ri][:, :], bank(bs), [bkey(bs)], [("rinv", ri)])
                s_ = rot("most", 2)
                tt("dve", most[s_][:, :], bank(bo), rinv[ri][:, :], ALU.mult, [bkey(bo), ("rinv", ri)],
                   [("most", s_)])
                tr.dma("sp", [(mixT_s[12 + h, :, s0:s0 + T], most[s_][:, :])], dsem(("most", s_)),
                       reads=[("most", s_)])

            stA(0); stA(1); stB(0); stA(2); stB(1); stC(0); stA(3); stB(2); stC(1); stD(0)
            stB(3); stC(2); stD(1); stC(3); stD(2); stD(3)
        tr.barrier()
        ar.release(m0)

    def skip_tile(h, qb, kt):
        if skip_thresh is None:
            return False
        q0, q1 = qb * 512, qb * 512 + 511
        k0, k1 = kt * P, kt * P + P - 1
        if k1 < q0:
            md = q0 - k1
        elif k0 > q1:
            md = k0 - q1
        else:
            return False
        return (2.0 ** -(h + 1)) * md >= skip_thresh

    def phase_m(l):
        m0 = ar.mark()
        sets = []
        set_off = []
        for s in range(2):
            set_off.append((ar.mark() + 31) // 32 * 32)
            d = {}
            d["KT"] = [ar.alloc(f"KT{s}{c}", [68, S], BF16) for c in range(2)]
            d["QA"] = [ar.alloc(f"QA{s}{c}", [68, S], BF16) for c in range(2)]
            d["QB"] = [ar.alloc(f"QB{s}{c}", [68, S], BF16) for c in range(2)]
            d["V"] = ar.alloc(f"V{s}", [P, NKT, P], BF16)
            sets.append(d)
        PT = [ar.alloc(f"PT{i}", [P, 1024], BF16) for i in range(3)]
        r0 = ar.alloc("r0", [P, 512], F32)
        r1 = ar.alloc("r1", [P, 512], F32)
        o_f = ar.alloc("o_f", [P, 512], F32)
        t_f = ar.alloc("t_f", [P, 512], F32)
        lnv = ar.alloc("m_lnv", [P, 512], F32)
        rsv = ar.alloc("m_rsv", [P, 512], F32)
        sqo = ar.alloc("sqo", [P, 512], BF16)
        ost = [ar.alloc(f"ost{i}", [P, 512], BF16) for i in range(2)]

        def load_head(h, s):
            d = sets[s]
            pairs = []
            keys = []
            for c in range(2):
                pairs.append((d["KT"][c][0:64, :], kT_s[h, c * 64:(c + 1) * 64, :]))
                pairs.append((d["QA"][c][0:64, :], qT_s[h, c * 64:(c + 1) * 64, :]))
                pairs.append((d["QB"][c][0:64, :], qT_s[h, c * 64:(c + 1) * 64, :]))
                pairs.append((d["QA"][c][64:68, :], qaug_b[h, 0]))
                pairs.append((d["QB"][c][64:68, :], qaug_b[h, 1]))
            pairs.append((d["V"][:, :, :], v_s[h]))
            tr.dma("sp", pairs, dsem(("head", s)), writes=[("head", s)])

        tr.dma("pool", [(sets[0]["KT"][c][64:68, :], kaug_d) for c in range(2)], dsem("kaug"),
               writes=[("head", 0)])
        load_head(0, 0)

        tr.dma("pool", [(sets[1]["KT"][c][64:68, :], kaug_d) for c in range(2)], dsem("kaug"),
               writes=[("head", 1)])

        PV = [4, 5]
        LS = [6, 7]
        for h in range(NH):
            s = h % 2
            d = sets[s]
            hk = ("head", s)
            if h + 1 < NH:
                load_head(h + 1, 1 - s)
            for qb in range(NQB):
                q0 = qb * 512
                kts = [kt for kt in range(NKT) if not skip_tile(h, qb, kt)]
                n = len(kts)

                def qk(i):
                    kt = kts[i]
                    k0 = kt * P
                    pr = i % 2
                    for c in range(2):
                        b = 2 * pr + c
                        KTc, QAc, QBc = d["KT"][c], d["QA"][c], d["QB"][c]
                        if k0 + P <= q0:
                            mm(bank(b), KTc[0:68, k0:k0 + P], QAc[0:68, q0:q0 + 512], True, True, [hk], [bkey(b)])
                        elif k0 >= q0 + 512:
                            mm(bank(b), KTc[0:68, k0:k0 + P], QBc[0:68, q0:q0 + 512], True, True, [hk], [bkey(b)])
                        else:
                            js = (k0 - q0) // P
                            if js > 0:
                                mm(bank(b)[:, 0:js * P], KTc[0:68, k0:k0 + P], QBc[0:68, q0:q0 + js * P],
                                   True, True, [hk], [bkey(b)])
                            mm(bank(b)[:, js * P:(js + 1) * P], KTc[0:64, k0:k0 + P],
                               QAc[0:64, q0 + js * P:q0 + (js + 1) * P], True, False, [hk], [bkey(b)])
                            mm(bank(b)[:, js * P:(js + 1) * P], ident_b[:, :], diagB[:, h, :], False, True,
                               ["ident_b", "diagB"], [bkey(b)])
                            if js < 3:
                                mm(bank(b)[:, (js + 1) * P:512], KTc[0:68, k0:k0 + P],
                                   QAc[0:68, q0 + (js + 1) * P:q0 + 512], True, True, [hk], [bkey(b)])

                def expav(i):
                    kt = kts[i]
                    pr = i % 2
                    pi = i % 3
                    act(PT[pi][:, :], pst[pr][:, :], AF.Exp, [bkey(2 * pr), bkey(2 * pr + 1)], [("PT", pi)],
                        scale=0.125)
                    first, last = (i == 0), (i == n - 1)
                    if h == 0 and qb == 0 and l == 0:
                        dbg_dump(f"PT{i}", PT[pi][:, :], [P, 1024], BF16, [("PT", pi)])
                    for c in range(2):
                        mm(bank(PV[c]), d["V"][:, kt, :], PT[pi][:, c * 512:(c + 1) * 512], first, last,
                           [hk, ("PT", pi)], [bkey(PV[c])])
                    for c in range(2):
                        mm(bank(LS[c]), ones_b[:, :], PT[pi][:, c * 512:(c + 1) * 512], first, last,
                           ["ones_b", ("PT", pi)], [bkey(LS[c])])

                qk(0)
                for i in range(n):
                    if i + 1 < n:
                        qk(i + 1)
                    expav(i)
                cp("act", t_f[:, :], bank(PV[1]), [bkey(PV[1])], ["t_f"])
                cp("dve", o_f[:, :], bank(PV[0]), [bkey(PV[0])], ["o_f"])
                cp("dve", r0[:, :], bank(LS[0]), [bkey(LS[0])], ["r0"])
                cp("dve", r1[:, :], bank(LS[1]), [bkey(LS[1])], ["r1"])
                recip(r0[:, :], r0[:, :], ["r0"], ["r0"])
                recip(r1[:, :], r1[:, :], ["r1"], ["r1"])
                tt("dve", o_f[:, :], o_f[:, :], r0[:, :], ALU.mult, ["o_f", "r0"], ["o_f"])
                tt("dve", t_f[:, :], t_f[:, :], r1[:, :], ALU.mult, ["t_f", "r1"], ["t_f"])
                oi = rot("ost", 2)
                stt("dve", ost[oi][:, :], t_f[:, :], neglam_t[:, l:l + 1], o_f[:, :], ALU.mult, ALU.add,
                    ["t_f", "o_f"], [("ost", oi)])
                tr.dma("sp", [(mixT_s[h, :, q0:q0 + 512], ost[oi][:, :])], dsem(("ost", oi)), reads=[("ost", oi)])
        tr.barrier()
        ar.release(m0)

    def conv_pieces(l):
        pcs = [("out", l, range(0, 4))]
        for q4 in range(4):
            pcs.append(("up", l, range(q4 * 4, q4 * 4 + 4)))
            pcs.append(("down", l, range(q4 * 4, q4 * 4 + 4)))
        if l + 1 < L:
            pcs.append(("in", l + 1, range(0, 5)))
            pcs.append(("in", l + 1, range(5, 10)))
            pcs.append(("kv", l + 1, range(0, 2)))
        return pcs

    for l in range(L):
        phase_kv(l)
        phase_p(l)
        phase_m(l)
    phase_p(L)
    tr.barrier()
    assert wpos[0] == len(wseq), (wpos[0], len(wseq))

    tr.finalize(engsem)
    with stack:
        with nc.Block() as block:
            @block.tensor
            def _(e):
                tr.replay("pe", e)

            @block.scalar
            def _(e):
                tr.replay("act", e)

            @block.vector
            def _(e):
                tr.replay("dve", e)

            @block.gpsimd
            def _(e):
                tr.replay("pool", e)

            @block.sync
            def _(e):
                tr.replay("sp", e)
    return nc


def make_consts(S):
    ident = np.eye(P, dtype=np.float32)
    pos = np.arange(S)
    hi = (pos // P) * P
    lo = pos % P
    kaug = np.stack([np.ones(S), np.ones(S), hi, lo]).astype(np.float32)
    qaug = np.zeros((NH, 2, 4, S), np.float32)
    diagb = np.zeros((NH, P, P), np.float32)
    ii = np.arange(P)
    for h in range(NH):
        m8 = 8.0 * 2.0 ** (-(h + 1))
        qaug[h, 0] = np.stack([-m8 * hi, -m8 * lo, m8 * np.ones(S), m8 * np.ones(S)])
        qaug[h, 1] = -qaug[h, 0]
        diagb[h] = -m8 * np.abs(ii[:, None] - ii[None, :])
    return {"c_ident": ident, "c_kaug": kaug, "c_qaug": qaug, "c_diagb": diagb}


_W_KEYS = ("w_in", "w_mem_kv", "w_out", "w_up", "w_down")


def pack_vecs(inputs, L):
    f = lambda k: np.asarray(inputs[k], dtype=np.float32)[:L]
    cols = []
    for k in ("g_mix", "g_mlp", "g_mem"):
        cols.append(f(k).reshape(L, KC, P).transpose(2, 0, 1).reshape(P, L * KC))
    for k in ("g_q_diff", "g_k_diff"):
        cols.append(np.concatenate([f(k).T, f(k).T], axis=0))
    for k in ("g_subln", "g_q_mem", "g_k_mem"):
        cols.append(f(k).T)
    cols.append(f("conv_w").reshape(L, 3, 4, P).transpose(3, 0, 1, 2).reshape(P, L * 12))
    cols.append(f("g_conv_out").reshape(L, 4, P).transpose(2, 0, 1).reshape(P, L * 4))
    return np.ascontiguousarray(np.concatenate(cols, axis=1))
_LAM_KEYS = ("lam_q1", "lam_k1", "lam_q2", "lam_k2")


def make_in_maps(inputs, S, L, ncores):
    shared = {k: np.ascontiguousarray(np.asarray(inputs[k], dtype=np.float32)[:L]) for k in _W_KEYS}
    for k in _LAM_KEYS:
        shared[k] = np.ascontiguousarray(np.asarray(inputs[k], dtype=np.float32)[:L].reshape(1, L * 64))
    shared["vecs"] = pack_vecs(inputs, L)
    shared.update(make_consts(S))
    x = np.asarray(inputs["x"], dtype=np.float32)
    mem = np.asarray(inputs["mem"], dtype=np.float32)
    maps = []
    for c in range(ncores):
        m = dict(shared)
        m["x"] = np.ascontiguousarray(x[c, :S])
        m["mem"] = np.ascontiguousarray(mem[c])
        maps.append(m)
    return maps


SKIP_THRESH = 60.0


def kernel(**inputs):
    x = inputs["x"]
    B, S, _ = x.shape
    L = inputs["w_in"].shape[0]
    nc = build_program(S, L, SKIP_THRESH)
    in_maps = make_in_maps(inputs, S, L, B)
    res = run_bass_kernel_spmd(nc, in_maps, core_ids=list(range(B)))
    return np.stack([np.asarray(r["out"], dtype=np.float32) for r in res.results], axis=0)
```

```python
import math
import numpy as np
import concourse.bass as bass
import concourse.mybir as mybir
from concourse.bass_utils import run_bass_kernel_spmd

F32 = mybir.dt.float32
BF16 = mybir.dt.bfloat16
AF = mybir.ActivationFunctionType
ALU = mybir.AluOpType
AX = mybir.AxisListType

P = 128
D = 2048
DFF = 8192
INW = 5120
NH = 8
T = 512
MEM = 256
KC = D // P
EPS = 1e-6
ENGS = ("pe", "act", "dve", "pool", "sp")


class Ev:
    __slots__ = ("eng", "is_dma", "needs", "sem", "val")

    def __init__(self, eng, is_dma):
        self.eng = eng
        self.is_dma = is_dma
        self.needs = False
        self.sem = None
        self.val = 0


class Tracker:
    def __init__(self):
        self.ops = {e: [] for e in ENGS}
        self.state = {}
        self.last = {e: None for e in ENGS}
        self.pending_dma = {}
        self.dma_cnt = {}

    def _deps(self, eng, reads, writes, is_dma):
        deps = []
        for k in reads:
            st = self.state.get(k)
            if st is not None and st[0] is not None:
                deps.append(st[0])
        for k in writes:
            st = self.state.get(k)
            if st is not None:
                if st[0] is not None:
                    deps.append(st[0])
                deps.extend(st[1].values())
                deps.extend(st[2])
        out = []
        seen = set()
        for d in deps:
            if id(d) in seen:
                continue
            seen.add(id(d))
            if d.is_dma or is_dma or d.eng != eng:
                d.needs = True
                out.append(d)
        return out

    def _update(self, ev, reads, writes):
        for k in reads:
            st = self.state.get(k)
            if st is None:
                st = [None, {}, []]
                self.state[k] = st
            if ev.is_dma:
                st[2].append(ev)
            else:
                st[1][ev.eng] = ev
        for k in writes:
            self.state[k] = [ev, {}, []]

    def emit(self, eng, fn, reads=(), writes=()):
        deps = self._deps(eng, reads, writes, False)
        ev = Ev(eng, False)
        self.ops[eng].append((deps, fn, ev))
        self._update(ev, reads, writes)
        self.last[eng] = ev
        return ev

    def dma(self, eng, pairs, sem, reads=(), writes=(), **kw):
        deps = self._deps(eng, reads, writes, True)
        ev = Ev(eng, True)
        ev.sem = sem
        self.dma_cnt[sem] = self.dma_cnt.get(sem, 0) + 16 * len(pairs)
        ev.val = self.dma_cnt[sem]

        def fn(e, pairs=pairs, kw=kw):
            return [e.dma_start(out=o, in_=i, **kw) for (o, i) in pairs]

        self.ops[eng].append((deps, fn, ev))
        self._update(ev, reads, writes)
        self.pending_dma[sem] = ev
        return ev

    def barrier(self):
        evs = [e for e in self.last.values() if e is not None] + list(self.pending_dma.values())
        for eng in ENGS:
            deps = []
            for d in evs:
                if d.is_dma or d.eng != eng:
                    d.needs = True
                    deps.append(d)
            self.ops[eng].append((deps, None, None))
        self.pending_dma = {}
        self.state = {}

    def finalize(self, engsem):
        for eng in ENGS:
            cnt = 0
            for (_, _, ev) in self.ops[eng]:
                if ev is not None and not ev.is_dma:
                    ev.sem = engsem[eng]
                    if ev.needs:
                        cnt += 1
                        ev.val = cnt

    def replay(self, engname, eng):
        waited = {}
        for (deps, fn, ev) in self.ops[engname]:
            for d in deps:
                key = id(d.sem)
                if waited.get(key, 0) < d.val:
                    eng.wait_ge(d.sem, d.val)
                    waited[key] = d.val
            if fn is None:
                continue
            ins = fn(eng)
            if ev.is_dma:
                for i in ins:
                    i.then_inc(ev.sem, 16)
            elif ev.needs:
                ins.then_inc(ev.sem, 1)


class Arena:
    def __init__(self, nc, lo, hi):
        self.nc = nc
        self.lo = lo
        self.hi = hi
        self.ptr = lo
        self.n = 0
        self.off = {}

    def alloc(self, name, shape, dtype):
        esz = 4 if dtype == F32 else 2
        nbytes = esz
        for s in shape[1:]:
            nbytes *= s
        off = (self.ptr + 31) // 32 * 32
        assert off + nbytes <= self.hi, f"SBUF overflow allocating {name}: {off}+{nbytes} > {self.hi}"
        self.ptr = off + nbytes
        self.n += 1
        self.off[name] = off
        return self.nc.alloc_sbuf_tensor_at(f"{name}_{self.n}", list(shape), dtype, offset=off)

    def mark(self):
        return self.ptr

    def release(self, m):
        self.ptr = m


def lambda_init(l):
    return 0.8 - 0.6 * math.exp(-0.3 * l)


def build_program(S, L, skip_thresh=None, debug=False):
    NT = S // T
    NKT = S // P
    NQB = S // 512
    nc = bass.Bass("TRN2", target_bir_lowering=False)
    tr = Tracker()

    def din(name, shape):
        return nc.dram_tensor(name, list(shape), F32, kind="ExternalInput").ap()

    x_d = din("x", [S, D])
    mem_d = din("mem", [MEM, D])
    w_in_d = din("w_in", [L, D, INW])
    w_kv_d = din("w_mem_kv", [L, D, 1024])
    w_out_d = din("w_out", [L, D, D])
    w_up_d = din("w_up", [L, D, DFF])
    w_down_d = din("w_down", [L, DFF, D])
    vecs_d = din("vecs", [P, 3 * L * KC + 5 * L + L * 12 + L * 4])
    lam_d = [din(n, [1, L * 64]) for n in ("lam_q1", "lam_k1", "lam_q2", "lam_k2")]
    ident_d = din("c_ident", [P, P])
    kaug_d = din("c_kaug", [4, S])
    qaug_d = din("c_qaug", [NH, 2, 4, S])
    diagb_d = din("c_diagb", [NH, P, P])
    out_d = nc.dram_tensor("out", [S, D], F32, kind="ExternalOutput").ap()

    def scr(name, shape, dt):
        kind = "ExternalOutput" if (debug and not name.startswith("wb_")) else "Internal"
        return nc.dram_tensor(name, list(shape), dt, kind=kind).ap()

    wb = {
        "in": scr("wb_in", [L, 10, P, 8192], BF16),
        "kv": scr("wb_kv", [L, 2, P, 8192], BF16),
        "out": scr("wb_out", [L, 4, P, 8192], BF16),
        "up": scr("wb_up", [L, 16, P, 8192], BF16),
        "down": scr("wb_down", [L, 16, P, 8192], BF16),
    }
    xres = scr("xres", [P, KC, S], F32)
    qT_s = scr("qT_s", [NH, P, S], BF16)
    kT_s = scr("kT_s", [NH, P, S], BF16)
    v_s = scr("v_s", [NH, P, NKT, P], BF16)
    zT_s = scr("zT_s", [4, P, S], BF16)
    gbT_s = scr("gbT_s", [4, P, S], BF16)
    mixT_s = scr("mixT_s", [16, P, S], BF16)
    qaug_b = scr("qaug_b", [NH, 2, 4, S], BF16)

    sem_n = [0]
    sem_objs = []

    from contextlib import ExitStack
    stack = ExitStack()

    def mksem(name):
        sem_n[0] += 1
        return stack.enter_context(nc.semaphore(f"{name}{sem_n[0]}"))

    engsem = {e: mksem("e_" + e) for e in ENGS if e != "sp"}
    engsem["sp"] = mksem("e_sp")
    semcache = {}

    def dsem(key):
        if key not in semcache:
            semcache[key] = mksem("d")
        return semcache[key]

    lo = (nc.sbuf_base + 31) // 32 * 32
    ar = Arena(nc, lo, nc.sbuf_top)
    wbuf = [ar.alloc(f"wbuf{i}", [P, 8192], BF16) for i in range(3)]
    ident_f = ar.alloc("ident_f", [P, P], F32)
    ident_b = ar.alloc("ident_b", [P, P], BF16)
    ones_b = ar.alloc("ones_b", [P, P], BF16)
    bd_b = ar.alloc("bd_b", [P, P], BF16)
    diagB = ar.alloc("diagB", [P, NH, P], BF16)
    eps_t = ar.alloc("eps_t", [P, 1], F32)
    NV = 3 * L * KC + 5 * L + L * 12 + L * 4
    vecs_t = ar.alloc("vecs_t", [P, NV], F32)
    _o = [0]

    def vview(n):
        a = _o[0]
        _o[0] += n
        return vecs_t[:, a:a + n]
    gmix_t = vview(L * KC).rearrange("p (l k) -> p l k", k=KC)
    gmlp_t = vview(L * KC).rearrange("p (l k) -> p l k", k=KC)
    gmem_t = vview(L * KC).rearrange("p (l k) -> p l k", k=KC)
    gq_t = vview(L)
    gk_t = vview(L)
    gsub_t = vview(L)
    gqm_t = vview(L)
    gkm_t = vview(L)
    convw_t = vview(L * 12).rearrange("p (l t j) -> p l t j", t=3, j=4)
    gconv_t = vview(L * 4).rearrange("p (l j) -> p l j", j=4)
    lams_t = ar.alloc("lams_t", [P, 2, L], F32)
    neglam_t = ar.alloc("neglam_t", [P, L], F32)
    kmT = ar.alloc("kmT", [P, 4, MEM], BF16)
    vm = ar.alloc("vm", [P, 2, 512], BF16)
    base_mark = ar.mark()
    lamv_t = [ar.alloc(f"lamv{i}", [P, 1, L * 64], F32) for i in range(4)]
    lamp_t = ar.alloc("lamp_t", [P, L, 64], F32)
    ar.release(base_mark)

    pst = [nc.alloc_psum_tensor(f"ps{i}", [P, 1024], F32) for i in range(4)]

    def bank(b):
        return pst[b // 2][:, (b % 2) * 512:(b % 2) * 512 + 512]

    def bkey(b):
        return ("ps", b)

    dbg_n = [0]

    def dbg_dump(name, ap, shape, dt, reads):
        if not debug:
            return
        dbg_n[0] += 1
        t = nc.dram_tensor("dbg_" + name, list(shape), dt, kind="ExternalOutput").ap()
        tr.dma("sp", [(t, ap)], dsem(("dbg", dbg_n[0])), reads=reads)

    def mm(out, lhsT, rhs, start, stop, reads, writes):
        return tr.emit("pe", lambda e, o=out, l=lhsT, r=rhs, s=start, t=stop:
                       e.matmul(o, lhsT=l, rhs=r, start=s, stop=t), reads, writes)

    def act(out, in_, func, reads, writes, scale=1.0, bias=None):
        if bias is None:
            return tr.emit("act", lambda e, o=out, i=in_, f=func, s=scale:
                           e.activation(out=o, in_=i, func=f, scale=s), reads, writes)
        return tr.emit("act", lambda e, o=out, i=in_, f=func, s=scale, b=bias:
                       e.activation(out=o, in_=i, func=f, scale=s, bias=b), reads, writes)

    def stt(eng, out, in0, scalar, in1, op0, op1, reads, writes):
        return tr.emit(eng, lambda e, o=out, a=in0, s=scalar, b=in1, p0=op0, p1=op1:
                       e.scalar_tensor_tensor(out=o, in0=a, scalar=s, in1=b, op0=p0, op1=p1), reads, writes)

    def tt(eng, out, in0, in1, op, reads, writes):
        return tr.emit(eng, lambda e, o=out, a=in0, b=in1, p=op:
                       e.tensor_tensor(out=o, in0=a, in1=b, op=p), reads, writes)

    def ts(eng, out, in0, s1, op0, reads, writes, s2=None, op1=None):
        if op1 is None:
            return tr.emit(eng, lambda e, o=out, a=in0, s=s1, p=op0:
                           e.tensor_scalar(out=o, in0=a, scalar1=s, scalar2=None, op0=p), reads, writes)
        return tr.emit(eng, lambda e, o=out, a=in0, s=s1, p=op0, s2=s2, p1=op1:
                       e.tensor_scalar(out=o, in0=a, scalar1=s, scalar2=s2, op0=p, op1=p1), reads, writes)

    def cp(eng, out, in_, reads, writes):
        if eng == "act":
            return tr.emit("act", lambda e, o=out, i=in_: e.copy(out=o, in_=i), reads, writes)
        return tr.emit(eng, lambda e, o=out, i=in_: e.tensor_copy(out=o, in_=i), reads, writes)

    def recip(out, in_, reads, writes):
        return tr.emit("dve", lambda e, o=out, i=in_: e.reciprocal(out=o, in_=i), reads, writes)

    def memset(eng, ap, val, writes):
        return tr.emit(eng, lambda e, a=ap, v=val: e.memset(a, v), (), writes)

    bank_rr = [0]

    def next_bank(n=6):
        b = bank_rr[0] % n
        bank_rr[0] += 1
        return b

    rr = {}

    def rot(name, n):
        i = rr.get(name, 0)
        rr[name] = i + 1
        return i % n

    wsrc = {"in": w_in_d, "kv": w_kv_d, "out": w_out_d, "up": w_up_d, "down": w_down_d}
    wnblk = {"in": 10, "kv": 2, "out": 4, "up": 16, "down": 16}

    def convert(name, l, blocks=None, pace=()):
        pairs = []
        for b in (range(wnblk[name]) if blocks is None else blocks):
            if name == "down":
                half, cb = b // 8, b % 8
                src = w_down_d[l, half * 4096:(half + 1) * 4096, cb * 256:(cb + 1) * 256].rearrange(
                    "(kc p) n -> p kc n", p=P)
                dst = wb[name][l, b].rearrange("p (kc n) -> p kc n", n=256)
            else:
                src = wsrc[name][l, :, b * 512:(b + 1) * 512].rearrange("(kc p) n -> p kc n", p=P)
                dst = wb[name][l, b].rearrange("p (kc n) -> p kc n", n=512)
            pairs.append((dst, src))
        tr.dma("pool", pairs, dsem(("cv", name, l)), reads=pace, writes=[("wb", name, l)])

    wseq = []
    wphase = []
    for l in range(L + 1):
        n0 = len(wseq)
        if l < L:
            wseq += [("kv", l, 0), ("kv", l, 1)]
        for t in range(NT):
            if l > 0:
                wseq += [("out", l - 1, b) for b in range(4)]
                for half in range(2):
                    wseq += [("up", l - 1, half * 8 + b) for b in range(8)]
                    wseq += [("down", l - 1, half * 8 + b) for b in range(8)]
            if l < L:
                wseq += [("in", l, b) for b in range(10)]
        wphase += [l] * (len(wseq) - n0)
    wpos = [0]
    wld = [0]

    def wnext(expect):
        i = wpos[0]
        assert wseq[i] == expect, (wseq[i], expect)
        while wld[0] < len(wseq) and wld[0] <= i + 2 and wphase[wld[0]] == wphase[i]:
            j = wld[0]
            name, l, b = wseq[j]
            tr.dma("sp", [(wbuf[j % 3][:, :], wb[name][l, b])], dsem(("wl", j % 3)),
                   reads=[("wb", name, l)], writes=[("wbuf", j % 3)])
            wld[0] += 1
        wpos[0] += 1
        return wbuf[i % 3], ("wbuf", i % 3)

    tr.dma("sp", [(ident_f[:, :], ident_d)], dsem("c0"), writes=["ident_f"])
    tr.dma("pool", [(ident_b[:, :], ident_d)], dsem("c1"), writes=["ident_b"])
    tr.dma("pool", [(diagB[:, :, :], diagb_d.rearrange("h i j -> i h j"))], dsem("c2"), writes=["diagB"])
    tr.dma("pool", [(qaug_b, qaug_d)], dsem("c3"), writes=["qaug_b"])
    convert("in", 0)
    convert("kv", 0)
    memset("dve", ones_b[:, :], 1.0, ["ones_b"])
    memset("dve", bd_b[:, :], 0.0, ["bd_b"])
    memset("dve", bd_b[0:64, 0:64], 1.0, ["bd_b"])
    memset("dve", bd_b[64:128, 64:128], 1.0, ["bd_b"])
    memset("dve", eps_t[:, :], EPS, ["eps_t"])
    tr.dma("sp", [(vecs_t[:, :], vecs_d)], dsem("c4"), writes=["small"])
    tr.dma("sp", [(lamv_t[i][:, :, :], lam_d[i].partition_broadcast(P)) for i in range(4)], dsem("c5"),
           writes=["lamv"])
    for j in range(2):
        tt("dve", lamp_t[:, :, :], lamv_t[2 * j][:, 0, :].rearrange("p (l d) -> p l d", d=64),
           lamv_t[2 * j + 1][:, 0, :].rearrange("p (l d) -> p l d", d=64), ALU.mult, ["lamv"], ["lamp"])
        tr.emit("dve", lambda e, o=lams_t[:, j, :], i=lamp_t[:, :, :]:
                e.tensor_reduce(out=o, in_=i, axis=AX.X, op=ALU.add), ["lamp"], ["lams"])
    act(lams_t[:, :, :], lams_t[:, :, :], AF.Exp, ["lams"], ["lams"])
    for l in range(L):
        stt("dve", neglam_t[:, l:l + 1], lams_t[:, 1, l:l + 1], -lambda_init(l), lams_t[:, 0, l:l + 1],
            ALU.add, ALU.subtract, ["lams"], ["neglam"])
        ts("dve", gsub_t[:, l:l + 1], gsub_t[:, l:l + 1], 1.0 - lambda_init(l), ALU.mult, ["small"], ["small"])
    dbg_dump("neglam", neglam_t[:, :], [P, L], F32, ["neglam"])
    dbg_dump("lams", lams_t[:, :, :], [P, 2, L], F32, ["lams"])
    dbg_dump("lamp", lamp_t[:, :, :], [P, L, 64], F32, ["lamp"])
    dbg_dump("vecs", vecs_t[:, :], [P, NV], F32, ["small"])
    tr.barrier()

    def rms_stats_to_rs(psb, pkey, n, lnv, rsv, lkey, width):
        act(lnv, psb, AF.Ln, [pkey], [lkey + "_ln"], scale=1.0 / n, bias=eps_t[:, 0:1])
        act(rsv, lnv, AF.Exp, [lkey + "_ln"], [lkey], scale=-0.5)

    def phase_kv(l):
        m0 = ar.mark()
        memtok = ar.alloc("memtok", [P, 2, D], F32)
        memT = ar.alloc("memT", [P, KC, MEM], F32)
        memsq = ar.alloc("memsq", [P, KC, MEM], BF16)
        memn = ar.alloc("memn", [P, KC, MEM], BF16)
        lnv = ar.alloc("kv_ln", [P, MEM], F32)
        rsv = ar.alloc("kv_rs", [P, MEM], F32)
        sqk = ar.alloc("kv_sqk", [P, MEM], BF16)
        tr.dma("sp", [(memtok[:, :, :], mem_d.rearrange("(mt p) d -> p mt d", p=P))], dsem("memtok"),
               writes=["memtok"])
        for kc in range(KC):
            b = next_bank()
            for mt in range(2):
                tr.emit("pe", lambda e, o=bank(b)[:, mt * P:(mt + 1) * P], i=memtok[:, mt, kc * P:(kc + 1) * P]:
                        e.transpose(o, i, ident_f[:, :]), ["memtok", "ident_f"], [bkey(b)])
            cp("act" if kc % 2 else "dve", memT[:, kc, :], bank(b)[:, 0:MEM], [bkey(b)], [("memT", kc)])
            act(memsq[:, kc, :], memT[:, kc, :], AF.Square, [("memT", kc)], [("memsq", kc)])
        b = next_bank()
        for kc in range(KC):
            mm(bank(b)[:, 0:MEM], ones_b[:, :], memsq[:, kc, :], kc == 0, kc == KC - 1,
               [("memsq", kc), "ones_b"], [bkey(b)])
        rms_stats_to_rs(bank(b)[:, 0:MEM], bkey(b), D, lnv[:, :], rsv[:, :], "kvrs", MEM)
        for kc in range(KC):
            stt("dve", memn[:, kc, :], memT[:, kc, :], gmem_t[:, l, kc:kc + 1], rsv[:, :], ALU.mult, ALU.mult,
                [("memT", kc), "kvrs"], [("memn", kc)])
        w, wk = wnext(("kv", l, 0))
        w3 = w[:, :].rearrange("p (kc n) -> p kc n", n=512)
        for h in range(4):
            b = next_bank()
            for kc in range(KC):
                mm(bank(b)[:, 0:MEM], w3[:, kc, h * P:(h + 1) * P], memn[:, kc, :], kc == 0, kc == KC - 1,
                   [wk, ("memn", kc)], [bkey(b)])
            act(sqk[:, :], bank(b)[:, 0:MEM], AF.Square, [bkey(b)], ["sqk"])
            b2 = next_bank()
            mm(bank(b2)[:, 0:MEM], ones_b[:, :], sqk[:, :], True, True, ["sqk", "ones_b"], [bkey(b2)])
            rms_stats_to_rs(bank(b2)[:, 0:MEM], bkey(b2), P, lnv[:, :], rsv[:, :], "kvrs2", MEM)
            stt("dve", kmT[:, h, :], bank(b)[:, 0:MEM], gkm_t[:, l:l + 1], rsv[:, :], ALU.mult, ALU.mult,
                [bkey(b), "kvrs2"], ["kmT"])
        w, wk = wnext(("kv", l, 1))
        w3 = w[:, :].rearrange("p (kc n) -> p kc n", n=512)
        for mt in range(2):
            b = next_bank()
            for kc in range(KC):
                mm(bank(b), memn[:, kc, mt * P:(mt + 1) * P], w3[:, kc, :], kc == 0, kc == KC - 1,
                   [wk, ("memn", kc)], [bkey(b)])
            cp("dve", vm[:, mt, :], bank(b), [bkey(b)], ["vm"])
        tr.barrier()
        ar.release(m0)

    def phase_p(l):
        m0 = ar.mark()
        xT = ar.alloc("xT", [P, KC, T], F32)
        hT = ar.alloc("hT", [P, KC, T], BF16)
        mixT = ar.alloc("mixT", [P, KC, T], BF16)
        aT = ar.alloc("aT", [P, 32, T], BF16)
        gcb = ar.alloc("gcb", [P, 4, T], F32)
        sq = [ar.alloc(f"sq{i}", [P, T], BF16) for i in range(4)]
        lnv = [ar.alloc(f"lnv{i}", [P, T], F32) for i in range(2)]
        rsv = [ar.alloc(f"rsv{i}", [P, T], F32) for i in range(2)]
        rl = [ar.alloc(f"rl{i}", [P, T], F32) for i in range(2)]
        qst = [ar.alloc(f"qst{i}", [P, T], BF16) for i in range(2)]
        vst = [ar.alloc(f"vst{i}", [P, T], BF16) for i in range(2)]
        gst = [ar.alloc(f"gst{i}", [P, T], BF16) for i in range(2)]
        most = [ar.alloc(f"most{i}", [P, T], BF16) for i in range(2)]
        qmn = [ar.alloc(f"qmn{i}", [P, T], BF16) for i in range(2)]
        pm = [ar.alloc(f"pm{i}", [P, 2 * T], BF16) for i in range(2)]
        rinv = [ar.alloc(f"rinv{i}", [P, T], F32) for i in range(2)]
        zc = ar.alloc("zc", [P, 4, T + 2], BF16)
        gbc = ar.alloc("gbc", [P, 4, T], BF16)
        cy = ar.alloc("cy", [P, T], F32)
        xin = nc.alloc_sbuf_tensor_at(f"xin_{l}", [P, 4, D], F32, offset=ar.off["aT"])
        xout = xin

        SB = 6

        sq_pending = []

        def sq_accum(c):
            while sq_pending:
                sq_pending.pop(0)()
            i = rot("sq", 4)
            act(sq[i][:, :], xT[:, c, :], AF.Square, [("xT", c)], [("sq", i)])
            sq_pending.append(lambda c=c, i=i: mm(bank(SB), ones_b[:, :], sq[i][:, :], c == 0, c == KC - 1,
                                                  [("sq", i), "ones_b"], [bkey(SB)]))

        def norm_finish(g_t, tagl):
            while sq_pending:
                sq_pending.pop(0)()
            i = rot("lnv", 2)
            act(lnv[i][:, :], bank(SB), AF.Ln, [bkey(SB)], [("lnv", i)], scale=1.0 / D, bias=eps_t[:, 0:1])
            act(rsv[i][:, :], lnv[i][:, :], AF.Exp, [("lnv", i)], [("rsv", i)], scale=-0.5)
            for kc in range(KC):
                stt("dve", hT[:, kc, :], xT[:, kc, :], g_t[:, tagl, kc:kc + 1],
                    rsv[i][:, :], ALU.mult, ALU.mult, [("xT", kc), ("rsv", i)], [("hT", kc)])

        def load_x(tt_):
            s1 = tt_ * T
            if l == 0:
                tr.dma("sp", [(xin[:, :, :], x_d[s1:s1 + T, :].rearrange("(j p) d -> p j d", p=P))], dsem("xin"),
                       writes=["xin"])
            else:
                tr.dma("sp", [(xT[:, :, :], xres[:, :, s1:s1 + T])], dsem("xT"),
                       writes=[("xT", kc) for kc in range(KC)])

        def load_mix(tt_):
            s1 = tt_ * T
            tr.dma("sp", [(mixT[:, 0:8, :], mixT_s[0:8, :, s1:s1 + T].rearrange("c p s -> p c s")),
                          (mixT[:, 12:16, :], mixT_s[12:16, :, s1:s1 + T].rearrange("c p s -> p c s"))],
                   dsem("mixTl"), writes=[("mixT", kc) for kc in range(8)] + [("mixT", kc) for kc in range(12, 16)])
            a = max(s1 - 1, 0)
            b_ = min(s1 + T + 1, S)
            if s1 == 0:
                memset("dve", zc[:, :, 0:1], 0.0, ["zc"])
            if s1 + T == S:
                memset("dve", zc[:, :, T + 1:T + 2], 0.0, ["zc"])
            tr.dma("sp", [(zc[:, :, a - (s1 - 1):b_ - (s1 - 1)], zT_s[:, :, a:b_].rearrange("j p s -> p j s")),
                          (gbc[:, :, :], gbT_s[:, :, s1:s1 + T].rearrange("j p s -> p j s"))],
                   dsem("zcl"), writes=["zc", "gbc"])

        def prep_steps():
            lp = l - 1
            steps = []
            for j in range(4):
                def conv_step(j=j):
                    ts("dve", cy[:, :], zc[:, j, 1:T + 1], convw_t[:, lp, 1, j:j + 1], ALU.mult, ["zc"], ["cy"])
                    stt("dve", cy[:, :], zc[:, j, 0:T], convw_t[:, lp, 0, j:j + 1], cy[:, :], ALU.mult, ALU.add,
                        ["zc", "cy"], ["cy"])
                    stt("dve", cy[:, :], zc[:, j, 2:T + 2], convw_t[:, lp, 2, j:j + 1], cy[:, :], ALU.mult, ALU.add,
                        ["zc", "cy"], ["cy"])
                    tt("dve", mixT[:, 8 + j, :], cy[:, :], gbc[:, j, :], ALU.mult, ["cy", "gbc"], [("mixT", 8 + j)])
                steps.append(conv_step)
            for kc in range(12):
                def norm_step(kc=kc):
                    i = rot("sq", 4)
                    act(sq[i][:, :], mixT[:, kc, :], AF.Square, [("mixT", kc)], [("sq", i)])
                    b = next_bank()
                    mm(bank(b), ones_b[:, :], sq[i][:, :], True, True, [("sq", i), "ones_b"], [bkey(b)])
                    k = rot("lnv", 2)
                    act(lnv[k][:, :], bank(b), AF.Ln, [bkey(b)], [("lnv", k)], scale=1.0 / P, bias=eps_t[:, 0:1])
                    act(rsv[k][:, :], lnv[k][:, :], AF.Exp, [("lnv", k)], [("rsv", k)], scale=-0.5)
                    g = gsub_t[:, lp:lp + 1] if kc < 8 else gconv_t[:, lp, kc - 8:kc - 7]
                    stt("dve", mixT[:, kc, :], mixT[:, kc, :], g, rsv[k][:, :], ALU.mult, ALU.mult,
                        [("mixT", kc), ("rsv", k)], [("mixT", kc)])
                steps.append(norm_step)
            return steps

        for t in range(NT):
            s0 = t * T
            if t == 0:
                load_x(0)
                if l > 0:
                    load_mix(0)
                    for st_ in prep_steps():
                        st_()
            if l == 0:
                for kc in range(KC):
                    b = next_bank()
                    for j in range(4):
                        tr.emit("pe", lambda e, o=bank(b)[:, j * P:(j + 1) * P], i=xin[:, j, kc * P:(kc + 1) * P]:
                                e.transpose(o, i, ident_f[:, :]), ["xin", "ident_f"], [bkey(b)])
                    cp("act" if kc % 2 else "dve", xT[:, kc, :], bank(b), [bkey(b)], [("xT", kc)])
                    sq_accum(kc)
                if t + 1 < NT:
                    load_x(t + 1)
            else:
                lp = l - 1
                for g in range(4):
                    w, wk = wnext(("out", lp, g))
                    w3 = w[:, :].rearrange("p (kc n) -> p kc n", n=512)
                    for dc in range(4):
                        b = next_bank()
                        for kc in range(KC):
                            mm(bank(b), w3[:, kc, dc * P:(dc + 1) * P], mixT[:, kc, :], kc == 0, kc == KC - 1,
                               [wk, ("mixT", kc)], [bkey(b)])
                        c = 4 * g + dc
                        tt("dve", xT[:, c, :], xT[:, c, :], bank(b), ALU.add, [("xT", c), bkey(b)], [("xT", c)])
                        sq_accum(c)
                norm_finish(gmlp_t, lp)
                pending = []
                if t + 1 < NT:
                    load_mix(t + 1)
                    pending = prep_steps()
                for half in range(2):
                    for g in range(8):
                        if pending:
                            pending.pop(0)()
                        w, wk = wnext(("up", lp, half * 8 + g))
                        w3 = w[:, :].rearrange("p (kc n) -> p kc n", n=512)
                        for hc in range(4):
                            b = next_bank()
                            for kc in range(KC):
                                mm(bank(b), w3[:, kc, hc * P:(hc + 1) * P], hT[:, kc, :], kc == 0, kc == KC - 1,
                                   [wk, ("hT", kc)], [bkey(b)])
                            i = rot("rl", 2)
                            a = g * 4 + hc
                            act(rl[i][:, :], bank(b), AF.Relu, [bkey(b)], [("rl", i)])
                            tt("dve", aT[:, a, :], rl[i][:, :], rl[i][:, :], ALU.mult,
                               [("rl", i)], [("aT", a)])
                    for g in range(8):
                        w, wk = wnext(("down", lp, half * 8 + g))
                        w3 = w[:, :].rearrange("p (kc n) -> p kc n", n=256)
                        for dc in range(2):
                            b = next_bank()
                            for a in range(32):
                                mm(bank(b), w3[:, a, dc * P:(dc + 1) * P], aT[:, a, :], a == 0, a == 31,
                                   [wk, ("aT", a)], [bkey(b)])
                            c = 2 * g + dc
                            tt("dve", xT[:, c, :], xT[:, c, :], bank(b), ALU.add, [("xT", c), bkey(b)], [("xT", c)])
                            if half == 1 and l < L:
                                sq_accum(c)
            if l == L:
                for j in range(4):
                    for c4 in range(4):
                        b = next_bank()
                        for k in range(4):
                            kc = c4 * 4 + k
                            tr.emit("pe", lambda e, o=bank(b)[:, k * P:(k + 1) * P], i=xT[:, kc, j * P:(j + 1) * P]:
                                    e.transpose(o, i, ident_f[:, :]), [("xT", kc), "ident_f"], [bkey(b)])
                        cp("act" if c4 % 2 else "dve", xout[:, j, c4 * 512:(c4 + 1) * 512], bank(b), [bkey(b)],
                           [("aT", a_) for a_ in range(32)])
                tr.dma("sp", [(out_d[s0:s0 + T, :].rearrange("(j p) d -> p j d", p=P), xout[:, :, :])],
                       dsem("xout"), reads=[("aT", a_) for a_ in range(32)])
                if t + 1 < NT:
                    load_x(t + 1)
                continue
            if l > 0:
                tr.dma("sp", [(xres[:, :, s0:s0 + T], xT[:, :, :])], dsem("xres_st"),
                       reads=[("xT", kc) for kc in range(KC)])
            else:
                tr.dma("sp", [(xres[:, :, s0:s0 + T], xT[:, :, :])], dsem("xres_st"),
                       reads=[("xT", kc) for kc in range(KC)])
            norm_finish(gmix_t, l)
            if l > 0 and t + 1 < NT:
                load_x(t + 1)
            pend = None

            def finish_qk(item):
                b, which, h = item
                i = rot("sq", 4)
                act(sq[i][:, :], bank(b), AF.Square, [bkey(b)], [("sq", i)])
                b2 = next_bank()
                mm(bank(b2), bd_b[:, :], sq[i][:, :], True, True, [("sq", i), "bd_b"], [bkey(b2)])
                k = rot("lnv", 2)
                act(lnv[k][:, :], bank(b2), AF.Ln, [bkey(b2)], [("lnv", k)], scale=1.0 / 64, bias=eps_t[:, 0:1])
                act(rsv[k][:, :], lnv[k][:, :], AF.Exp, [("lnv", k)], [("rsv", k)], scale=-0.5)
                s = rot("qst", 2)
                g = gq_t if which == 0 else gk_t
                stt("dve", qst[s][:, :], bank(b), g[:, l:l + 1], rsv[k][:, :], ALU.mult, ALU.mult,
                    [bkey(b), ("rsv", k)], [("qst", s)])
                dst = qT_s if which == 0 else kT_s
                tr.dma("sp", [(dst[h, :, s0:s0 + T], qst[s][:, :])], dsem(("qst", s)), reads=[("qst", s)])

            for blk in range(4):
                w, wk = wnext(("in", l, blk))
                w3 = w[:, :].rearrange("p (kc n) -> p kc n", n=512)
                for oc in range(4):
                    b = next_bank()
                    for kc in range(KC):
                        mm(bank(b), w3[:, kc, oc * P:(oc + 1) * P], hT[:, kc, :], kc == 0, kc == KC - 1,
                           [wk, ("hT", kc)], [bkey(b)])
                    if pend is not None:
                        finish_qk(pend)
                    pend = (b, blk // 2, (blk % 2) * 4 + oc)
            for blk in range(4, 6):
                w, wk = wnext(("in", l, blk))
                w3 = w[:, :].rearrange("p (kc n) -> p kc n", n=512)
                for j in range(4):
                    b = next_bank()
                    for kc in range(KC):
                        mm(bank(b), hT[:, kc, j * P:(j + 1) * P], w3[:, kc, :], kc == 0, kc == KC - 1,
                           [wk, ("hT", kc)], [bkey(b)])
                    if pend is not None:
                        finish_qk(pend)
                        pend = None
                    s = rot("vst", 2)
                    cp("act", vst[s][:, :], bank(b), [bkey(b)], [("vst", s)])
                    hb = (blk - 4) * 4
                    tr.dma("sp", [(v_s[hb:hb + 4, :, t * 4 + j, :].rearrange("h p d -> p h d"),
                                   vst[s][:, :].rearrange("p (h d) -> p h d", d=P))],
                           dsem(("vst", s)), reads=[("vst", s)])
            for blk in range(6, 9):
                w, wk = wnext(("in", l, blk))
                w3 = w[:, :].rearrange("p (kc n) -> p kc n", n=512)
                for oc in range(4):
                    b = next_bank()
                    for kc in range(KC):
                        mm(bank(b), w3[:, kc, oc * P:(oc + 1) * P], hT[:, kc, :], kc == 0, kc == KC - 1,
                           [wk, ("hT", kc)], [bkey(b)])
                    if blk == 6:
                        s = rot("gst", 2)
                        cp("act", gst[s][:, :], bank(b), [bkey(b)], [("gst", s)])
                        tr.dma("sp", [(gbT_s[oc, :, s0:s0 + T], gst[s][:, :])], dsem(("gst", s)), reads=[("gst", s)])
                    elif blk == 7:
                        cp("act", gcb[:, oc, :], bank(b), [bkey(b)], [("gcb", oc)])
                    else:
                        s = rot("gst", 2)
                        tt("dve", gst[s][:, :], bank(b), gcb[:, oc, :], ALU.mult, [bkey(b), ("gcb", oc)],
                           [("gst", s)])
                        tr.dma("sp", [(zT_s[oc, :, s0:s0 + T], gst[s][:, :])], dsem(("gst", s)), reads=[("gst", s)])
            w, wk = wnext(("in", l, 9))
            w3 = w[:, :].rearrange("p (kc n) -> p kc n", n=512)
            st8 = [dict() for _ in range(4)]

            def stA(h):
                b = next_bank()
                for kc in range(KC):
                    mm(bank(b), w3[:, kc, h * P:(h + 1) * P], hT[:, kc, :], kc == 0, kc == KC - 1,
                       [wk, ("hT", kc)], [bkey(b)])
                i = rot("sq", 4)
                act(sq[i][:, :], bank(b), AF.Square, [bkey(b)], [("sq", i)])
                st8[h].update(b=b, i=i)

            def stB(h):
                b, i = st8[h]["b"], st8[h]["i"]
                b2 = next_bank()
                mm(bank(b2), ones_b[:, :], sq[i][:, :], True, True, [("sq", i), "ones_b"], [bkey(b2)])
                k = rot("lnv", 2)
                act(lnv[k][:, :], bank(b2), AF.Ln, [bkey(b2)], [("lnv", k)], scale=1.0 / P, bias=eps_t[:, 0:1])
                act(rsv[k][:, :], lnv[k][:, :], AF.Exp, [("lnv", k)], [("rsv", k)], scale=-0.5)
                qi = rot("qmn", 2)
                stt("dve", qmn[qi][:, :], bank(b), gqm_t[:, l:l + 1], rsv[k][:, :], ALU.mult, ALU.mult,
                    [bkey(b), ("rsv", k)], [("qmn", qi)])
                st8[h].update(qi=qi)

            def stC(h):
                qi = st8[h]["qi"]
                for mt in range(2):
                    mm(bank(6 + mt), kmT[:, h, mt * P:(mt + 1) * P], qmn[qi][:, :], True, True,
                       [("qmn", qi), "kmT"], [bkey(6 + mt)])
                pi = rot("pm", 2)
                act(pm[pi][:, :], pst[3][:, :], AF.Exp, [bkey(6), bkey(7)], [("pm", pi)], scale=float(P) ** -0.5)
                st8[h].update(pi=pi)

            def stD(h):
                pi = st8[h]["pi"]
                bo = next_bank()
                for mt in range(2):
                    mm(bank(bo), vm[:, mt, h * P:(h + 1) * P], pm[pi][:, mt * T:(mt + 1) * T], mt == 0, mt == 1,
                       [("pm", pi), "vm"], [bkey(bo)])
                bs = next_bank()
                for mt in range(2):
                    mm(bank(bs), ones_b[:, :], pm[pi][:, mt * T:(mt + 1) * T], mt == 0, mt == 1,
                       [("pm", pi), "ones_b"], [bkey(bs)])
                ri = rot("rinv", 2)
                recip(rinv[ri][:, :], bank(bs), [bkey(bs)], [("rinv", ri)])
                s_ = rot("most", 2)
                tt("dve", most[s_][:, :], bank(bo), rinv[ri][:, :], ALU.mult, [bkey(bo), ("rinv", ri)],
                   [("most", s_)])
                tr.dma("sp", [(mixT_s[12 + h, :, s0:s0 + T], most[s_][:, :])], dsem(("most", s_)),
                       reads=[("most", s_)])

            stA(0); stA(1); stB(0); stA(2); stB(1); stC(0); stA(3); stB(2); stC(1); stD(0)
            stB(3); stC(2); stD(1); stC(3); stD(2); stD(3)
        tr.barrier()
        ar.release(m0)

    def skip_tile(h, qb, kt):
        if skip_thresh is None:
            return False
        q0, q1 = qb * 512, qb * 512 + 511
        k0, k1 = kt * P, kt * P + P - 1
        if k1 < q0:
            md = q0 - k1
        elif k0 > q1:
            md = k0 - q1
        else:
            return False
        return (2.0 ** -(h + 1)) * md >= skip_thresh

    def phase_m(l):
        m0 = ar.mark()
        sets = []
        set_off = []
        for s in range(2):
            set_off.append((ar.mark() + 31) // 32 * 32)
            d = {}
            d["KT"] = [ar.alloc(f"KT{s}{c}", [68, S], BF16) for c in range(2)]
            d["QA"] = [ar.alloc(f"QA{s}{c}", [68, S], BF16) for c in range(2)]
            d["QB"] = [ar.alloc(f"QB{s}{c}", [68, S], BF16) for c in range(2)]
            d["V"] = ar.alloc(f"V{s}", [P, NKT, P], BF16)
            sets.append(d)
        PT = [ar.alloc(f"PT{i}", [P, 1024], BF16) for i in range(3)]
        r0 = ar.alloc("r0", [P, 512], F32)
        r1 = ar.alloc("r1", [P, 512], F32)
        o_f = ar.alloc("o_f", [P, 512], F32)
        t_f = ar.alloc("t_f", [P, 512], F32)
        lnv = ar.alloc("m_lnv", [P, 512], F32)
        rsv = ar.alloc("m_rsv", [P, 512], F32)
        sqo = ar.alloc("sqo", [P, 512], BF16)
        ost = [ar.alloc(f"ost{i}", [P, 512], BF16) for i in range(2)]

        def load_head(h, s):
            d = sets[s]
            pairs = []
            keys = []
            for c in range(2):
                pairs.append((d["KT"][c][0:64, :], kT_s[h, c * 64:(c + 1) * 64, :]))
                pairs.append((d["QA"][c][0:64, :], qT_s[h, c * 64:(c + 1) * 64, :]))
                pairs.append((d["QB"][c][0:64, :], qT_s[h, c * 64:(c + 1) * 64, :]))
                pairs.append((d["QA"][c][64:68, :], qaug_b[h, 0]))
                pairs.append((d["QB"][c][64:68, :], qaug_b[h, 1]))
            pairs.append((d["V"][:, :, :], v_s[h]))
            tr.dma("sp", pairs, dsem(("head", s)), writes=[("head", s)])

        tr.dma("pool", [(sets[0]["KT"][c][64:68, :], kaug_d) for c in range(2)], dsem("kaug"),
               writes=[("head", 0)])
        load_head(0, 0)

        tr.dma("pool", [(sets[1]["KT"][c][64:68, :], kaug_d) for c in range(2)], dsem("kaug"),
               writes=[("head", 1)])

        for pc in conv_pieces(l):
            convert(pc[0], pc[1], pc[2])

        PV = [4, 5]
        LS = [6, 7]
        for h in range(NH):
            s = h % 2
            d = sets[s]
            hk = ("head", s)
            if h + 1 < NH:
                load_head(h + 1, 1 - s)
            for qb in range(NQB):
                q0 = qb * 512
                kts = [kt for kt in range(NKT) if not skip_tile(h, qb, kt)]
                n = len(kts)

                def qk(i):
                    kt = kts[i]
                    k0 = kt * P
                    pr = i % 2
                    for c in range(2):
                        b = 2 * pr + c
                        KTc, QAc, QBc = d["KT"][c], d["QA"][c], d["QB"][c]
                        if k0 + P <= q0:
                            mm(bank(b), KTc[0:68, k0:k0 + P], QAc[0:68, q0:q0 + 512], True, True, [hk], [bkey(b)])
                        elif k0 >= q0 + 512:
                            mm(bank(b), KTc[0:68, k0:k0 + P], QBc[0:68, q0:q0 + 512], True, True, [hk], [bkey(b)])
                        else:
                            js = (k0 - q0) // P
                            if js > 0:
                                mm(bank(b)[:, 0:js * P], KTc[0:68, k0:k0 + P], QBc[0:68, q0:q0 + js * P],
                                   True, True, [hk], [bkey(b)])
                            mm(bank(b)[:, js * P:(js + 1) * P], KTc[0:64, k0:k0 + P],
                               QAc[0:64, q0 + js * P:q0 + (js + 1) * P], True, False, [hk], [bkey(b)])
                            mm(bank(b)[:, js * P:(js + 1) * P], ident_b[:, :], diagB[:, h, :], False, True,
                               ["ident_b", "diagB"], [bkey(b)])
                            if js < 3:
                                mm(bank(b)[:, (js + 1) * P:512], KTc[0:68, k0:k0 + P],
                                   QAc[0:68, q0 + (js + 1) * P:q0 + 512], True, True, [hk], [bkey(b)])

                def expav(i):
                    kt = kts[i]
                    pr = i % 2
                    pi = i % 3
                    act(PT[pi][:, :], pst[pr][:, :], AF.Exp, [bkey(2 * pr), bkey(2 * pr + 1)], [("PT", pi)],
                        scale=0.125)
                    first, last = (i == 0), (i == n - 1)
                    if h == 0 and qb == 0 and l == 0:
                        dbg_dump(f"PT{i}", PT[pi][:, :], [P, 1024], BF16, [("PT", pi)])
                    for c in range(2):
                        mm(bank(PV[c]), d["V"][:, kt, :], PT[pi][:, c * 512:(c + 1) * 512], first, last,
                           [hk, ("PT", pi)], [bkey(PV[c])])
                    for c in range(2):
                        mm(bank(LS[c]), ones_b[:, :], PT[pi][:, c * 512:(c + 1) * 512], first, last,
                           ["ones_b", ("PT", pi)], [bkey(LS[c])])

                qk(0)
                for i in range(n):
                    if i + 1 < n:
                        qk(i + 1)
                    expav(i)
                cp("act", t_f[:, :], bank(PV[1]), [bkey(PV[1])], ["t_f"])
                cp("dve", o_f[:, :], bank(PV[0]), [bkey(PV[0])], ["o_f"])
                cp("dve", r0[:, :], bank(LS[0]), [bkey(LS[0])], ["r0"])
                cp("dve", r1[:, :], bank(LS[1]), [bkey(LS[1])], ["r1"])
                recip(r0[:, :], r0[:, :], ["r0"], ["r0"])
                recip(r1[:, :], r1[:, :], ["r1"], ["r1"])
                tt("dve", o_f[:, :], o_f[:, :], r0[:, :], ALU.mult, ["o_f", "r0"], ["o_f"])
                tt("dve", t_f[:, :], t_f[:, :], r1[:, :], ALU.mult, ["t_f", "r1"], ["t_f"])
                oi = rot("ost", 2)
                stt("dve", ost[oi][:, :], t_f[:, :], neglam_t[:, l:l + 1], o_f[:, :], ALU.mult, ALU.add,
                    ["t_f", "o_f"], [("ost", oi)])
                tr.dma("sp", [(mixT_s[h, :, q0:q0 + 512], ost[oi][:, :])], dsem(("ost", oi)), reads=[("ost", oi)])
        tr.barrier()
        ar.release(m0)

    def conv_pieces(l):
        pcs = [("out", l, range(0, 4))]
        for q4 in range(4):
            pcs.append(("up", l, range(q4 * 4, q4 * 4 + 4)))
            pcs.append(("down", l, range(q4 * 4, q4 * 4 + 4)))
        if l + 1 < L:
            pcs.append(("in", l + 1, range(0, 5)))
            pcs.append(("in", l + 1, range(5, 10)))
            pcs.append(("kv", l + 1, range(0, 2)))
        return pcs

    for l in range(L):
        phase_kv(l)
        phase_p(l)
        phase_m(l)
    phase_p(L)
    tr.barrier()
    assert wpos[0] == len(wseq), (wpos[0], len(wseq))

    tr.finalize(engsem)
    with stack:
        with nc.Block() as block:
            @block.tensor
            def _(e):
                tr.replay("pe", e)

            @block.scalar
            def _(e):
                tr.replay("act", e)

            @block.vector
            def _(e):
                tr.replay("dve", e)

            @block.gpsimd
            def _(e):
                tr.replay("pool", e)

            @block.sync
            def _(e):
                tr.replay("sp", e)
    return nc


def make_consts(S):
    ident = np.eye(P, dtype=np.float32)
    pos = np.arange(S)
    hi = (pos // P) * P
    lo = pos % P
    kaug = np.stack([np.ones(S), np.ones(S), hi, lo]).astype(np.float32)
    qaug = np.zeros((NH, 2, 4, S), np.float32)
    diagb = np.zeros((NH, P, P), np.float32)
    ii = np.arange(P)
    for h in range(NH):
        m8 = 8.0 * 2.0 ** (-(h + 1))
        qaug[h, 0] = np.stack([-m8 * hi, -m8 * lo, m8 * np.ones(S), m8 * np.ones(S)])
        qaug[h, 1] = -qaug[h, 0]
        diagb[h] = -m8 * np.abs(ii[:, None] - ii[None, :])
    return {"c_ident": ident, "c_kaug": kaug, "c_qaug": qaug, "c_diagb": diagb}


_W_KEYS = ("w_in", "w_mem_kv", "w_out", "w_up", "w_down")


def pack_vecs(inputs, L):
    f = lambda k: np.asarray(inputs[k], dtype=np.float32)[:L]
    cols = []
    for k in ("g_mix", "g_mlp", "g_mem"):
        cols.append(f(k).reshape(L, KC, P).transpose(2, 0, 1).reshape(P, L * KC))
    for k in ("g_q_diff", "g_k_diff"):
        cols.append(np.concatenate([f(k).T, f(k).T], axis=0))
    for k in ("g_subln", "g_q_mem", "g_k_mem"):
        cols.append(f(k).T)
    cols.append(f("conv_w").reshape(L, 3, 4, P).transpose(3, 0, 1, 2).reshape(P, L * 12))
    cols.append(f("g_conv_out").reshape(L, 4, P).transpose(2, 0, 1).reshape(P, L * 4))
    return np.ascontiguousarray(np.concatenate(cols, axis=1))
_LAM_KEYS = ("lam_q1", "lam_k1", "lam_q2", "lam_k2")


def make_in_maps(inputs, S, L, ncores):
    shared = {k: np.ascontiguousarray(np.asarray(inputs[k], dtype=np.float32)[:L]) for k in _W_KEYS}
    for k in _LAM_KEYS:
        shared[k] = np.ascontiguousarray(np.asarray(inputs[k], dtype=np.float32)[:L].reshape(1, L * 64))
    shared["vecs"] = pack_vecs(inputs, L)
    shared.update(make_consts(S))
    x = np.asarray(inputs["x"], dtype=np.float32)
    mem = np.asarray(inputs["mem"], dtype=np.float32)
    maps = []
    for c in range(ncores):
        m = dict(shared)
        m["x"] = np.ascontiguousarray(x[c, :S])
        m["mem"] = np.ascontiguousarray(mem[c])
        maps.append(m)
    return maps


SKIP_THRESH = 60.0


def kernel(**inputs):
    x = inputs["x"]
    B, S, _ = x.shape
    L = inputs["w_in"].shape[0]
    nc = build_program(S, L, SKIP_THRESH)
    in_maps = make_in_maps(inputs, S, L, B)
    res = run_bass_kernel_spmd(nc, in_maps, core_ids=list(range(B)))
    return np.stack([np.asarray(r["out"], dtype=np.float32) for r in res.results], axis=0)
```

```python
import math
import numpy as np
import concourse.bass as bass
import concourse.mybir as mybir
from concourse.bass_utils import run_bass_kernel_spmd

F32 = mybir.dt.float32
BF16 = mybir.dt.bfloat16
AF = mybir.ActivationFunctionType
ALU = mybir.AluOpType
AX = mybir.AxisListType

P = 128
D = 2048
DFF = 8192
INW = 5120
NH = 8
T = 512
MEM = 256
KC = D // P
EPS = 1e-6
ENGS = ("pe", "act", "dve", "pool", "sp")


class Ev:
    __slots__ = ("eng", "is_dma", "needs", "sem", "val")

    def __init__(self, eng, is_dma):
        self.eng = eng
        self.is_dma = is_dma
        self.needs = False
        self.sem = None
        self.val = 0


class Tracker:
    def __init__(self):
        self.ops = {e: [] for e in ENGS}
        self.state = {}
        self.last = {e: None for e in ENGS}
        self.pending_dma = {}
        self.dma_cnt = {}

    def _deps(self, eng, reads, writes, is_dma):
        deps = []
        for k in reads:
            st = self.state.get(k)
            if st is not None and st[0] is not None:
                deps.append(st[0])
        for k in writes:
            st = self.state.get(k)
            if st is not None:
                if st[0] is not None:
                    deps.append(st[0])
                deps.extend(st[1].values())
                deps.extend(st[2])
        out = []
        seen = set()
        for d in deps:
            if id(d) in seen:
                continue
            seen.add(id(d))
            if d.is_dma or is_dma or d.eng != eng:
                d.needs = True
                out.append(d)
        return out

    def _update(self, ev, reads, writes):
        for k in reads:
            st = self.state.get(k)
            if st is None:
                st = [None, {}, []]
                self.state[k] = st
            if ev.is_dma:
                st[2].append(ev)
            else:
                st[1][ev.eng] = ev
        for k in writes:
            self.state[k] = [ev, {}, []]

    def emit(self, eng, fn, reads=(), writes=()):
        deps = self._deps(eng, reads, writes, False)
        ev = Ev(eng, False)
        self.ops[eng].append((deps, fn, ev))
        self._update(ev, reads, writes)
        self.last[eng] = ev
        return ev

    def dma(self, eng, pairs, sem, reads=(), writes=(), **kw):
        deps = self._deps(eng, reads, writes, True)
        ev = Ev(eng, True)
        ev.sem = sem
        self.dma_cnt[sem] = self.dma_cnt.get(sem, 0) + 16 * len(pairs)
        ev.val = self.dma_cnt[sem]

        def fn(e, pairs=pairs, kw=kw):
            return [e.dma_start(out=o, in_=i, **kw) for (o, i) in pairs]

        self.ops[eng].append((deps, fn, ev))
        self._update(ev, reads, writes)
        self.pending_dma[sem] = ev
        return ev

    def barrier(self):
        evs = [e for e in self.last.values() if e is not None] + list(self.pending_dma.values())
        for eng in ENGS:
            deps = []
            for d in evs:
                if d.is_dma or d.eng != eng:
                    d.needs = True
                    deps.append(d)
            self.ops[eng].append((deps, None, None))
        self.pending_dma = {}
        self.state = {}

    def finalize(self, engsem):
        for eng in ENGS:
            cnt = 0
            for (_, _, ev) in self.ops[eng]:
                if ev is not None and not ev.is_dma:
                    ev.sem = engsem[eng]
                    if ev.needs:
                        cnt += 1
                        ev.val = cnt

    def replay(self, engname, eng):
        waited = {}
        for (deps, fn, ev) in self.ops[engname]:
            for d in deps:
                key = id(d.sem)
                if waited.get(key, 0) < d.val:
                    eng.wait_ge(d.sem, d.val)
                    waited[key] = d.val
            if fn is None:
                continue
            ins = fn(eng)
            if ev.is_dma:
                for i in ins:
                    i.then_inc(ev.sem, 16)
            elif ev.needs:
                ins.then_inc(ev.sem, 1)


class Arena:
    def __init__(self, nc, lo, hi):
        self.nc = nc
        self.lo = lo
        self.hi = hi
        self.ptr = lo
        self.n = 0
        self.off = {}

    def alloc(self, name, shape, dtype):
        esz = 4 if dtype == F32 else 2
        nbytes = esz
        for s in shape[1:]:
            nbytes *= s
        off = (self.ptr + 31) // 32 * 32
        assert off + nbytes <= self.hi, f"SBUF overflow allocating {name}: {off}+{nbytes} > {self.hi}"
        self.ptr = off + nbytes
        self.n += 1
        self.off[name] = off
        return self.nc.alloc_sbuf_tensor_at(f"{name}_{self.n}", list(shape), dtype, offset=off)

    def mark(self):
        return self.ptr

    def release(self, m):
        self.ptr = m


def lambda_init(l):
    return 0.8 - 0.6 * math.exp(-0.3 * l)


def build_program(S, L, skip_thresh=None, debug=False):
    NT = S // T
    NKT = S // P
    NQB = S // 512
    nc = bass.Bass("TRN2", target_bir_lowering=False)
    tr = Tracker()

    def din(name, shape):
        return nc.dram_tensor(name, list(shape), F32, kind="ExternalInput").ap()

    x_d = din("x", [S, D])
    mem_d = din("mem", [MEM, D])
    w_in_d = din("w_in", [L, D, INW])
    w_kv_d = din("w_mem_kv", [L, D, 1024])
    w_out_d = din("w_out", [L, D, D])
    w_up_d = din("w_up", [L, D, DFF])
    w_down_d = din("w_down", [L, DFF, D])
    vecs_d = din("vecs", [P, 3 * L * KC + 5 * L + L * 12 + L * 4])
    lam_d = [din(n, [1, L * 64]) for n in ("lam_q1", "lam_k1", "lam_q2", "lam_k2")]
    ident_d = din("c_ident", [P, P])
    kaug_d = din("c_kaug", [4, S])
    qaug_d = din("c_qaug", [NH, 2, 4, S])
    diagb_d = din("c_diagb", [NH, P, P])
    out_d = nc.dram_tensor("out", [S, D], F32, kind="ExternalOutput").ap()

    def scr(name, shape, dt):
        kind = "ExternalOutput" if (debug and not name.startswith("wb_")) else "Internal"
        return nc.dram_tensor(name, list(shape), dt, kind=kind).ap()

    wb = {
        "in": scr("wb_in", [L, 10, P, 8192], BF16),
        "kv": scr("wb_kv", [L, 2, P, 8192], BF16),
        "out": scr("wb_out", [L, 4, P, 8192], BF16),
        "up": scr("wb_up", [L, 16, P, 8192], BF16),
        "down": scr("wb_down", [L, 16, P, 8192], BF16),
    }
    xres = scr("xres", [P, KC, S], F32)
    qT_s = scr("qT_s", [NH, P, S], BF16)
    kT_s = scr("kT_s", [NH, P, S], BF16)
    v_s = scr("v_s", [NH, P, NKT, P], BF16)
    zT_s = scr("zT_s", [4, P, S], BF16)
    gbT_s = scr("gbT_s", [4, P, S], BF16)
    mixT_s = scr("mixT_s", [16, P, S], BF16)
    qaug_b = scr("qaug_b", [NH, 2, 4, S], BF16)

    sem_n = [0]
    sem_objs = []

    from contextlib import ExitStack
    stack = ExitStack()

    def mksem(name):
        sem_n[0] += 1
        return stack.enter_context(nc.semaphore(f"{name}{sem_n[0]}"))

    engsem = {e: mksem("e_" + e) for e in ENGS if e != "sp"}
    engsem["sp"] = mksem("e_sp")
    semcache = {}

    def dsem(key):
        if key not in semcache:
            semcache[key] = mksem("d")
        return semcache[key]

    lo = (nc.sbuf_base + 31) // 32 * 32
    ar = Arena(nc, lo, nc.sbuf_top)
    wbuf = [ar.alloc(f"wbuf{i}", [P, 8192], BF16) for i in range(3)]
    ident_f = ar.alloc("ident_f", [P, P], F32)
    ident_b = ar.alloc("ident_b", [P, P], BF16)
    ones_b = ar.alloc("ones_b", [P, P], BF16)
    bd_b = ar.alloc("bd_b", [P, P], BF16)
    diagB = ar.alloc("diagB", [P, NH, P], BF16)
    eps_t = ar.alloc("eps_t", [P, 1], F32)
    NV = 3 * L * KC + 5 * L + L * 12 + L * 4
    vecs_t = ar.alloc("vecs_t", [P, NV], F32)
    _o = [0]

    def vview(n):
        a = _o[0]
        _o[0] += n
        return vecs_t[:, a:a + n]
    gmix_t = vview(L * KC).rearrange("p (l k) -> p l k", k=KC)
    gmlp_t = vview(L * KC).rearrange("p (l k) -> p l k", k=KC)
    gmem_t = vview(L * KC).rearrange("p (l k) -> p l k", k=KC)
    gq_t = vview(L)
    gk_t = vview(L)
    gsub_t = vview(L)
    gqm_t = vview(L)
    gkm_t = vview(L)
    convw_t = vview(L * 12).rearrange("p (l t j) -> p l t j", t=3, j=4)
    gconv_t = vview(L * 4).rearrange("p (l j) -> p l j", j=4)
    lams_t = ar.alloc("lams_t", [P, 2, L], F32)
    neglam_t = ar.alloc("neglam_t", [P, L], F32)
    kmT = ar.alloc("kmT", [P, 4, MEM], BF16)
    vm = ar.alloc("vm", [P, 2, 512], BF16)
    base_mark = ar.mark()
    lamv_t = [ar.alloc(f"lamv{i}", [P, 1, L * 64], F32) for i in range(4)]
    lamp_t = ar.alloc("lamp_t", [P, L, 64], F32)
    ar.release(base_mark)

    pst = [nc.alloc_psum_tensor(f"ps{i}", [P, 1024], F32) for i in range(4)]

    def bank(b):
        return pst[b // 2][:, (b % 2) * 512:(b % 2) * 512 + 512]

    def bkey(b):
        return ("ps", b)

    dbg_n = [0]

    def dbg_dump(name, ap, shape, dt, reads):
        if not debug:
            return
        dbg_n[0] += 1
        t = nc.dram_tensor("dbg_" + name, list(shape), dt, kind="ExternalOutput").ap()
        tr.dma("sp", [(t, ap)], dsem(("dbg", dbg_n[0])), reads=reads)

    def mm(out, lhsT, rhs, start, stop, reads, writes):
        return tr.emit("pe", lambda e, o=out, l=lhsT, r=rhs, s=start, t=stop:
                       e.matmul(o, lhsT=l, rhs=r, start=s, stop=t), reads, writes)

    def act(out, in_, func, reads, writes, scale=1.0, bias=None):
        if bias is None:
            return tr.emit("act", lambda e, o=out, i=in_, f=func, s=scale:
                           e.activation(out=o, in_=i, func=f, scale=s), reads, writes)
        return tr.emit("act", lambda e, o=out, i=in_, f=func, s=scale, b=bias:
                       e.activation(out=o, in_=i, func=f, scale=s, bias=b), reads, writes)

    def stt(eng, out, in0, scalar, in1, op0, op1, reads, writes):
        return tr.emit(eng, lambda e, o=out, a=in0, s=scalar, b=in1, p0=op0, p1=op1:
                       e.scalar_tensor_tensor(out=o, in0=a, scalar=s, in1=b, op0=p0, op1=p1), reads, writes)

    def tt(eng, out, in0, in1, op, reads, writes):
        return tr.emit(eng, lambda e, o=out, a=in0, b=in1, p=op:
                       e.tensor_tensor(out=o, in0=a, in1=b, op=p), reads, writes)

    def ts(eng, out, in0, s1, op0, reads, writes, s2=None, op1=None):
        if op1 is None:
            return tr.emit(eng, lambda e, o=out, a=in0, s=s1, p=op0:
                           e.tensor_scalar(out=o, in0=a, scalar1=s, scalar2=None, op0=p), reads, writes)
        return tr.emit(eng, lambda e, o=out, a=in0, s=s1, p=op0, s2=s2, p1=op1:
                       e.tensor_scalar(out=o, in0=a, scalar1=s, scalar2=s2, op0=p, op1=p1), reads, writes)

    def cp(eng, out, in_, reads, writes):
        if eng == "act":
            return tr.emit("act", lambda e, o=out, i=in_: e.copy(out=o, in_=i), reads, writes)
        return tr.emit(eng, lambda e, o=out, i=in_: e.tensor_copy(out=o, in_=i), reads, writes)

    def recip(out, in_, reads, writes):
        return tr.emit("dve", lambda e, o=out, i=in_: e.reciprocal(out=o, in_=i), reads, writes)

    def memset(eng, ap, val, writes):
        return tr.emit(eng, lambda e, a=ap, v=val: e.memset(a, v), (), writes)

    bank_rr = [0]

    def next_bank(n=6):
        b = bank_rr[0] % n
        bank_rr[0] += 1
        return b

    rr = {}

    def rot(name, n):
        i = rr.get(name, 0)
        rr[name] = i + 1
        return i % n

    wsrc = {"in": w_in_d, "kv": w_kv_d, "out": w_out_d, "up": w_up_d, "down": w_down_d}
    wnblk = {"in": 10, "kv": 2, "out": 4, "up": 16, "down": 16}

    def convert(name, l, blocks=None, pace=()):
        pairs = []
        for b in (range(wnblk[name]) if blocks is None else blocks):
            if name == "down":
                half, cb = b // 8, b % 8
                src = w_down_d[l, half * 4096:(half + 1) * 4096, cb * 256:(cb + 1) * 256].rearrange(
                    "(kc p) n -> p kc n", p=P)
                dst = wb[name][l, b].rearrange("p (kc n) -> p kc n", n=256)
            else:
                src = wsrc[name][l, :, b * 512:(b + 1) * 512].rearrange("(kc p) n -> p kc n", p=P)
                dst = wb[name][l, b].rearrange("p (kc n) -> p kc n", n=512)
            pairs.append((dst, src))
        tr.dma("pool", pairs, dsem(("cv", name, l)), reads=pace, writes=[("wb", name, l)])

    wseq = []
    wphase = []
    for l in range(L + 1):
        n0 = len(wseq)
        if l < L:
            wseq += [("kv", l, 0), ("kv", l, 1)]
        for t in range(NT):
            if l > 0:
                wseq += [("out", l - 1, b) for b in range(4)]
                for half in range(2):
                    wseq += [("up", l - 1, half * 8 + b) for b in range(8)]
                    wseq += [("down", l - 1, half * 8 + b) for b in range(8)]
            if l < L:
                wseq += [("in", l, b) for b in range(10)]
        wphase += [l] * (len(wseq) - n0)
    wpos = [0]
    wld = [0]

    def wnext(expect):
        i = wpos[0]
        assert wseq[i] == expect, (wseq[i], expect)
        while wld[0] < len(wseq) and wld[0] <= i + 2 and wphase[wld[0]] == wphase[i]:
            j = wld[0]
            name, l, b = wseq[j]
            tr.dma("sp", [(wbuf[j % 3][:, :], wb[name][l, b])], dsem(("wl", j % 3)),
                   reads=[("wb", name, l)], writes=[("wbuf", j % 3)])
            wld[0] += 1
        wpos[0] += 1
        return wbuf[i % 3], ("wbuf", i % 3)

    tr.dma("sp", [(ident_f[:, :], ident_d)], dsem("c0"), writes=["ident_f"])
    tr.dma("pool", [(ident_b[:, :], ident_d)], dsem("c1"), writes=["ident_b"])
    tr.dma("pool", [(diagB[:, :, :], diagb_d.rearrange("h i j -> i h j"))], dsem("c2"), writes=["diagB"])
    tr.dma("pool", [(qaug_b, qaug_d)], dsem("c3"), writes=["qaug_b"])
    convert("in", 0)
    convert("kv", 0)
    memset("dve", ones_b[:, :], 1.0, ["ones_b"])
    memset("dve", bd_b[:, :], 0.0, ["bd_b"])
    memset("dve", bd_b[0:64, 0:64], 1.0, ["bd_b"])
    memset("dve", bd_b[64:128, 64:128], 1.0, ["bd_b"])
    memset("dve", eps_t[:, :], EPS, ["eps_t"])
    tr.dma("sp", [(vecs_t[:, :], vecs_d)], dsem("c4"), writes=["small"])
    tr.dma("sp", [(lamv_t[i][:, :, :], lam_d[i].partition_broadcast(P)) for i in range(4)], dsem("c5"),
           writes=["lamv"])
    for j in range(2):
        tt("dve", lamp_t[:, :, :], lamv_t[2 * j][:, 0, :].rearrange("p (l d) -> p l d", d=64),
           lamv_t[2 * j + 1][:, 0, :].rearrange("p (l d) -> p l d", d=64), ALU.mult, ["lamv"], ["lamp"])
        tr.emit("dve", lambda e, o=lams_t[:, j, :], i=lamp_t[:, :, :]:
                e.tensor_reduce(out=o, in_=i, axis=AX.X, op=ALU.add), ["lamp"], ["lams"])
    act(lams_t[:, :, :], lams_t[:, :, :], AF.Exp, ["lams"], ["lams"])
    for l in range(L):
        stt("dve", neglam_t[:, l:l + 1], lams_t[:, 1, l:l + 1], -lambda_init(l), lams_t[:, 0, l:l + 1],
            ALU.add, ALU.subtract, ["lams"], ["neglam"])
        ts("dve", gsub_t[:, l:l + 1], gsub_t[:, l:l + 1], 1.0 - lambda_init(l), ALU.mult, ["small"], ["small"])
    dbg_dump("neglam", neglam_t[:, :], [P, L], F32, ["neglam"])
    dbg_dump("lams", lams_t[:, :, :], [P, 2, L], F32, ["lams"])
    dbg_dump("lamp", lamp_t[:, :, :], [P, L, 64], F32, ["lamp"])
    dbg_dump("vecs", vecs_t[:, :], [P, NV], F32, ["small"])
    tr.barrier()

    def rms_stats_to_rs(psb, pkey, n, lnv, rsv, lkey, width):
        act(lnv, psb, AF.Ln, [pkey], [lkey + "_ln"], scale=1.0 / n, bias=eps_t[:, 0:1])
        act(rsv, lnv, AF.Exp, [lkey + "_ln"], [lkey], scale=-0.5)

    def phase_kv(l):
        m0 = ar.mark()
        memtok = ar.alloc("memtok", [P, 2, D], F32)
        memT = ar.alloc("memT", [P, KC, MEM], F32)
        memsq = ar.alloc("memsq", [P, KC, MEM], BF16)
        memn = ar.alloc("memn", [P, KC, MEM], BF16)
        lnv = ar.alloc("kv_ln", [P, MEM], F32)
        rsv = ar.alloc("kv_rs", [P, MEM], F32)
        sqk = ar.alloc("kv_sqk", [P, MEM], BF16)
        tr.dma("sp", [(memtok[:, :, :], mem_d.rearrange("(mt p) d -> p mt d", p=P))], dsem("memtok"),
               writes=["memtok"])
        for kc in range(KC):
            b = next_bank()
            for mt in range(2):
                tr.emit("pe", lambda e, o=bank(b)[:, mt * P:(mt + 1) * P], i=memtok[:, mt, kc * P:(kc + 1) * P]:
                        e.transpose(o, i, ident_f[:, :]), ["memtok", "ident_f"], [bkey(b)])
            cp("act" if kc % 2 else "dve", memT[:, kc, :], bank(b)[:, 0:MEM], [bkey(b)], [("memT", kc)])
            act(memsq[:, kc, :], memT[:, kc, :], AF.Square, [("memT", kc)], [("memsq", kc)])
        b = next_bank()
        for kc in range(KC):
            mm(bank(b)[:, 0:MEM], ones_b[:, :], memsq[:, kc, :], kc == 0, kc == KC - 1,
               [("memsq", kc), "ones_b"], [bkey(b)])
        rms_stats_to_rs(bank(b)[:, 0:MEM], bkey(b), D, lnv[:, :], rsv[:, :], "kvrs", MEM)
        for kc in range(KC):
            stt("dve", memn[:, kc, :], memT[:, kc, :], gmem_t[:, l, kc:kc + 1], rsv[:, :], ALU.mult, ALU.mult,
                [("memT", kc), "kvrs"], [("memn", kc)])
        w, wk = wnext(("kv", l, 0))
        w3 = w[:, :].rearrange("p (kc n) -> p kc n", n=512)
        for h in range(4):
            b = next_bank()
            for kc in range(KC):
                mm(bank(b)[:, 0:MEM], w3[:, kc, h * P:(h + 1) * P], memn[:, kc, :], kc == 0, kc == KC - 1,
                   [wk, ("memn", kc)], [bkey(b)])
            act(sqk[:, :], bank(b)[:, 0:MEM], AF.Square, [bkey(b)], ["sqk"])
            b2 = next_bank()
            mm(bank(b2)[:, 0:MEM], ones_b[:, :], sqk[:, :], True, True, ["sqk", "ones_b"], [bkey(b2)])
            rms_stats_to_rs(bank(b2)[:, 0:MEM], bkey(b2), P, lnv[:, :], rsv[:, :], "kvrs2", MEM)
            stt("dve", kmT[:, h, :], bank(b)[:, 0:MEM], gkm_t[:, l:l + 1], rsv[:, :], ALU.mult, ALU.mult,
                [bkey(b), "kvrs2"], ["kmT"])
        w, wk = wnext(("kv", l, 1))
        w3 = w[:, :].rearrange("p (kc n) -> p kc n", n=512)
        for mt in range(2):
            b = next_bank()
            for kc in range(KC):
                mm(bank(b), memn[:, kc, mt * P:(mt + 1) * P], w3[:, kc, :], kc == 0, kc == KC - 1,
                   [wk, ("memn", kc)], [bkey(b)])
            cp("dve", vm[:, mt, :], bank(b), [bkey(b)], ["vm"])
        tr.barrier()
        ar.release(m0)

    def phase_p(l):
        m0 = ar.mark()
        xT = ar.alloc("xT", [P, KC, T], F32)
        hT = ar.alloc("hT", [P, KC, T], BF16)
        mixT = ar.alloc("mixT", [P, KC, T], BF16)
        aT = ar.alloc("aT", [P, 32, T], BF16)
        gcb = ar.alloc("gcb", [P, 4, T], F32)
        sq = [ar.alloc(f"sq{i}", [P, T], BF16) for i in range(4)]
        lnv = [ar.alloc(f"lnv{i}", [P, T], F32) for i in range(2)]
        rsv = [ar.alloc(f"rsv{i}", [P, T], F32) for i in range(2)]
        rl = [ar.alloc(f"rl{i}", [P, T], F32) for i in range(2)]
        qst = [ar.alloc(f"qst{i}", [P, T], BF16) for i in range(2)]
        vst = [ar.alloc(f"vst{i}", [P, T], BF16) for i in range(2)]
        gst = [ar.alloc(f"gst{i}", [P, T], BF16) for i in range(2)]
        most = [ar.alloc(f"most{i}", [P, T], BF16) for i in range(2)]
        qmn = [ar.alloc(f"qmn{i}", [P, T], BF16) for i in range(2)]
        pm = [ar.alloc(f"pm{i}", [P, 2 * T], BF16) for i in range(2)]
        rinv = [ar.alloc(f"rinv{i}", [P, T], F32) for i in range(2)]
        zc = ar.alloc("zc", [P, 4, T + 2], BF16)
        gbc = ar.alloc("gbc", [P, 4, T], BF16)
        cy = ar.alloc("cy", [P, T], F32)
        xin = nc.alloc_sbuf_tensor_at(f"xin_{l}", [P, 4, D], F32, offset=ar.off["aT"])
        xout = xin

        SB = 6

        sq_pending = []

        def sq_accum(c):
            while sq_pending:
                sq_pending.pop(0)()
            i = rot("sq", 4)
            act(sq[i][:, :], xT[:, c, :], AF.Square, [("xT", c)], [("sq", i)])
            sq_pending.append(lambda c=c, i=i: mm(bank(SB), ones_b[:, :], sq[i][:, :], c == 0, c == KC - 1,
                                                  [("sq", i), "ones_b"], [bkey(SB)]))

        def norm_finish(g_t, tagl):
            while sq_pending:
                sq_pending.pop(0)()
            i = rot("lnv", 2)
            act(lnv[i][:, :], bank(SB), AF.Ln, [bkey(SB)], [("lnv", i)], scale=1.0 / D, bias=eps_t[:, 0:1])
            act(rsv[i][:, :], lnv[i][:, :], AF.Exp, [("lnv", i)], [("rsv", i)], scale=-0.5)
            for kc in range(KC):
                stt("dve", hT[:, kc, :], xT[:, kc, :], g_t[:, tagl, kc:kc + 1],
                    rsv[i][:, :], ALU.mult, ALU.mult, [("xT", kc), ("rsv", i)], [("hT", kc)])

        def load_x(tt_):
            s1 = tt_ * T
            if l == 0:
                tr.dma("sp", [(xin[:, :, :], x_d[s1:s1 + T, :].rearrange("(j p) d -> p j d", p=P))], dsem("xin"),
                       writes=["xin"])
            else:
                tr.dma("sp", [(xT[:, :, :], xres[:, :, s1:s1 + T])], dsem("xT"),
                       writes=[("xT", kc) for kc in range(KC)])

        def load_mix(tt_):
            s1 = tt_ * T
            tr.dma("sp", [(mixT[:, 0:8, :], mixT_s[0:8, :, s1:s1 + T].rearrange("c p s -> p c s")),
                          (mixT[:, 12:16, :], mixT_s[12:16, :, s1:s1 + T].rearrange("c p s -> p c s"))],
                   dsem("mixTl"), writes=[("mixT", kc) for kc in range(8)] + [("mixT", kc) for kc in range(12, 16)])
            a = max(s1 - 1, 0)
            b_ = min(s1 + T + 1, S)
            if s1 == 0:
                memset("dve", zc[:, :, 0:1], 0.0, ["zc"])
            if s1 + T == S:
                memset("dve", zc[:, :, T + 1:T + 2], 0.0, ["zc"])
            tr.dma("sp", [(zc[:, :, a - (s1 - 1):b_ - (s1 - 1)], zT_s[:, :, a:b_].rearrange("j p s -> p j s")),
                          (gbc[:, :, :], gbT_s[:, :, s1:s1 + T].rearrange("j p s -> p j s"))],
                   dsem("zcl"), writes=["zc", "gbc"])

        def prep_steps():
            lp = l - 1
            steps = []
            for j in range(4):
                def conv_step(j=j):
                    ts("dve", cy[:, :], zc[:, j, 1:T + 1], convw_t[:, lp, 1, j:j + 1], ALU.mult, ["zc"], ["cy"])
                    stt("dve", cy[:, :], zc[:, j, 0:T], convw_t[:, lp, 0, j:j + 1], cy[:, :], ALU.mult, ALU.add,
                        ["zc", "cy"], ["cy"])
                    stt("dve", cy[:, :], zc[:, j, 2:T + 2], convw_t[:, lp, 2, j:j + 1], cy[:, :], ALU.mult, ALU.add,
                        ["zc", "cy"], ["cy"])
                    tt("dve", mixT[:, 8 + j, :], cy[:, :], gbc[:, j, :], ALU.mult, ["cy", "gbc"], [("mixT", 8 + j)])
                steps.append(conv_step)
            sqi = {}

            def norm_a(kc):
                i = rot("sq", 4)
                sqi[kc] = i
                act(sq[i][:, :], mixT[:, kc, :], AF.Square, [("mixT", kc)], [("sq", i)])

            def norm_b(kc):
                i = sqi[kc]
                b = next_bank()
                mm(bank(b), ones_b[:, :], sq[i][:, :], True, True, [("sq", i), "ones_b"], [bkey(b)])
                k = rot("lnv", 2)
                act(lnv[k][:, :], bank(b), AF.Ln, [bkey(b)], [("lnv", k)], scale=1.0 / P, bias=eps_t[:, 0:1])
                act(rsv[k][:, :], lnv[k][:, :], AF.Exp, [("lnv", k)], [("rsv", k)], scale=-0.5)
                g = gsub_t[:, lp:lp + 1] if kc < 8 else gconv_t[:, lp, kc - 8:kc - 7]
                stt("dve", mixT[:, kc, :], mixT[:, kc, :], g, rsv[k][:, :], ALU.mult, ALU.mult,
                    [("mixT", kc), ("rsv", k)], [("mixT", kc)])

            conv = steps
            steps = [lambda: (conv[0](), conv[1]()), lambda: (conv[2](), conv[3]()), lambda: norm_a(0)]
            for kc in range(11):
                steps.append(lambda kc=kc: (norm_b(kc), norm_a(kc + 1)))
            steps.append(lambda: norm_b(11))
            return steps

        for t in range(NT):
            s0 = t * T
            if t == 0:
                load_x(0)
                if l > 0:
                    load_mix(0)
                    for st_ in prep_steps():
                        st_()
            if l == 0:
                for kc in range(KC):
                    b = next_bank()
                    for j in range(4):
                        tr.emit("pe", lambda e, o=bank(b)[:, j * P:(j + 1) * P], i=xin[:, j, kc * P:(kc + 1) * P]:
                                e.transpose(o, i, ident_f[:, :]), ["xin", "ident_f"], [bkey(b)])
                    cp("act" if kc % 2 else "dve", xT[:, kc, :], bank(b), [bkey(b)], [("xT", kc)])
                    sq_accum(kc)
                if t + 1 < NT:
                    load_x(t + 1)
            else:
                lp = l - 1
                for g in range(4):
                    w, wk = wnext(("out", lp, g))
                    w3 = w[:, :].rearrange("p (kc n) -> p kc n", n=512)
                    for dc in range(4):
                        b = next_bank()
                        for kc in range(KC):
                            mm(bank(b), w3[:, kc, dc * P:(dc + 1) * P], mixT[:, kc, :], kc == 0, kc == KC - 1,
                               [wk, ("mixT", kc)], [bkey(b)])
                        c = 4 * g + dc
                        tt("dve", xT[:, c, :], xT[:, c, :], bank(b), ALU.add, [("xT", c), bkey(b)], [("xT", c)])
                        sq_accum(c)
                norm_finish(gmlp_t, lp)
                pending = []
                if t + 1 < NT:
                    load_mix(t + 1)
                    pending = prep_steps()
                for half in range(2):
                    for g in range(8):
                        if pending:
                            pending.pop(0)()
                        w, wk = wnext(("up", lp, half * 8 + g))
                        w3 = w[:, :].rearrange("p (kc n) -> p kc n", n=512)
                        for hc in range(4):
                            b = next_bank()
                            for kc in range(KC):
                                mm(bank(b), w3[:, kc, hc * P:(hc + 1) * P], hT[:, kc, :], kc == 0, kc == KC - 1,
                                   [wk, ("hT", kc)], [bkey(b)])
                            i = rot("rl", 2)
                            a = g * 4 + hc
                            act(rl[i][:, :], bank(b), AF.Relu, [bkey(b)], [("rl", i)])
                            tt("dve", aT[:, a, :], rl[i][:, :], rl[i][:, :], ALU.mult,
                               [("rl", i)], [("aT", a)])
                    for g in range(8):
                        w, wk = wnext(("down", lp, half * 8 + g))
                        w3 = w[:, :].rearrange("p (kc n) -> p kc n", n=256)
                        for dc in range(2):
                            b = next_bank()
                            for a in range(32):
                                mm(bank(b), w3[:, a, dc * P:(dc + 1) * P], aT[:, a, :], a == 0, a == 31,
                                   [wk, ("aT", a)], [bkey(b)])
                            c = 2 * g + dc
                            tt("dve", xT[:, c, :], xT[:, c, :], bank(b), ALU.add, [("xT", c), bkey(b)], [("xT", c)])
                            if half == 1 and l < L:
                                sq_accum(c)
            if l == L:
                for j in range(4):
                    for c4 in range(4):
                        b = next_bank()
                        for k in range(4):
                            kc = c4 * 4 + k
                            tr.emit("pe", lambda e, o=bank(b)[:, k * P:(k + 1) * P], i=xT[:, kc, j * P:(j + 1) * P]:
                                    e.transpose(o, i, ident_f[:, :]), [("xT", kc), "ident_f"], [bkey(b)])
                        cp("act" if c4 % 2 else "dve", xout[:, j, c4 * 512:(c4 + 1) * 512], bank(b), [bkey(b)],
                           [("aT", a_) for a_ in range(32)])
                tr.dma("sp", [(out_d[s0:s0 + T, :].rearrange("(j p) d -> p j d", p=P), xout[:, :, :])],
                       dsem("xout"), reads=[("aT", a_) for a_ in range(32)])
                if t + 1 < NT:
                    load_x(t + 1)
                continue
            if l > 0:
                tr.dma("sp", [(xres[:, :, s0:s0 + T], xT[:, :, :])], dsem("xres_st"),
                       reads=[("xT", kc) for kc in range(KC)])
            else:
                tr.dma("sp", [(xres[:, :, s0:s0 + T], xT[:, :, :])], dsem("xres_st"),
                       reads=[("xT", kc) for kc in range(KC)])
            norm_finish(gmix_t, l)
            if l > 0 and t + 1 < NT:
                load_x(t + 1)
            pend = None

            def finish_qk(item):
                b, which, h = item
                i = rot("sq", 4)
                act(sq[i][:, :], bank(b), AF.Square, [bkey(b)], [("sq", i)])
                b2 = next_bank()
                mm(bank(b2), bd_b[:, :], sq[i][:, :], True, True, [("sq", i), "bd_b"], [bkey(b2)])
                k = rot("lnv", 2)
                act(lnv[k][:, :], bank(b2), AF.Ln, [bkey(b2)], [("lnv", k)], scale=1.0 / 64, bias=eps_t[:, 0:1])
                act(rsv[k][:, :], lnv[k][:, :], AF.Exp, [("lnv", k)], [("rsv", k)], scale=-0.5)
                s = rot("qst", 2)
                g = gq_t if which == 0 else gk_t
                stt("dve", qst[s][:, :], bank(b), g[:, l:l + 1], rsv[k][:, :], ALU.mult, ALU.mult,
                    [bkey(b), ("rsv", k)], [("qst", s)])
                dst = qT_s if which == 0 else kT_s
                tr.dma("sp", [(dst[h, :, s0:s0 + T], qst[s][:, :])], dsem(("qst", s)), reads=[("qst", s)])

            for blk in range(4):
                w, wk = wnext(("in", l, blk))
                w3 = w[:, :].rearrange("p (kc n) -> p kc n", n=512)
                for oc in range(4):
                    b = next_bank()
                    for kc in range(KC):
                        mm(bank(b), w3[:, kc, oc * P:(oc + 1) * P], hT[:, kc, :], kc == 0, kc == KC - 1,
                           [wk, ("hT", kc)], [bkey(b)])
                    if pend is not None:
                        finish_qk(pend)
                    pend = (b, blk // 2, (blk % 2) * 4 + oc)
            for blk in range(4, 6):
                w, wk = wnext(("in", l, blk))
                w3 = w[:, :].rearrange("p (kc n) -> p kc n", n=512)
                for j in range(4):
                    b = next_bank()
                    for kc in range(KC):
                        mm(bank(b), hT[:, kc, j * P:(j + 1) * P], w3[:, kc, :], kc == 0, kc == KC - 1,
                           [wk, ("hT", kc)], [bkey(b)])
                    if pend is not None:
                        finish_qk(pend)
                        pend = None
                    s = rot("vst", 2)
                    cp("act", vst[s][:, :], bank(b), [bkey(b)], [("vst", s)])
                    hb = (blk - 4) * 4
                    tr.dma("sp", [(v_s[hb:hb + 4, :, t * 4 + j, :].rearrange("h p d -> p h d"),
                                   vst[s][:, :].rearrange("p (h d) -> p h d", d=P))],
                           dsem(("vst", s)), reads=[("vst", s)])
            for blk in range(6, 9):
                w, wk = wnext(("in", l, blk))
                w3 = w[:, :].rearrange("p (kc n) -> p kc n", n=512)
                for oc in range(4):
                    b = next_bank()
                    for kc in range(KC):
                        mm(bank(b), w3[:, kc, oc * P:(oc + 1) * P], hT[:, kc, :], kc == 0, kc == KC - 1,
                           [wk, ("hT", kc)], [bkey(b)])
                    if blk == 6:
                        s = rot("gst", 2)
                        cp("act", gst[s][:, :], bank(b), [bkey(b)], [("gst", s)])
                        tr.dma("sp", [(gbT_s[oc, :, s0:s0 + T], gst[s][:, :])], dsem(("gst", s)), reads=[("gst", s)])
                    elif blk == 7:
                        cp("act", gcb[:, oc, :], bank(b), [bkey(b)], [("gcb", oc)])
                    else:
                        s = rot("gst", 2)
                        tt("dve", gst[s][:, :], bank(b), gcb[:, oc, :], ALU.mult, [bkey(b), ("gcb", oc)],
                           [("gst", s)])
                        tr.dma("sp", [(zT_s[oc, :, s0:s0 + T], gst[s][:, :])], dsem(("gst", s)), reads=[("gst", s)])
            w, wk = wnext(("in", l, 9))
            w3 = w[:, :].rearrange("p (kc n) -> p kc n", n=512)
            st8 = [dict() for _ in range(4)]

            def stA(h):
                b = next_bank()
                for kc in range(KC):
                    mm(bank(b), w3[:, kc, h * P:(h + 1) * P], hT[:, kc, :], kc == 0, kc == KC - 1,
                       [wk, ("hT", kc)], [bkey(b)])
                i = rot("sq", 4)
                act(sq[i][:, :], bank(b), AF.Square, [bkey(b)], [("sq", i)])
                st8[h].update(b=b, i=i)

            def stB(h):
                b, i = st8[h]["b"], st8[h]["i"]
                b2 = next_bank()
                mm(bank(b2), ones_b[:, :], sq[i][:, :], True, True, [("sq", i), "ones_b"], [bkey(b2)])
                k = rot("lnv", 2)
                act(lnv[k][:, :], bank(b2), AF.Ln, [bkey(b2)], [("lnv", k)], scale=1.0 / P, bias=eps_t[:, 0:1])
                act(rsv[k][:, :], lnv[k][:, :], AF.Exp, [("lnv", k)], [("rsv", k)], scale=-0.5)
                qi = rot("qmn", 2)
                stt("dve", qmn[qi][:, :], bank(b), gqm_t[:, l:l + 1], rsv[k][:, :], ALU.mult, ALU.mult,
                    [bkey(b), ("rsv", k)], [("qmn", qi)])
                st8[h].update(qi=qi)

            def stC(h):
                qi = st8[h]["qi"]
                for mt in range(2):
                    mm(bank(6 + mt), kmT[:, h, mt * P:(mt + 1) * P], qmn[qi][:, :], True, True,
                       [("qmn", qi), "kmT"], [bkey(6 + mt)])
                pi = rot("pm", 2)
                act(pm[pi][:, :], pst[3][:, :], AF.Exp, [bkey(6), bkey(7)], [("pm", pi)], scale=float(P) ** -0.5)
                st8[h].update(pi=pi)

            def stD(h):
                pi = st8[h]["pi"]
                bo = next_bank()
                for mt in range(2):
                    mm(bank(bo), vm[:, mt, h * P:(h + 1) * P], pm[pi][:, mt * T:(mt + 1) * T], mt == 0, mt == 1,
                       [("pm", pi), "vm"], [bkey(bo)])
                bs = next_bank()
                for mt in range(2):
                    mm(bank(bs), ones_b[:, :], pm[pi][:, mt * T:(mt + 1) * T], mt == 0, mt == 1,
                       [("pm", pi), "ones_b"], [bkey(bs)])
                ri = rot("rinv", 2)
                recip(rinv[ri][:, :], bank(bs), [bkey(bs)], [("rinv", ri)])
                s_ = rot("most", 2)
                tt("dve", most[s_][:, :], bank(bo), rinv[ri][:, :], ALU.mult, [bkey(bo), ("rinv", ri)],
                   [("most", s_)])
                tr.dma("sp", [(mixT_s[12 + h, :, s0:s0 + T], most[s_][:, :])], dsem(("most", s_)),
                       reads=[("most", s_)])

            stA(0); stA(1); stB(0); stA(2); stB(1); stC(0); stA(3); stB(2); stC(1); stD(0)
            stB(3); stC(2); stD(1); stC(3); stD(2); stD(3)
        tr.barrier()
        ar.release(m0)

    def skip_tile(h, qb, kt):
        if skip_thresh is None:
            return False
        q0, q1 = qb * 512, qb * 512 + 511
        k0, k1 = kt * P, kt * P + P - 1
        if k1 < q0:
            md = q0 - k1
        elif k0 > q1:
            md = k0 - q1
        else:
            return False
        return (2.0 ** -(h + 1)) * md >= skip_thresh

    def phase_m(l):
        m0 = ar.mark()
        sets = []
        set_off = []
        for s in range(2):
            set_off.append((ar.mark() + 31) // 32 * 32)
            d = {}
            d["KT"] = [ar.alloc(f"KT{s}{c}", [68, S], BF16) for c in range(2)]
            d["QA"] = [ar.alloc(f"QA{s}{c}", [68, S], BF16) for c in range(2)]
            d["QB"] = [ar.alloc(f"QB{s}{c}", [68, S], BF16) for c in range(2)]
            d["V"] = ar.alloc(f"V{s}", [P, NKT, P], BF16)
            sets.append(d)
        PT = [ar.alloc(f"PT{i}", [P, 1024], BF16) for i in range(3)]
        r0 = ar.alloc("r0", [P, 512], F32)
        r1 = ar.alloc("r1", [P, 512], F32)
        o_f = ar.alloc("o_f", [P, 512], F32)
        t_f = ar.alloc("t_f", [P, 512], F32)
        lnv = ar.alloc("m_lnv", [P, 512], F32)
        rsv = ar.alloc("m_rsv", [P, 512], F32)
        sqo = ar.alloc("sqo", [P, 512], BF16)
        ost = [ar.alloc(f"ost{i}", [P, 512], BF16) for i in range(2)]

        def load_head(h, s):
            d = sets[s]
            pairs = []
            keys = []
            for c in range(2):
                pairs.append((d["KT"][c][0:64, :], kT_s[h, c * 64:(c + 1) * 64, :]))
                pairs.append((d["QA"][c][0:64, :], qT_s[h, c * 64:(c + 1) * 64, :]))
                pairs.append((d["QB"][c][0:64, :], qT_s[h, c * 64:(c + 1) * 64, :]))
                pairs.append((d["QA"][c][64:68, :], qaug_b[h, 0]))
                pairs.append((d["QB"][c][64:68, :], qaug_b[h, 1]))
            pairs.append((d["V"][:, :, :], v_s[h]))
            tr.dma("sp", pairs, dsem(("head", s)), writes=[("head", s)])

        tr.dma("pool", [(sets[0]["KT"][c][64:68, :], kaug_d) for c in range(2)], dsem("kaug"),
               writes=[("head", 0)])
        load_head(0, 0)

        tr.dma("pool", [(sets[1]["KT"][c][64:68, :], kaug_d) for c in range(2)], dsem("kaug"),
               writes=[("head", 1)])

        for pc in conv_pieces(l):
            convert(pc[0], pc[1], pc[2])

        PV = [4, 5]
        LS = [6, 7]
        for h in range(NH):
            s = h % 2
            d = sets[s]
            hk = ("head", s)
            if h + 1 < NH:
                load_head(h + 1, 1 - s)
            for qb in range(NQB):
                q0 = qb * 512
                kts = [kt for kt in range(NKT) if not skip_tile(h, qb, kt)]
                n = len(kts)

                def qk(i):
                    kt = kts[i]
                    k0 = kt * P
                    pr = i % 2
                    for c in range(2):
                        b = 2 * pr + c
                        KTc, QAc, QBc = d["KT"][c], d["QA"][c], d["QB"][c]
                        if k0 + P <= q0:
                            mm(bank(b), KTc[0:68, k0:k0 + P], QAc[0:68, q0:q0 + 512], True, True, [hk], [bkey(b)])
                        elif k0 >= q0 + 512:
                            mm(bank(b), KTc[0:68, k0:k0 + P], QBc[0:68, q0:q0 + 512], True, True, [hk], [bkey(b)])
                        else:
                            js = (k0 - q0) // P
                            if js > 0:
                                mm(bank(b)[:, 0:js * P], KTc[0:68, k0:k0 + P], QBc[0:68, q0:q0 + js * P],
                                   True, True, [hk], [bkey(b)])
                            mm(bank(b)[:, js * P:(js + 1) * P], KTc[0:64, k0:k0 + P],
                               QAc[0:64, q0 + js * P:q0 + (js + 1) * P], True, False, [hk], [bkey(b)])
                            mm(bank(b)[:, js * P:(js + 1) * P], ident_b[:, :], diagB[:, h, :], False, True,
                               ["ident_b", "diagB"], [bkey(b)])
                            if js < 3:
                                mm(bank(b)[:, (js + 1) * P:512], KTc[0:68, k0:k0 + P],
                                   QAc[0:68, q0 + (js + 1) * P:q0 + 512], True, True, [hk], [bkey(b)])

                def expav(i):
                    kt = kts[i]
                    pr = i % 2
                    pi = i % 3
                    act(PT[pi][:, :], pst[pr][:, :], AF.Exp, [bkey(2 * pr), bkey(2 * pr + 1)], [("PT", pi)],
                        scale=0.125)
                    first, last = (i == 0), (i == n - 1)
                    if h == 0 and qb == 0 and l == 0:
                        dbg_dump(f"PT{i}", PT[pi][:, :], [P, 1024], BF16, [("PT", pi)])
                    for c in range(2):
                        mm(bank(PV[c]), d["V"][:, kt, :], PT[pi][:, c * 512:(c + 1) * 512], first, last,
                           [hk, ("PT", pi)], [bkey(PV[c])])
                    for c in range(2):
                        mm(bank(LS[c]), ones_b[:, :], PT[pi][:, c * 512:(c + 1) * 512], first, last,
                           ["ones_b", ("PT", pi)], [bkey(LS[c])])

                qk(0)
                for i in range(n):
                    if i + 1 < n:
                        qk(i + 1)
                    expav(i)
                cp("act", t_f[:, :], bank(PV[1]), [bkey(PV[1])], ["t_f"])
                cp("dve", o_f[:, :], bank(PV[0]), [bkey(PV[0])], ["o_f"])
                cp("dve", r0[:, :], bank(LS[0]), [bkey(LS[0])], ["r0"])
                cp("dve", r1[:, :], bank(LS[1]), [bkey(LS[1])], ["r1"])
                recip(r0[:, :], r0[:, :], ["r0"], ["r0"])
                recip(r1[:, :], r1[:, :], ["r1"], ["r1"])
                tt("dve", o_f[:, :], o_f[:, :], r0[:, :], ALU.mult, ["o_f", "r0"], ["o_f"])
                tt("dve", t_f[:, :], t_f[:, :], r1[:, :], ALU.mult, ["t_f", "r1"], ["t_f"])
                oi = rot("ost", 2)
                stt("dve", ost[oi][:, :], t_f[:, :], neglam_t[:, l:l + 1], o_f[:, :], ALU.mult, ALU.add,
                    ["t_f", "o_f"], [("ost", oi)])
                tr.dma("sp", [(mixT_s[h, :, q0:q0 + 512], ost[oi][:, :])], dsem(("ost", oi)), reads=[("ost", oi)])
        tr.barrier()
        ar.release(m0)

    def conv_pieces(l):
        pcs = [("out", l, range(0, 4))]
        for q4 in range(4):
            pcs.append(("up", l, range(q4 * 4, q4 * 4 + 4)))
            pcs.append(("down", l, range(q4 * 4, q4 * 4 + 4)))
        if l + 1 < L:
            pcs.append(("in", l + 1, range(0, 5)))
            pcs.append(("in", l + 1, range(5, 10)))
            pcs.append(("kv", l + 1, range(0, 2)))
        return pcs

    for l in range(L):
        phase_kv(l)
        phase_p(l)
        phase_m(l)
    phase_p(L)
    tr.barrier()
    assert wpos[0] == len(wseq), (wpos[0], len(wseq))

    tr.finalize(engsem)
    with stack:
        with nc.Block() as block:
            @block.tensor
            def _(e):
                tr.replay("pe", e)

            @block.scalar
            def _(e):
                tr.replay("act", e)

            @block.vector
            def _(e):
                tr.replay("dve", e)

            @block.gpsimd
            def _(e):
                tr.replay("pool", e)

            @block.sync
            def _(e):
                tr.replay("sp", e)
    return nc


def make_consts(S):
    ident = np.eye(P, dtype=np.float32)
    pos = np.arange(S)
    hi = (pos // P) * P
    lo = pos % P
    kaug = np.stack([np.ones(S), np.ones(S), hi, lo]).astype(np.float32)
    qaug = np.zeros((NH, 2, 4, S), np.float32)
    diagb = np.zeros((NH, P, P), np.float32)
    ii = np.arange(P)
    for h in range(NH):
        m8 = 8.0 * 2.0 ** (-(h + 1))
        qaug[h, 0] = np.stack([-m8 * hi, -m8 * lo, m8 * np.ones(S), m8 * np.ones(S)])
        qaug[h, 1] = -qaug[h, 0]
        diagb[h] = -m8 * np.abs(ii[:, None] - ii[None, :])
    return {"c_ident": ident, "c_kaug": kaug, "c_qaug": qaug, "c_diagb": diagb}


_W_KEYS = ("w_in", "w_mem_kv", "w_out", "w_up", "w_down")


def pack_vecs(inputs, L):
    f = lambda k: np.asarray(inputs[k], dtype=np.float32)[:L]
    cols = []
    for k in ("g_mix", "g_mlp", "g_mem"):
        cols.append(f(k).reshape(L, KC, P).transpose(2, 0, 1).reshape(P, L * KC))
    for k in ("g_q_diff", "g_k_diff"):
        cols.append(np.concatenate([f(k).T, f(k).T], axis=0))
    for k in ("g_subln", "g_q_mem", "g_k_mem"):
        cols.append(f(k).T)
    cols.append(f("conv_w").reshape(L, 3, 4, P).transpose(3, 0, 1, 2).reshape(P, L * 12))
    cols.append(f("g_conv_out").reshape(L, 4, P).transpose(2, 0, 1).reshape(P, L * 4))
    return np.ascontiguousarray(np.concatenate(cols, axis=1))
_LAM_KEYS = ("lam_q1", "lam_k1", "lam_q2", "lam_k2")


def make_in_maps(inputs, S, L, ncores):
    shared = {k: np.ascontiguousarray(np.asarray(inputs[k], dtype=np.float32)[:L]) for k in _W_KEYS}
    for k in _LAM_KEYS:
        shared[k] = np.ascontiguousarray(np.asarray(inputs[k], dtype=np.float32)[:L].reshape(1, L * 64))
    shared["vecs"] = pack_vecs(inputs, L)
    shared.update(make_consts(S))
    x = np.asarray(inputs["x"], dtype=np.float32)
    mem = np.asarray(inputs["mem"], dtype=np.float32)
    maps = []
    for c in range(ncores):
        m = dict(shared)
        m["x"] = np.ascontiguousarray(x[c, :S])
        m["mem"] = np.ascontiguousarray(mem[c])
        maps.append(m)
    return maps


SKIP_THRESH = 60.0


def kernel(**inputs):
    x = inputs["x"]
    B, S, _ = x.shape
    L = inputs["w_in"].shape[0]
    nc = build_program(S, L, SKIP_THRESH)
    in_maps = make_in_maps(inputs, S, L, B)
    res = run_bass_kernel_spmd(nc, in_maps, core_ids=list(range(B)))
    return np.stack([np.asarray(r["out"], dtype=np.float32) for r in res.results], axis=0)
```

```python
import math
import numpy as np
import concourse.bass as bass
import concourse.mybir as mybir
from concourse.bass_utils import run_bass_kernel_spmd

F32 = mybir.dt.float32
BF16 = mybir.dt.bfloat16
AF = mybir.ActivationFunctionType
ALU = mybir.AluOpType
AX = mybir.AxisListType

P = 128
D = 2048
DFF = 8192
INW = 5120
NH = 8
T = 512
MEM = 256
KC = D // P
EPS = 1e-6
ENGS = ("pe", "act", "dve", "pool", "sp")


class Ev:
    __slots__ = ("eng", "is_dma", "needs", "sem", "val")

    def __init__(self, eng, is_dma):
        self.eng = eng
        self.is_dma = is_dma
        self.needs = False
        self.sem = None
        self.val = 0


class Tracker:
    def __init__(self):
        self.ops = {e: [] for e in ENGS}
        self.state = {}
        self.last = {e: None for e in ENGS}
        self.pending_dma = {}
        self.dma_cnt = {}

    def _deps(self, eng, reads, writes, is_dma):
        deps = []
        for k in reads:
            st = self.state.get(k)
            if st is not None and st[0] is not None:
                deps.append(st[0])
        for k in writes:
            st = self.state.get(k)
            if st is not None:
                if st[0] is not None:
                    deps.append(st[0])
                deps.extend(st[1].values())
                deps.extend(st[2])
        out = []
        seen = set()
        for d in deps:
            if id(d) in seen:
                continue
            seen.add(id(d))
            if d.is_dma or is_dma or d.eng != eng:
                d.needs = True
                out.append(d)
        return out

    def _update(self, ev, reads, writes):
        for k in reads:
            st = self.state.get(k)
            if st is None:
                st = [None, {}, []]
                self.state[k] = st
            if ev.is_dma:
                st[2].append(ev)
            else:
                st[1][ev.eng] = ev
        for k in writes:
            self.state[k] = [ev, {}, []]

    def emit(self, eng, fn, reads=(), writes=()):
        deps = self._deps(eng, reads, writes, False)
        ev = Ev(eng, False)
        self.ops[eng].append((deps, fn, ev))
        self._update(ev, reads, writes)
        self.last[eng] = ev
        return ev

    def dma(self, eng, pairs, sem, reads=(), writes=(), **kw):
        deps = self._deps(eng, reads, writes, True)
        ev = Ev(eng, True)
        ev.sem = sem
        self.dma_cnt[sem] = self.dma_cnt.get(sem, 0) + 16 * len(pairs)
        ev.val = self.dma_cnt[sem]

        def fn(e, pairs=pairs, kw=kw):
            return [e.dma_start(out=o, in_=i, **kw) for (o, i) in pairs]

        self.ops[eng].append((deps, fn, ev))
        self._update(ev, reads, writes)
        self.pending_dma[sem] = ev
        return ev

    def barrier(self):
        evs = [e for e in self.last.values() if e is not None] + list(self.pending_dma.values())
        for eng in ENGS:
            deps = []
            for d in evs:
                if d.is_dma or d.eng != eng:
                    d.needs = True
                    deps.append(d)
            self.ops[eng].append((deps, None, None))
        self.pending_dma = {}
        self.state = {}

    def finalize(self, engsem):
        for eng in ENGS:
            cnt = 0
            for (_, _, ev) in self.ops[eng]:
                if ev is not None and not ev.is_dma:
                    ev.sem = engsem[eng]
                    if ev.needs:
                        cnt += 1
                        ev.val = cnt

    def replay(self, engname, eng):
        waited = {}
        for (deps, fn, ev) in self.ops[engname]:
            for d in deps:
                key = id(d.sem)
                if waited.get(key, 0) < d.val:
                    eng.wait_ge(d.sem, d.val)
                    waited[key] = d.val
            if fn is None:
                continue
            ins = fn(eng)
            if ev.is_dma:
                for i in ins:
                    i.then_inc(ev.sem, 16)
            elif ev.needs:
                ins.then_inc(ev.sem, 1)


class Arena:
    def __init__(self, nc, lo, hi):
        self.nc = nc
        self.lo = lo
        self.hi = hi
        self.ptr = lo
        self.n = 0
        self.off = {}

    def alloc(self, name, shape, dtype):
        esz = 4 if dtype == F32 else 2
        nbytes = esz
        for s in shape[1:]:
            nbytes *= s
        off = (self.ptr + 31) // 32 * 32
        assert off + nbytes <= self.hi, f"SBUF overflow allocating {name}: {off}+{nbytes} > {self.hi}"
        self.ptr = off + nbytes
        self.n += 1
        self.off[name] = off
        return self.nc.alloc_sbuf_tensor_at(f"{name}_{self.n}", list(shape), dtype, offset=off)

    def mark(self):
        return self.ptr

    def release(self, m):
        self.ptr = m


def lambda_init(l):
    return 0.8 - 0.6 * math.exp(-0.3 * l)


def build_program(S, L, skip_thresh=None, debug=False):
    NT = S // T
    NKT = S // P
    NQB = S // 512
    nc = bass.Bass("TRN2", target_bir_lowering=False)
    tr = Tracker()

    def din(name, shape):
        return nc.dram_tensor(name, list(shape), F32, kind="ExternalInput").ap()

    x_d = din("x", [S, D])
    mem_d = din("mem", [MEM, D])
    w_in_d = din("w_in", [L, D, INW])
    w_kv_d = din("w_mem_kv", [L, D, 1024])
    w_out_d = din("w_out", [L, D, D])
    w_up_d = din("w_up", [L, D, DFF])
    w_down_d = din("w_down", [L, DFF, D])
    vecs_d = din("vecs", [P, 3 * L * KC + 5 * L + L * 12 + L * 4])
    lam_d = [din(n, [1, L * 64]) for n in ("lam_q1", "lam_k1", "lam_q2", "lam_k2")]
    ident_d = din("c_ident", [P, P])
    kaug_d = din("c_kaug", [4, S])
    qaug_d = din("c_qaug", [NH, 2, 4, S])
    diagb_d = din("c_diagb", [NH, P, P])
    out_d = nc.dram_tensor("out", [S, D], F32, kind="ExternalOutput").ap()

    def scr(name, shape, dt):
        kind = "ExternalOutput" if (debug and not name.startswith("wb_")) else "Internal"
        return nc.dram_tensor(name, list(shape), dt, kind=kind).ap()

    wb = {
        "in": scr("wb_in", [L, 10, P, 8192], BF16),
        "kv": scr("wb_kv", [L, 2, P, 8192], BF16),
        "out": scr("wb_out", [L, 4, P, 8192], BF16),
        "up": scr("wb_up", [L, 16, P, 8192], BF16),
        "down": scr("wb_down", [L, 16, P, 8192], BF16),
    }
    xres = scr("xres", [P, KC, S], F32)
    qT_s = scr("qT_s", [NH, P, S], BF16)
    kT_s = scr("kT_s", [NH, P, S], BF16)
    v_s = scr("v_s", [NH, P, NKT, P], BF16)
    zT_s = scr("zT_s", [4, P, S], BF16)
    gbT_s = scr("gbT_s", [4, P, S], BF16)
    mixT_s = scr("mixT_s", [16, P, S], BF16)
    qaug_b = scr("qaug_b", [NH, 2, 4, S], BF16)

    sem_n = [0]
    sem_objs = []

    from contextlib import ExitStack
    stack = ExitStack()

    def mksem(name):
        sem_n[0] += 1
        return stack.enter_context(nc.semaphore(f"{name}{sem_n[0]}"))

    engsem = {e: mksem("e_" + e) for e in ENGS if e != "sp"}
    engsem["sp"] = mksem("e_sp")
    semcache = {}

    def dsem(key):
        if key not in semcache:
            semcache[key] = mksem("d")
        return semcache[key]

    lo = (nc.sbuf_base + 31) // 32 * 32
    ar = Arena(nc, lo, nc.sbuf_top)
    wbuf = [ar.alloc(f"wbuf{i}", [P, 8192], BF16) for i in range(3)]
    ident_f = ar.alloc("ident_f", [P, P], F32)
    ident_b = ar.alloc("ident_b", [P, P], BF16)
    ones_b = ar.alloc("ones_b", [P, P], BF16)
    bd_b = ar.alloc("bd_b", [P, P], BF16)
    sel_b = [ar.alloc(f"sel_b{c}", [P, P], BF16) for c in range(2)]
    diagB = ar.alloc("diagB", [P, NH, P], BF16)
    eps_t = ar.alloc("eps_t", [P, 1], F32)
    NV = 3 * L * KC + 5 * L + L * 12 + L * 4
    vecs_t = ar.alloc("vecs_t", [P, NV], F32)
    _o = [0]

    def vview(n):
        a = _o[0]
        _o[0] += n
        return vecs_t[:, a:a + n]
    gmix_t = vview(L * KC).rearrange("p (l k) -> p l k", k=KC)
    gmlp_t = vview(L * KC).rearrange("p (l k) -> p l k", k=KC)
    gmem_t = vview(L * KC).rearrange("p (l k) -> p l k", k=KC)
    gq_t = vview(L)
    gk_t = vview(L)
    gsub_t = vview(L)
    gqm_t = vview(L)
    gkm_t = vview(L)
    convw_t = vview(L * 12).rearrange("p (l t j) -> p l t j", t=3, j=4)
    gconv_t = vview(L * 4).rearrange("p (l j) -> p l j", j=4)
    lams_t = ar.alloc("lams_t", [P, 2, L], F32)
    neglam_t = ar.alloc("neglam_t", [P, L], F32)
    kmT = ar.alloc("kmT", [P, 4, MEM], BF16)
    vm = ar.alloc("vm", [P, 2, 512], BF16)
    base_mark = ar.mark()
    lamv_t = [ar.alloc(f"lamv{i}", [P, 1, L * 64], F32) for i in range(4)]
    lamp_t = ar.alloc("lamp_t", [P, L, 64], F32)
    ar.release(base_mark)

    pst = [nc.alloc_psum_tensor(f"ps{i}", [P, 1024], F32) for i in range(4)]

    def bank(b):
        return pst[b // 2][:, (b % 2) * 512:(b % 2) * 512 + 512]

    def bkey(b):
        return ("ps", b)

    dbg_n = [0]

    def dbg_dump(name, ap, shape, dt, reads):
        if not debug:
            return
        dbg_n[0] += 1
        t = nc.dram_tensor("dbg_" + name, list(shape), dt, kind="ExternalOutput").ap()
        tr.dma("sp", [(t, ap)], dsem(("dbg", dbg_n[0])), reads=reads)

    def mm(out, lhsT, rhs, start, stop, reads, writes, tp=None):
        if tp is not None:
            return tr.emit("pe", lambda e, o=out, l=lhsT, r=rhs, s=start, t=stop, tp=tp:
                           e.matmul(o, lhsT=l, rhs=r, start=s, stop=t, tile_position=tp), reads, writes)
        return tr.emit("pe", lambda e, o=out, l=lhsT, r=rhs, s=start, t=stop:
                       e.matmul(o, lhsT=l, rhs=r, start=s, stop=t), reads, writes)

    def act(out, in_, func, reads, writes, scale=1.0, bias=None):
        if bias is None:
            return tr.emit("act", lambda e, o=out, i=in_, f=func, s=scale:
                           e.activation(out=o, in_=i, func=f, scale=s), reads, writes)
        return tr.emit("act", lambda e, o=out, i=in_, f=func, s=scale, b=bias:
                       e.activation(out=o, in_=i, func=f, scale=s, bias=b), reads, writes)

    def stt(eng, out, in0, scalar, in1, op0, op1, reads, writes):
        return tr.emit(eng, lambda e, o=out, a=in0, s=scalar, b=in1, p0=op0, p1=op1:
                       e.scalar_tensor_tensor(out=o, in0=a, scalar=s, in1=b, op0=p0, op1=p1), reads, writes)

    def tt(eng, out, in0, in1, op, reads, writes):
        return tr.emit(eng, lambda e, o=out, a=in0, b=in1, p=op:
                       e.tensor_tensor(out=o, in0=a, in1=b, op=p), reads, writes)

    def ts(eng, out, in0, s1, op0, reads, writes, s2=None, op1=None):
        if op1 is None:
            return tr.emit(eng, lambda e, o=out, a=in0, s=s1, p=op0:
                           e.tensor_scalar(out=o, in0=a, scalar1=s, scalar2=None, op0=p), reads, writes)
        return tr.emit(eng, lambda e, o=out, a=in0, s=s1, p=op0, s2=s2, p1=op1:
                       e.tensor_scalar(out=o, in0=a, scalar1=s, scalar2=s2, op0=p, op1=p1), reads, writes)

    def cp(eng, out, in_, reads, writes):
        if eng == "act":
            return tr.emit("act", lambda e, o=out, i=in_: e.copy(out=o, in_=i), reads, writes)
        return tr.emit(eng, lambda e, o=out, i=in_: e.tensor_copy(out=o, in_=i), reads, writes)

    def recip(out, in_, reads, writes):
        return tr.emit("dve", lambda e, o=out, i=in_: e.reciprocal(out=o, in_=i), reads, writes)

    def memset(eng, ap, val, writes):
        return tr.emit(eng, lambda e, a=ap, v=val: e.memset(a, v), (), writes)

    bank_rr = [0]

    def next_bank(n=6):
        b = bank_rr[0] % n
        bank_rr[0] += 1
        return b

    rr = {}

    def rot(name, n):
        i = rr.get(name, 0)
        rr[name] = i + 1
        return i % n

    wsrc = {"in": w_in_d, "kv": w_kv_d, "out": w_out_d, "up": w_up_d, "down": w_down_d}
    wnblk = {"in": 10, "kv": 2, "out": 4, "up": 16, "down": 16}

    def convert(name, l, blocks=None, pace=()):
        pairs = []
        for b in (range(wnblk[name]) if blocks is None else blocks):
            if name == "down":
                half, cb = b // 8, b % 8
                src = w_down_d[l, half * 4096:(half + 1) * 4096, cb * 256:(cb + 1) * 256].rearrange(
                    "(kc p) n -> p kc n", p=P)
                dst = wb[name][l, b].rearrange("p (kc n) -> p kc n", n=256)
            else:
                src = wsrc[name][l, :, b * 512:(b + 1) * 512].rearrange("(kc p) n -> p kc n", p=P)
                dst = wb[name][l, b].rearrange("p (kc n) -> p kc n", n=512)
            pairs.append((dst, src))
        tr.dma("pool", pairs, dsem(("cv", name, l)), reads=pace, writes=[("wb", name, l)])

    wseq = []
    wphase = []
    for l in range(L + 1):
        n0 = len(wseq)
        if l < L:
            wseq += [("kv", l, 0), ("kv", l, 1)]
        for t in range(NT):
            if l > 0:
                wseq += [("out", l - 1, b) for b in range(4)]
                for half in range(2):
                    wseq += [("up", l - 1, half * 8 + b) for b in range(8)]
                    wseq += [("down", l - 1, half * 8 + b) for b in range(8)]
            if l < L:
                wseq += [("in", l, b) for b in range(10)]
        wphase += [l] * (len(wseq) - n0)
    wpos = [0]
    wld = [0]

    def wnext(expect):
        i = wpos[0]
        assert wseq[i] == expect, (wseq[i], expect)
        while wld[0] < len(wseq) and wld[0] <= i + 2 and wphase[wld[0]] == wphase[i]:
            j = wld[0]
            name, l, b = wseq[j]
            tr.dma("sp", [(wbuf[j % 3][:, :], wb[name][l, b])], dsem(("wl", j % 3)),
                   reads=[("wb", name, l)], writes=[("wbuf", j % 3)])
            wld[0] += 1
        wpos[0] += 1
        return wbuf[i % 3], ("wbuf", i % 3)

    tr.dma("sp", [(ident_f[:, :], ident_d)], dsem("c0"), writes=["ident_f"])
    tr.dma("pool", [(ident_b[:, :], ident_d)], dsem("c1"), writes=["ident_b"])
    tr.dma("pool", [(diagB[:, :, :], diagb_d.rearrange("h i j -> i h j"))], dsem("c2"), writes=["diagB"])
    tr.dma("pool", [(qaug_b, qaug_d)], dsem("c3"), writes=["qaug_b"])
    convert("in", 0)
    convert("kv", 0)
    memset("dve", ones_b[:, :], 1.0, ["ones_b"])
    memset("dve", bd_b[:, :], 0.0, ["bd_b"])
    memset("dve", bd_b[0:64, 0:64], 1.0, ["bd_b"])
    memset("dve", bd_b[64:128, 64:128], 1.0, ["bd_b"])
    memset("dve", eps_t[:, :], EPS, ["eps_t"])
    for c in range(2):
        memset("dve", sel_b[c][:, :], 0.0, ["sel_b"])
        memset("dve", sel_b[c][64 * c:64 * c + 1, :], 1.0, ["sel_b"])
    tr.dma("sp", [(vecs_t[:, :], vecs_d)], dsem("c4"), writes=["small"])
    tr.dma("sp", [(lamv_t[i][:, :, :], lam_d[i].partition_broadcast(P)) for i in range(4)], dsem("c5"),
           writes=["lamv"])
    for j in range(2):
        tt("dve", lamp_t[:, :, :], lamv_t[2 * j][:, 0, :].rearrange("p (l d) -> p l d", d=64),
           lamv_t[2 * j + 1][:, 0, :].rearrange("p (l d) -> p l d", d=64), ALU.mult, ["lamv"], ["lamp"])
        tr.emit("dve", lambda e, o=lams_t[:, j, :], i=lamp_t[:, :, :]:
                e.tensor_reduce(out=o, in_=i, axis=AX.X, op=ALU.add), ["lamp"], ["lams"])
    act(lams_t[:, :, :], lams_t[:, :, :], AF.Exp, ["lams"], ["lams"])
    for l in range(L):
        stt("dve", neglam_t[:, l:l + 1], lams_t[:, 1, l:l + 1], -lambda_init(l), lams_t[:, 0, l:l + 1],
            ALU.add, ALU.subtract, ["lams"], ["neglam"])
        ts("dve", gsub_t[:, l:l + 1], gsub_t[:, l:l + 1], 1.0 - lambda_init(l), ALU.mult, ["small"], ["small"])
    dbg_dump("neglam", neglam_t[:, :], [P, L], F32, ["neglam"])
    dbg_dump("lams", lams_t[:, :, :], [P, 2, L], F32, ["lams"])
    dbg_dump("lamp", lamp_t[:, :, :], [P, L, 64], F32, ["lamp"])
    dbg_dump("vecs", vecs_t[:, :], [P, NV], F32, ["small"])
    tr.barrier()

    def rms_stats_to_rs(psb, pkey, n, lnv, rsv, lkey, width):
        act(lnv, psb, AF.Ln, [pkey], [lkey + "_ln"], scale=1.0 / n, bias=eps_t[:, 0:1])
        act(rsv, lnv, AF.Exp, [lkey + "_ln"], [lkey], scale=-0.5)

    def phase_kv(l):
        m0 = ar.mark()
        memtok = ar.alloc("memtok", [P, 2, D], F32)
        memT = ar.alloc("memT", [P, KC, MEM], F32)
        memsq = ar.alloc("memsq", [P, KC, MEM], BF16)
        memn = ar.alloc("memn", [P, KC, MEM], BF16)
        lnv = ar.alloc("kv_ln", [P, MEM], F32)
        rsv = ar.alloc("kv_rs", [P, MEM], F32)
        sqk = ar.alloc("kv_sqk", [P, MEM], BF16)
        tr.dma("sp", [(memtok[:, :, :], mem_d.rearrange("(mt p) d -> p mt d", p=P))], dsem("memtok"),
               writes=["memtok"])
        for kc in range(KC):
            b = next_bank()
            for mt in range(2):
                tr.emit("pe", lambda e, o=bank(b)[:, mt * P:(mt + 1) * P], i=memtok[:, mt, kc * P:(kc + 1) * P]:
                        e.transpose(o, i, ident_f[:, :]), ["memtok", "ident_f"], [bkey(b)])
            cp("act" if kc % 2 else "dve", memT[:, kc, :], bank(b)[:, 0:MEM], [bkey(b)], [("memT", kc)])
            act(memsq[:, kc, :], memT[:, kc, :], AF.Square, [("memT", kc)], [("memsq", kc)])
        b = next_bank()
        for kc in range(KC):
            mm(bank(b)[:, 0:MEM], ones_b[:, :], memsq[:, kc, :], kc == 0, kc == KC - 1,
               [("memsq", kc), "ones_b"], [bkey(b)])
        rms_stats_to_rs(bank(b)[:, 0:MEM], bkey(b), D, lnv[:, :], rsv[:, :], "kvrs", MEM)
        for kc in range(KC):
            stt("dve", memn[:, kc, :], memT[:, kc, :], gmem_t[:, l, kc:kc + 1], rsv[:, :], ALU.mult, ALU.mult,
                [("memT", kc), "kvrs"], [("memn", kc)])
        w, wk = wnext(("kv", l, 0))
        w3 = w[:, :].rearrange("p (kc n) -> p kc n", n=512)
        for h in range(4):
            b = next_bank()
            for kc in range(KC):
                mm(bank(b)[:, 0:MEM], w3[:, kc, h * P:(h + 1) * P], memn[:, kc, :], kc == 0, kc == KC - 1,
                   [wk, ("memn", kc)], [bkey(b)])
            act(sqk[:, :], bank(b)[:, 0:MEM], AF.Square, [bkey(b)], ["sqk"])
            b2 = next_bank()
            mm(bank(b2)[:, 0:MEM], ones_b[:, :], sqk[:, :], True, True, ["sqk", "ones_b"], [bkey(b2)])
            rms_stats_to_rs(bank(b2)[:, 0:MEM], bkey(b2), P, lnv[:, :], rsv[:, :], "kvrs2", MEM)
            stt("dve", kmT[:, h, :], bank(b)[:, 0:MEM], gkm_t[:, l:l + 1], rsv[:, :], ALU.mult, ALU.mult,
                [bkey(b), "kvrs2"], ["kmT"])
        w, wk = wnext(("kv", l, 1))
        w3 = w[:, :].rearrange("p (kc n) -> p kc n", n=512)
        for mt in range(2):
            b = next_bank()
            for kc in range(KC):
                mm(bank(b), memn[:, kc, mt * P:(mt + 1) * P], w3[:, kc, :], kc == 0, kc == KC - 1,
                   [wk, ("memn", kc)], [bkey(b)])
            cp("dve", vm[:, mt, :], bank(b), [bkey(b)], ["vm"])
        tr.barrier()
        ar.release(m0)

    def phase_p(l):
        m0 = ar.mark()
        xT = ar.alloc("xT", [P, KC, T], F32)
        hT = ar.alloc("hT", [P, KC, T], BF16)
        mixT = ar.alloc("mixT", [P, KC, T], BF16)
        aT = ar.alloc("aT", [P, 32, T], BF16)
        gcb = ar.alloc("gcb", [P, 4, T], F32)
        sq = [ar.alloc(f"sq{i}", [P, T], BF16) for i in range(4)]
        lnv = [ar.alloc(f"lnv{i}", [P, T], F32) for i in range(2)]
        rsv = [ar.alloc(f"rsv{i}", [P, T], F32) for i in range(2)]
        rl = [ar.alloc(f"rl{i}", [P, T], F32) for i in range(2)]
        qst = [ar.alloc(f"qst{i}", [P, T], BF16) for i in range(2)]
        vst = [ar.alloc(f"vst{i}", [P, T], BF16) for i in range(2)]
        gst = [ar.alloc(f"gst{i}", [P, T], BF16) for i in range(2)]
        most = [ar.alloc(f"most{i}", [P, T], BF16) for i in range(2)]
        qmn = [ar.alloc(f"qmn{i}", [P, T], BF16) for i in range(2)]
        pm = [ar.alloc(f"pm{i}", [P, 2 * T], BF16) for i in range(2)]
        rinv = [ar.alloc(f"rinv{i}", [P, T], F32) for i in range(2)]
        zc = ar.alloc("zc", [P, 4, T + 2], BF16)
        gbc = ar.alloc("gbc", [P, 4, T], BF16)
        cy = ar.alloc("cy", [P, T], F32)
        xin = nc.alloc_sbuf_tensor_at(f"xin_{l}", [P, 4, D], F32, offset=ar.off["aT"])
        xout = xin

        SB = 6

        sq_pending = []

        def sq_accum(c):
            while sq_pending:
                sq_pending.pop(0)()
            i = rot("sq", 4)
            act(sq[i][:, :], xT[:, c, :], AF.Square, [("xT", c)], [("sq", i)])
            sq_pending.append(lambda c=c, i=i: mm(bank(SB), ones_b[:, :], sq[i][:, :], c == 0, c == KC - 1,
                                                  [("sq", i), "ones_b"], [bkey(SB)]))

        def norm_finish(g_t, tagl):
            while sq_pending:
                sq_pending.pop(0)()
            i = rot("lnv", 2)
            act(lnv[i][:, :], bank(SB), AF.Ln, [bkey(SB)], [("lnv", i)], scale=1.0 / D, bias=eps_t[:, 0:1])
            act(rsv[i][:, :], lnv[i][:, :], AF.Exp, [("lnv", i)], [("rsv", i)], scale=-0.5)
            for kc in range(KC):
                stt("dve", hT[:, kc, :], xT[:, kc, :], g_t[:, tagl, kc:kc + 1],
                    rsv[i][:, :], ALU.mult, ALU.mult, [("xT", kc), ("rsv", i)], [("hT", kc)])

        def load_x(tt_):
            s1 = tt_ * T
            if l == 0:
                tr.dma("sp", [(xin[:, :, :], x_d[s1:s1 + T, :].rearrange("(j p) d -> p j d", p=P))], dsem("xin"),
                       writes=["xin"])
            else:
                tr.dma("sp", [(xT[:, :, :], xres[:, :, s1:s1 + T])], dsem("xT"),
                       writes=[("xT", kc) for kc in range(KC)])

        def load_mix(tt_):
            s1 = tt_ * T
            tr.dma("sp", [(mixT[:, 0:8, :], mixT_s[0:8, :, s1:s1 + T].rearrange("c p s -> p c s")),
                          (mixT[:, 12:16, :], mixT_s[12:16, :, s1:s1 + T].rearrange("c p s -> p c s"))],
                   dsem("mixTl"), writes=[("mixT", kc) for kc in range(8)] + [("mixT", kc) for kc in range(12, 16)])
            a = max(s1 - 1, 0)
            b_ = min(s1 + T + 1, S)
            if s1 == 0:
                memset("dve", zc[:, :, 0:1], 0.0, ["zc"])
            if s1 + T == S:
                memset("dve", zc[:, :, T + 1:T + 2], 0.0, ["zc"])
            tr.dma("sp", [(zc[:, :, a - (s1 - 1):b_ - (s1 - 1)], zT_s[:, :, a:b_].rearrange("j p s -> p j s")),
                          (gbc[:, :, :], gbT_s[:, :, s1:s1 + T].rearrange("j p s -> p j s"))],
                   dsem("zcl"), writes=["zc", "gbc"])

        def prep_steps():
            lp = l - 1
            steps = []
            for j in range(4):
                def conv_step(j=j):
                    ts("dve", cy[:, :], zc[:, j, 1:T + 1], convw_t[:, lp, 1, j:j + 1], ALU.mult, ["zc"], ["cy"])
                    stt("dve", cy[:, :], zc[:, j, 0:T], convw_t[:, lp, 0, j:j + 1], cy[:, :], ALU.mult, ALU.add,
                        ["zc", "cy"], ["cy"])
                    stt("dve", cy[:, :], zc[:, j, 2:T + 2], convw_t[:, lp, 2, j:j + 1], cy[:, :], ALU.mult, ALU.add,
                        ["zc", "cy"], ["cy"])
                    tt("dve", mixT[:, 8 + j, :], cy[:, :], gbc[:, j, :], ALU.mult, ["cy", "gbc"], [("mixT", 8 + j)])
                steps.append(conv_step)
            sqi = {}

            def norm_a(kc):
                i = rot("sq", 4)
                sqi[kc] = i
                act(sq[i][:, :], mixT[:, kc, :], AF.Square, [("mixT", kc)], [("sq", i)])

            def norm_b(kc):
                i = sqi[kc]
                b = next_bank()
                mm(bank(b), ones_b[:, :], sq[i][:, :], True, True, [("sq", i), "ones_b"], [bkey(b)])
                k = rot("lnv", 2)
                act(lnv[k][:, :], bank(b), AF.Ln, [bkey(b)], [("lnv", k)], scale=1.0 / P, bias=eps_t[:, 0:1])
                act(rsv[k][:, :], lnv[k][:, :], AF.Exp, [("lnv", k)], [("rsv", k)], scale=-0.5)
                g = gsub_t[:, lp:lp + 1] if kc < 8 else gconv_t[:, lp, kc - 8:kc - 7]
                stt("dve", mixT[:, kc, :], mixT[:, kc, :], g, rsv[k][:, :], ALU.mult, ALU.mult,
                    [("mixT", kc), ("rsv", k)], [("mixT", kc)])

            conv = steps
            steps = [lambda: (conv[0](), conv[1]()), lambda: (conv[2](), conv[3]()), lambda: norm_a(0)]
            for kc in range(11):
                steps.append(lambda kc=kc: (norm_b(kc), norm_a(kc + 1)))
            steps.append(lambda: norm_b(11))
            return steps

        for t in range(NT):
            s0 = t * T
            if t == 0:
                load_x(0)
                if l > 0:
                    load_mix(0)
                    for st_ in prep_steps():
                        st_()
            if l == 0:
                for kc in range(KC):
                    b = next_bank()
                    for j in range(4):
                        tr.emit("pe", lambda e, o=bank(b)[:, j * P:(j + 1) * P], i=xin[:, j, kc * P:(kc + 1) * P]:
                                e.transpose(o, i, ident_f[:, :]), ["xin", "ident_f"], [bkey(b)])
                    cp("act" if kc % 2 else "dve", xT[:, kc, :], bank(b), [bkey(b)], [("xT", kc)])
                    sq_accum(kc)
                if t + 1 < NT:
                    load_x(t + 1)
            else:
                lp = l - 1
                for g in range(4):
                    w, wk = wnext(("out", lp, g))
                    w3 = w[:, :].rearrange("p (kc n) -> p kc n", n=512)
                    for dc in range(4):
                        b = next_bank()
                        for kc in range(KC):
                            mm(bank(b), w3[:, kc, dc * P:(dc + 1) * P], mixT[:, kc, :], kc == 0, kc == KC - 1,
                               [wk, ("mixT", kc)], [bkey(b)])
                        c = 4 * g + dc
                        tt("dve", xT[:, c, :], xT[:, c, :], bank(b), ALU.add, [("xT", c), bkey(b)], [("xT", c)])
                        sq_accum(c)
                norm_finish(gmlp_t, lp)
                pending = []
                if t + 1 < NT:
                    load_mix(t + 1)
                    pending = prep_steps()
                for half in range(2):
                    for g in range(8):
                        if pending:
                            pending.pop(0)()
                        w, wk = wnext(("up", lp, half * 8 + g))
                        w3 = w[:, :].rearrange("p (kc n) -> p kc n", n=512)
                        for hc in range(4):
                            b = next_bank()
                            for kc in range(KC):
                                mm(bank(b), w3[:, kc, hc * P:(hc + 1) * P], hT[:, kc, :], kc == 0, kc == KC - 1,
                                   [wk, ("hT", kc)], [bkey(b)])
                            i = rot("rl", 2)
                            a = g * 4 + hc
                            act(rl[i][:, :], bank(b), AF.Relu, [bkey(b)], [("rl", i)])
                            tt("dve", aT[:, a, :], rl[i][:, :], rl[i][:, :], ALU.mult,
                               [("rl", i)], [("aT", a)])
                    for g in range(8):
                        w, wk = wnext(("down", lp, half * 8 + g))
                        w3 = w[:, :].rearrange("p (kc n) -> p kc n", n=256)
                        for dc in range(2):
                            b = next_bank()
                            for a in range(32):
                                mm(bank(b), w3[:, a, dc * P:(dc + 1) * P], aT[:, a, :], a == 0, a == 31,
                                   [wk, ("aT", a)], [bkey(b)])
                            c = 2 * g + dc
                            tt("dve", xT[:, c, :], xT[:, c, :], bank(b), ALU.add, [("xT", c), bkey(b)], [("xT", c)])
                            if half == 1 and l < L:
                                sq_accum(c)
            if l == L:
                for j in range(4):
                    for c4 in range(4):
                        b = next_bank()
                        for k in range(4):
                            kc = c4 * 4 + k
                            tr.emit("pe", lambda e, o=bank(b)[:, k * P:(k + 1) * P], i=xT[:, kc, j * P:(j + 1) * P]:
                                    e.transpose(o, i, ident_f[:, :]), [("xT", kc), "ident_f"], [bkey(b)])
                        cp("act" if c4 % 2 else "dve", xout[:, j, c4 * 512:(c4 + 1) * 512], bank(b), [bkey(b)],
                           [("aT", a_) for a_ in range(32)])
                tr.dma("sp", [(out_d[s0:s0 + T, :].rearrange("(j p) d -> p j d", p=P), xout[:, :, :])],
                       dsem("xout"), reads=[("aT", a_) for a_ in range(32)])
                if t + 1 < NT:
                    load_x(t + 1)
                continue
            if l > 0:
                tr.dma("sp", [(xres[:, :, s0:s0 + T], xT[:, :, :])], dsem("xres_st"),
                       reads=[("xT", kc) for kc in range(KC)])
            else:
                tr.dma("sp", [(xres[:, :, s0:s0 + T], xT[:, :, :])], dsem("xres_st"),
                       reads=[("xT", kc) for kc in range(KC)])
            norm_finish(gmix_t, l)
            if l > 0 and t + 1 < NT:
                load_x(t + 1)
            pend = None

            def finish_qk(item):
                b, which, h = item
                i = rot("sq", 4)
                act(sq[i][:, :], bank(b), AF.Square, [bkey(b)], [("sq", i)])
                b2 = next_bank()
                mm(bank(b2), bd_b[:, :], sq[i][:, :], True, True, [("sq", i), "bd_b"], [bkey(b2)])
                k = rot("lnv", 2)
                act(lnv[k][:, :], bank(b2), AF.Ln, [bkey(b2)], [("lnv", k)], scale=1.0 / 64, bias=eps_t[:, 0:1])
                act(rsv[k][:, :], lnv[k][:, :], AF.Exp, [("lnv", k)], [("rsv", k)], scale=-0.5)
                s = rot("qst", 2)
                g = gq_t if which == 0 else gk_t
                stt("dve", qst[s][:, :], bank(b), g[:, l:l + 1], rsv[k][:, :], ALU.mult, ALU.mult,
                    [bkey(b), ("rsv", k)], [("qst", s)])
                dst = qT_s if which == 0 else kT_s
                tr.dma("sp", [(dst[h, :, s0:s0 + T], qst[s][:, :])], dsem(("qst", s)), reads=[("qst", s)])

            for blk in range(4):
                w, wk = wnext(("in", l, blk))
                w3 = w[:, :].rearrange("p (kc n) -> p kc n", n=512)
                for oc in range(4):
                    b = next_bank()
                    for kc in range(KC):
                        mm(bank(b), w3[:, kc, oc * P:(oc + 1) * P], hT[:, kc, :], kc == 0, kc == KC - 1,
                           [wk, ("hT", kc)], [bkey(b)])
                    if pend is not None:
                        finish_qk(pend)
                    pend = (b, blk // 2, (blk % 2) * 4 + oc)
            for blk in range(4, 6):
                w, wk = wnext(("in", l, blk))
                w3 = w[:, :].rearrange("p (kc n) -> p kc n", n=512)
                for j in range(4):
                    b = next_bank()
                    for kc in range(KC):
                        mm(bank(b), hT[:, kc, j * P:(j + 1) * P], w3[:, kc, :], kc == 0, kc == KC - 1,
                           [wk, ("hT", kc)], [bkey(b)])
                    if pend is not None:
                        finish_qk(pend)
                        pend = None
                    s = rot("vst", 2)
                    cp("act", vst[s][:, :], bank(b), [bkey(b)], [("vst", s)])
                    hb = (blk - 4) * 4
                    tr.dma("sp", [(v_s[hb:hb + 4, :, t * 4 + j, :].rearrange("h p d -> p h d"),
                                   vst[s][:, :].rearrange("p (h d) -> p h d", d=P))],
                           dsem(("vst", s)), reads=[("vst", s)])
            for blk in range(6, 9):
                w, wk = wnext(("in", l, blk))
                w3 = w[:, :].rearrange("p (kc n) -> p kc n", n=512)
                for oc in range(4):
                    b = next_bank()
                    for kc in range(KC):
                        mm(bank(b), w3[:, kc, oc * P:(oc + 1) * P], hT[:, kc, :], kc == 0, kc == KC - 1,
                           [wk, ("hT", kc)], [bkey(b)])
                    if blk == 6:
                        s = rot("gst", 2)
                        cp("act", gst[s][:, :], bank(b), [bkey(b)], [("gst", s)])
                        tr.dma("sp", [(gbT_s[oc, :, s0:s0 + T], gst[s][:, :])], dsem(("gst", s)), reads=[("gst", s)])
                    elif blk == 7:
                        cp("act", gcb[:, oc, :], bank(b), [bkey(b)], [("gcb", oc)])
                    else:
                        s = rot("gst", 2)
                        tt("dve", gst[s][:, :], bank(b), gcb[:, oc, :], ALU.mult, [bkey(b), ("gcb", oc)],
                           [("gst", s)])
                        tr.dma("sp", [(zT_s[oc, :, s0:s0 + T], gst[s][:, :])], dsem(("gst", s)), reads=[("gst", s)])
            w, wk = wnext(("in", l, 9))
            w3 = w[:, :].rearrange("p (kc n) -> p kc n", n=512)
            st8 = [dict() for _ in range(4)]

            def stA(h):
                b = next_bank()
                for kc in range(KC):
                    mm(bank(b), w3[:, kc, h * P:(h + 1) * P], hT[:, kc, :], kc == 0, kc == KC - 1,
                       [wk, ("hT", kc)], [bkey(b)])
                i = rot("sq", 4)
                act(sq[i][:, :], bank(b), AF.Square, [bkey(b)], [("sq", i)])
                st8[h].update(b=b, i=i)

            def stB(h):
                b, i = st8[h]["b"], st8[h]["i"]
                b2 = next_bank()
                mm(bank(b2), ones_b[:, :], sq[i][:, :], True, True, [("sq", i), "ones_b"], [bkey(b2)])
                k = rot("lnv", 2)
                act(lnv[k][:, :], bank(b2), AF.Ln, [bkey(b2)], [("lnv", k)], scale=1.0 / P, bias=eps_t[:, 0:1])
                act(rsv[k][:, :], lnv[k][:, :], AF.Exp, [("lnv", k)], [("rsv", k)], scale=-0.5)
                qi = rot("qmn", 2)
                stt("dve", qmn[qi][:, :], bank(b), gqm_t[:, l:l + 1], rsv[k][:, :], ALU.mult, ALU.mult,
                    [bkey(b), ("rsv", k)], [("qmn", qi)])
                st8[h].update(qi=qi)

            def stC(h):
                qi = st8[h]["qi"]
                for mt in range(2):
                    mm(bank(6 + mt), kmT[:, h, mt * P:(mt + 1) * P], qmn[qi][:, :], True, True,
                       [("qmn", qi), "kmT"], [bkey(6 + mt)])
                pi = rot("pm", 2)
                act(pm[pi][:, :], pst[3][:, :], AF.Exp, [bkey(6), bkey(7)], [("pm", pi)], scale=float(P) ** -0.5)
                st8[h].update(pi=pi)

            def stD(h):
                pi = st8[h]["pi"]
                bo = next_bank()
                for mt in range(2):
                    mm(bank(bo), vm[:, mt, h * P:(h + 1) * P], pm[pi][:, mt * T:(mt + 1) * T], mt == 0, mt == 1,
                       [("pm", pi), "vm"], [bkey(bo)])
                bs = next_bank()
                for mt in range(2):
                    mm(bank(bs), ones_b[:, :], pm[pi][:, mt * T:(mt + 1) * T], mt == 0, mt == 1,
                       [("pm", pi), "ones_b"], [bkey(bs)])
                ri = rot("rinv", 2)
                recip(rinv[ri][:, :], bank(bs), [bkey(bs)], [("rinv", ri)])
                s_ = rot("most", 2)
                tt("dve", most[s_][:, :], bank(bo), rinv[ri][:, :], ALU.mult, [bkey(bo), ("rinv", ri)],
                   [("most", s_)])
                tr.dma("sp", [(mixT_s[12 + h, :, s0:s0 + T], most[s_][:, :])], dsem(("most", s_)),
                       reads=[("most", s_)])

            stA(0); stA(1); stB(0); stA(2); stB(1); stC(0); stA(3); stB(2); stC(1); stD(0)
            stB(3); stC(2); stD(1); stC(3); stD(2); stD(3)
        tr.barrier()
        ar.release(m0)

    def skip_tile(h, qb, kt):
        if skip_thresh is None:
            return False
        q0, q1 = qb * 512, qb * 512 + 511
        k0, k1 = kt * P, kt * P + P - 1
        if k1 < q0:
            md = q0 - k1
        elif k0 > q1:
            md = k0 - q1
        else:
            return False
        return (2.0 ** -(h + 1)) * md >= skip_thresh

    def phase_m(l):
        m0 = ar.mark()
        sets = []
        set_off = []
        for s in range(2):
            set_off.append((ar.mark() + 31) // 32 * 32)
            d = {}
            d["KT"] = [ar.alloc(f"KT{s}{c}", [68, S], BF16) for c in range(2)]
            d["QA"] = [ar.alloc(f"QA{s}{c}", [68, S], BF16) for c in range(2)]
            d["QB"] = [ar.alloc(f"QB{s}{c}", [68, S], BF16) for c in range(2)]
            d["V"] = ar.alloc(f"V{s}", [P, NKT, P], BF16)
            sets.append(d)
        PT = [ar.alloc(f"PT{i}", [P, 1024], BF16) for i in range(3)]
        lc = ar.alloc("lc", [P, 512], F32)
        lhi = ar.alloc("lhi", [P, 512], BF16)
        llo = ar.alloc("llo", [P, 512], BF16)
        r0 = ar.alloc("r0", [P, 512], F32)
        r1 = ar.alloc("r1", [P, 512], F32)
        o_f = ar.alloc("o_f", [P, 512], F32)
        t_f = ar.alloc("t_f", [P, 512], F32)
        lnv = ar.alloc("m_lnv", [P, 512], F32)
        rsv = ar.alloc("m_rsv", [P, 512], F32)
        sqo = ar.alloc("sqo", [P, 512], BF16)
        ost = [ar.alloc(f"ost{i}", [P, 512], BF16) for i in range(2)]

        def load_head(h, s):
            d = sets[s]
            pairs = []
            keys = []
            for c in range(2):
                pairs.append((d["KT"][c][0:64, :], kT_s[h, c * 64:(c + 1) * 64, :]))
                pairs.append((d["QA"][c][0:64, :], qT_s[h, c * 64:(c + 1) * 64, :]))
                pairs.append((d["QB"][c][0:64, :], qT_s[h, c * 64:(c + 1) * 64, :]))
                pairs.append((d["QA"][c][64:68, :], qaug_b[h, 0]))
                pairs.append((d["QB"][c][64:68, :], qaug_b[h, 1]))
            pairs.append((d["V"][:, :, :], v_s[h]))
            tr.dma("sp", pairs, dsem(("head", s)), writes=[("head", s)])

        tr.dma("pool", [(sets[0]["KT"][c][64:68, :], kaug_d) for c in range(2)], dsem("kaug"),
               writes=[("head", 0)])
        load_head(0, 0)

        tr.dma("pool", [(sets[1]["KT"][c][64:68, :], kaug_d) for c in range(2)], dsem("kaug"),
               writes=[("head", 1)])

        for pc in conv_pieces(l):
            convert(pc[0], pc[1], pc[2])

        PV = [4, 5]
        LB = 6
        RB = 7
        pend_tail = [None]
        for h in range(NH):
            s = h % 2
            d = sets[s]
            hk = ("head", s)
            if h + 1 < NH:
                load_head(h + 1, 1 - s)
            for qb in range(NQB):
                q0 = qb * 512
                kts = [kt for kt in range(NKT) if not skip_tile(h, qb, kt)]
                n = len(kts)

                def qk(i):
                    kt = kts[i]
                    k0 = kt * P
                    pr = i % 2
                    for c in range(2):
                        b = 2 * pr + c
                        KTc, QAc, QBc = d["KT"][c], d["QA"][c], d["QB"][c]
                        if k0 + P <= q0:
                            mm(bank(b), KTc[0:68, k0:k0 + P], QAc[0:68, q0:q0 + 512], True, True, [hk], [bkey(b)])
                        elif k0 >= q0 + 512:
                            mm(bank(b), KTc[0:68, k0:k0 + P], QBc[0:68, q0:q0 + 512], True, True, [hk], [bkey(b)])
                        else:
                            js = (k0 - q0) // P
                            if js > 0:
                                mm(bank(b)[:, 0:js * P], KTc[0:68, k0:k0 + P], QBc[0:68, q0:q0 + js * P],
                                   True, True, [hk], [bkey(b)])
                            mm(bank(b)[:, js * P:(js + 1) * P], KTc[0:64, k0:k0 + P],
                               QAc[0:64, q0 + js * P:q0 + (js + 1) * P], True, False, [hk], [bkey(b)])
                            mm(bank(b)[:, js * P:(js + 1) * P], ident_b[:, :], diagB[:, h, :], False, True,
                               ["ident_b", "diagB"], [bkey(b)])
                            if js < 3:
                                mm(bank(b)[:, (js + 1) * P:512], KTc[0:68, k0:k0 + P],
                                   QAc[0:68, q0 + (js + 1) * P:q0 + 512], True, True, [hk], [bkey(b)])

                def expav(i):
                    kt = kts[i]
                    pr = i % 2
                    pi = i % 3
                    act(PT[pi][:, :], pst[pr][:, :], AF.Exp, [bkey(2 * pr), bkey(2 * pr + 1)], [("PT", pi)],
                        scale=0.125)
                    first, last = (i == 0), (i == n - 1)
                    if h == 0 and qb == 0 and l == 0:
                        dbg_dump(f"PT{i}", PT[pi][:, :], [P, 1024], BF16, [("PT", pi)])
                    for c in range(2):
                        mm(bank(PV[c]), d["V"][:, kt, :], PT[pi][:, c * 512:(c + 1) * 512], first, last,
                           [hk, ("PT", pi)], [bkey(PV[c])])
                    for c in range(2):
                        mm(bank(LB)[64 * c:64 * c + 64, :], ones_b[:, 0:64], PT[pi][:, c * 512:(c + 1) * 512],
                           first, last, ["ones_b", ("PT", pi)], [bkey(LB)], tp=(0, 64 * c))

                qk(0)
                for i in range(n):
                    if i + 1 < n:
                        qk(i + 1)
                    expav(i)
                    if pend_tail[0] is not None and i == min(1, n - 1):
                        pend_tail[0]()
                        pend_tail[0] = None
                cp("act", t_f[:, :], bank(PV[1]), [bkey(PV[1])], ["t_f"])
                cp("dve", o_f[:, :], bank(PV[0]), [bkey(PV[0])], ["o_f"])
                cp("dve", lc[:, :], bank(LB), [bkey(LB)], ["lc"])
                cp("dve", lhi[:, :], lc[:, :], ["lc"], ["lhi"])
                tt("dve", llo[:, :], lc[:, :], lhi[:, :], ALU.subtract, ["lc", "lhi"], ["llo"])

                def tail(h=h, q0=q0):
                    rr_ = [r0, r1]
                    for c in range(2):
                        mm(bank(RB), sel_b[c][:, :], lhi[:, :], True, False, ["sel_b", "lhi"], [bkey(RB)])
                        mm(bank(RB), sel_b[c][:, :], llo[:, :], False, True, ["sel_b", "llo"], [bkey(RB)])
                        recip(rr_[c][:, :], bank(RB), [bkey(RB)], ["r%d" % c])
                    tt("dve", o_f[:, :], o_f[:, :], r0[:, :], ALU.mult, ["o_f", "r0"], ["o_f"])
                    tt("dve", t_f[:, :], t_f[:, :], r1[:, :], ALU.mult, ["t_f", "r1"], ["t_f"])
                    oi = rot("ost", 2)
                    stt("dve", ost[oi][:, :], t_f[:, :], neglam_t[:, l:l + 1], o_f[:, :], ALU.mult, ALU.add,
                        ["t_f", "o_f"], [("ost", oi)])
                    tr.dma("sp", [(mixT_s[h, :, q0:q0 + 512], ost[oi][:, :])], dsem(("ost", oi)),
                           reads=[("ost", oi)])
                pend_tail[0] = tail
        pend_tail[0]()
        pend_tail[0] = None
        tr.barrier()
        ar.release(m0)

    def conv_pieces(l):
        pcs = [("out", l, range(0, 4))]
        for q4 in range(4):
            pcs.append(("up", l, range(q4 * 4, q4 * 4 + 4)))
            pcs.append(("down", l, range(q4 * 4, q4 * 4 + 4)))
        if l + 1 < L:
            pcs.append(("in", l + 1, range(0, 5)))
            pcs.append(("in", l + 1, range(5, 10)))
            pcs.append(("kv", l + 1, range(0, 2)))
        return pcs

    for l in range(L):
        phase_kv(l)
        phase_p(l)
        phase_m(l)
    phase_p(L)
    tr.barrier()
    assert wpos[0] == len(wseq), (wpos[0], len(wseq))

    tr.finalize(engsem)
    with stack:
        with nc.Block() as block:
            @block.tensor
            def _(e):
                tr.replay("pe", e)

            @block.scalar
            def _(e):
                tr.replay("act", e)

            @block.vector
            def _(e):
                tr.replay("dve", e)

            @block.gpsimd
            def _(e):
                tr.replay("pool", e)

            @block.sync
            def _(e):
                tr.replay("sp", e)
    return nc


def make_consts(S):
    ident = np.eye(P, dtype=np.float32)
    pos = np.arange(S)
    hi = (pos // P) * P
    lo = pos % P
    kaug = np.stack([np.ones(S), np.ones(S), hi, lo]).astype(np.float32)
    qaug = np.zeros((NH, 2, 4, S), np.float32)
    diagb = np.zeros((NH, P, P), np.float32)
    ii = np.arange(P)
    for h in range(NH):
        m8 = 8.0 * 2.0 ** (-(h + 1))
        qaug[h, 0] = np.stack([-m8 * hi, -m8 * lo, m8 * np.ones(S), m8 * np.ones(S)])
        qaug[h, 1] = -qaug[h, 0]
        diagb[h] = -m8 * np.abs(ii[:, None] - ii[None, :])
    return {"c_ident": ident, "c_kaug": kaug, "c_qaug": qaug, "c_diagb": diagb}


_W_KEYS = ("w_in", "w_mem_kv", "w_out", "w_up", "w_down")


def pack_vecs(inputs, L):
    f = lambda k: np.asarray(inputs[k], dtype=np.float32)[:L]
    cols = []
    for k in ("g_mix", "g_mlp", "g_mem"):
        cols.append(f(k).reshape(L, KC, P).transpose(2, 0, 1).reshape(P, L * KC))
    for k in ("g_q_diff", "g_k_diff"):
        cols.append(np.concatenate([f(k).T, f(k).T], axis=0))
    for k in ("g_subln", "g_q_mem", "g_k_mem"):
        cols.append(f(k).T)
    cols.append(f("conv_w").reshape(L, 3, 4, P).transpose(3, 0, 1, 2).reshape(P, L * 12))
    cols.append(f("g_conv_out").reshape(L, 4, P).transpose(2, 0, 1).reshape(P, L * 4))
    return np.ascontiguousarray(np.concatenate(cols, axis=1))
_LAM_KEYS = ("lam_q1", "lam_k1", "lam_q2", "lam_k2")


def make_in_maps(inputs, S, L, ncores):
    shared = {k: np.ascontiguousarray(np.asarray(inputs[k], dtype=np.float32)[:L]) for k in _W_KEYS}
    for k in _LAM_KEYS:
        shared[k] = np.ascontiguousarray(np.asarray(inputs[k], dtype=np.float32)[:L].reshape(1, L * 64))
    shared["vecs"] = pack_vecs(inputs, L)
    shared.update(make_consts(S))
    x = np.asarray(inputs["x"], dtype=np.float32)
    mem = np.asarray(inputs["mem"], dtype=np.float32)
    maps = []
    for c in range(ncores):
        m = dict(shared)
        m["x"] = np.ascontiguousarray(x[c, :S])
        m["mem"] = np.ascontiguousarray(mem[c])
        maps.append(m)
    return maps


SKIP_THRESH = 60.0


def kernel(**inputs):
    x = inputs["x"]
    B, S, _ = x.shape
    L = inputs["w_in"].shape[0]
    nc = build_program(S, L, SKIP_THRESH)
    in_maps = make_in_maps(inputs, S, L, B)
    res = run_bass_kernel_spmd(nc, in_maps, core_ids=list(range(B)))
    return np.stack([np.asarray(r["out"], dtype=np.float32) for r in res.results], axis=0)
```

```python
import math
import numpy as np
import concourse.bass as bass
import concourse.mybir as mybir
from concourse.bass_utils import run_bass_kernel_spmd

F32 = mybir.dt.float32
BF16 = mybir.dt.bfloat16
AF = mybir.ActivationFunctionType
ALU = mybir.AluOpType
AX = mybir.AxisListType

P = 128
D = 2048
DFF = 8192
INW = 5120
NH = 8
T = 512
MEM = 256
KC = D // P
EPS = 1e-6
ENGS = ("pe", "act", "dve", "pool", "sp")


class Ev:
    __slots__ = ("eng", "is_dma", "needs", "sem", "val")

    def __init__(self, eng, is_dma):
        self.eng = eng
        self.is_dma = is_dma
        self.needs = False
        self.sem = None
        self.val = 0


class Tracker:
    def __init__(self):
        self.ops = {e: [] for e in ENGS}
        self.state = {}
        self.last = {e: None for e in ENGS}
        self.pending_dma = {}
        self.dma_cnt = {}

    def _deps(self, eng, reads, writes, is_dma):
        deps = []
        for k in reads:
            st = self.state.get(k)
            if st is not None and st[0] is not None:
                deps.append(st[0])
        for k in writes:
            st = self.state.get(k)
            if st is not None:
                if st[0] is not None:
                    deps.append(st[0])
                deps.extend(st[1].values())
                deps.extend(st[2])
        out = []
        seen = set()
        for d in deps:
            if id(d) in seen:
                continue
            seen.add(id(d))
            if d.is_dma or is_dma or d.eng != eng:
                d.needs = True
                out.append(d)
        return out

    def _update(self, ev, reads, writes):
        for k in reads:
            st = self.state.get(k)
            if st is None:
                st = [None, {}, []]
                self.state[k] = st
            if ev.is_dma:
                st[2].append(ev)
            else:
                st[1][ev.eng] = ev
        for k in writes:
            self.state[k] = [ev, {}, []]

    def emit(self, eng, fn, reads=(), writes=()):
        deps = self._deps(eng, reads, writes, False)
        ev = Ev(eng, False)
        self.ops[eng].append((deps, fn, ev))
        self._update(ev, reads, writes)
        self.last[eng] = ev
        return ev

    def dma(self, eng, pairs, sem, reads=(), writes=(), **kw):
        deps = self._deps(eng, reads, writes, True)
        ev = Ev(eng, True)
        ev.sem = sem
        self.dma_cnt[sem] = self.dma_cnt.get(sem, 0) + 16 * len(pairs)
        ev.val = self.dma_cnt[sem]

        def fn(e, pairs=pairs, kw=kw):
            return [e.dma_start(out=o, in_=i, **kw) for (o, i) in pairs]

        self.ops[eng].append((deps, fn, ev))
        self._update(ev, reads, writes)
        self.pending_dma[sem] = ev
        return ev

    def barrier(self):
        evs = [e for e in self.last.values() if e is not None] + list(self.pending_dma.values())
        for eng in ENGS:
            deps = []
            for d in evs:
                if d.is_dma or d.eng != eng:
                    d.needs = True
                    deps.append(d)
            self.ops[eng].append((deps, None, None))
        self.pending_dma = {}
        self.state = {}

    def finalize(self, engsem):
        for eng in ENGS:
            cnt = 0
            for (_, _, ev) in self.ops[eng]:
                if ev is not None and not ev.is_dma:
                    ev.sem = engsem[eng]
                    if ev.needs:
                        cnt += 1
                        ev.val = cnt

    def replay(self, engname, eng):
        waited = {}
        for (deps, fn, ev) in self.ops[engname]:
            for d in deps:
                key = id(d.sem)
                if waited.get(key, 0) < d.val:
                    eng.wait_ge(d.sem, d.val)
                    waited[key] = d.val
            if fn is None:
                continue
            ins = fn(eng)
            if ev.is_dma:
                for i in ins:
                    i.then_inc(ev.sem, 16)
            elif ev.needs:
                ins.then_inc(ev.sem, 1)


class Arena:
    def __init__(self, nc, lo, hi):
        self.nc = nc
        self.lo = lo
        self.hi = hi
        self.ptr = lo
        self.n = 0
        self.off = {}

    def alloc(self, name, shape, dtype):
        esz = 4 if dtype == F32 else 2
        nbytes = esz
        for s in shape[1:]:
            nbytes *= s
        off = (self.ptr + 31) // 32 * 32
        assert off + nbytes <= self.hi, f"SBUF overflow allocating {name}: {off}+{nbytes} > {self.hi}"
        self.ptr = off + nbytes
        self.n += 1
        self.off[name] = off
        return self.nc.alloc_sbuf_tensor_at(f"{name}_{self.n}", list(shape), dtype, offset=off)

    def mark(self):
        return self.ptr

    def release(self, m):
        self.ptr = m


def lambda_init(l):
    return 0.8 - 0.6 * math.exp(-0.3 * l)


def build_program(S, L, skip_thresh=None, debug=False):
    NT = S // T
    NKT = S // P
    NQB = S // 512
    nc = bass.Bass("TRN2", target_bir_lowering=False)
    tr = Tracker()

    def din(name, shape):
        return nc.dram_tensor(name, list(shape), F32, kind="ExternalInput").ap()

    x_d = din("x", [S, D])
    mem_d = din("mem", [MEM, D])
    w_in_d = din("w_in", [L, D, INW])
    w_kv_d = din("w_mem_kv", [L, D, 1024])
    w_out_d = din("w_out", [L, D, D])
    w_up_d = din("w_up", [L, D, DFF])
    w_down_d = din("w_down", [L, DFF, D])
    vecs_d = din("vecs", [P, 3 * L * KC + 5 * L + L * 12 + L * 4])
    lam_d = [din(n, [1, L * 64]) for n in ("lam_q1", "lam_k1", "lam_q2", "lam_k2")]
    ident_d = din("c_ident", [P, P])
    kaug_d = din("c_kaug", [4, S])
    qaug_d = din("c_qaug", [NH, 2, 4, S])
    diagb_d = din("c_diagb", [NH, P, P])
    out_d = nc.dram_tensor("out", [S, D], F32, kind="ExternalOutput").ap()

    def scr(name, shape, dt):
        kind = "ExternalOutput" if (debug and not name.startswith("wb_")) else "Internal"
        return nc.dram_tensor(name, list(shape), dt, kind=kind).ap()

    wb = {
        "in": scr("wb_in", [L, 10, P, 8192], BF16),
        "kv": scr("wb_kv", [L, 2, P, 8192], BF16),
        "out": scr("wb_out", [L, 4, P, 8192], BF16),
        "up": scr("wb_up", [L, 16, P, 8192], BF16),
        "down": scr("wb_down", [L, 16, P, 8192], BF16),
    }
    xres = scr("xres", [P, KC, S], F32)
    qT_s = scr("qT_s", [NH, P, S], BF16)
    kT_s = scr("kT_s", [NH, P, S], BF16)
    v_s = scr("v_s", [NH, P, NKT, P], BF16)
    zT_s = scr("zT_s", [4, P, S], BF16)
    gbT_s = scr("gbT_s", [4, P, S], BF16)
    mixT_s = scr("mixT_s", [16, P, S], BF16)
    qaug_b = scr("qaug_b", [NH, 2, 4, S], BF16)

    sem_n = [0]
    sem_objs = []

    from contextlib import ExitStack
    stack = ExitStack()

    def mksem(name):
        sem_n[0] += 1
        return stack.enter_context(nc.semaphore(f"{name}{sem_n[0]}"))

    engsem = {e: mksem("e_" + e) for e in ENGS if e != "sp"}
    engsem["sp"] = mksem("e_sp")
    semcache = {}

    def dsem(key):
        if key not in semcache:
            semcache[key] = mksem("d")
        return semcache[key]

    lo = (nc.sbuf_base + 31) // 32 * 32
    ar = Arena(nc, lo, nc.sbuf_top)
    wbuf = [ar.alloc(f"wbuf{i}", [P, 8192], BF16) for i in range(3)]
    ident_f = ar.alloc("ident_f", [P, P], F32)
    ident_b = ar.alloc("ident_b", [P, P], BF16)
    ones_b = ar.alloc("ones_b", [P, P], BF16)
    bd_b = ar.alloc("bd_b", [P, P], BF16)
    diagB = ar.alloc("diagB", [P, NH, P], BF16)
    eps_t = ar.alloc("eps_t", [P, 1], F32)
    NV = 3 * L * KC + 5 * L + L * 12 + L * 4
    vecs_t = ar.alloc("vecs_t", [P, NV], F32)
    _o = [0]

    def vview(n):
        a = _o[0]
        _o[0] += n
        return vecs_t[:, a:a + n]
    gmix_t = vview(L * KC).rearrange("p (l k) -> p l k", k=KC)
    gmlp_t = vview(L * KC).rearrange("p (l k) -> p l k", k=KC)
    gmem_t = vview(L * KC).rearrange("p (l k) -> p l k", k=KC)
    gq_t = vview(L)
    gk_t = vview(L)
    gsub_t = vview(L)
    gqm_t = vview(L)
    gkm_t = vview(L)
    convw_t = vview(L * 12).rearrange("p (l t j) -> p l t j", t=3, j=4)
    gconv_t = vview(L * 4).rearrange("p (l j) -> p l j", j=4)
    lams_t = ar.alloc("lams_t", [P, 2, L], F32)
    neglam_t = ar.alloc("neglam_t", [P, L], F32)
    kmT = ar.alloc("kmT", [P, 4, MEM], BF16)
    vm = ar.alloc("vm", [P, 2, 512], BF16)
    base_mark = ar.mark()
    lamv_t = [ar.alloc(f"lamv{i}", [P, 1, L * 64], F32) for i in range(4)]
    lamp_t = ar.alloc("lamp_t", [P, L, 64], F32)
    ar.release(base_mark)

    pst = [nc.alloc_psum_tensor(f"ps{i}", [P, 1024], F32) for i in range(4)]

    def bank(b):
        return pst[b // 2][:, (b % 2) * 512:(b % 2) * 512 + 512]

    def bkey(b):
        return ("ps", b)

    dbg_n = [0]

    def dbg_dump(name, ap, shape, dt, reads):
        if not debug:
            return
        dbg_n[0] += 1
        t = nc.dram_tensor("dbg_" + name, list(shape), dt, kind="ExternalOutput").ap()
        tr.dma("sp", [(t, ap)], dsem(("dbg", dbg_n[0])), reads=reads)

    def mm(out, lhsT, rhs, start, stop, reads, writes):
        return tr.emit("pe", lambda e, o=out, l=lhsT, r=rhs, s=start, t=stop:
                       e.matmul(o, lhsT=l, rhs=r, start=s, stop=t), reads, writes)

    def act(out, in_, func, reads, writes, scale=1.0, bias=None):
        if bias is None:
            return tr.emit("act", lambda e, o=out, i=in_, f=func, s=scale:
                           e.activation(out=o, in_=i, func=f, scale=s), reads, writes)
        return tr.emit("act", lambda e, o=out, i=in_, f=func, s=scale, b=bias:
                       e.activation(out=o, in_=i, func=f, scale=s, bias=b), reads, writes)

    def stt(eng, out, in0, scalar, in1, op0, op1, reads, writes):
        return tr.emit(eng, lambda e, o=out, a=in0, s=scalar, b=in1, p0=op0, p1=op1:
                       e.scalar_tensor_tensor(out=o, in0=a, scalar=s, in1=b, op0=p0, op1=p1), reads, writes)

    def tt(eng, out, in0, in1, op, reads, writes):
        return tr.emit(eng, lambda e, o=out, a=in0, b=in1, p=op:
                       e.tensor_tensor(out=o, in0=a, in1=b, op=p), reads, writes)

    def ts(eng, out, in0, s1, op0, reads, writes, s2=None, op1=None):
        if op1 is None:
            return tr.emit(eng, lambda e, o=out, a=in0, s=s1, p=op0:
                           e.tensor_scalar(out=o, in0=a, scalar1=s, scalar2=None, op0=p), reads, writes)
        return tr.emit(eng, lambda e, o=out, a=in0, s=s1, p=op0, s2=s2, p1=op1:
                       e.tensor_scalar(out=o, in0=a, scalar1=s, scalar2=s2, op0=p, op1=p1), reads, writes)

    def cp(eng, out, in_, reads, writes):
        if eng == "act":
            return tr.emit("act", lambda e, o=out, i=in_: e.copy(out=o, in_=i), reads, writes)
        return tr.emit(eng, lambda e, o=out, i=in_: e.tensor_copy(out=o, in_=i), reads, writes)

    def recip(out, in_, reads, writes):
        return tr.emit("dve", lambda e, o=out, i=in_: e.reciprocal(out=o, in_=i), reads, writes)

    def memset(eng, ap, val, writes):
        return tr.emit(eng, lambda e, a=ap, v=val: e.memset(a, v), (), writes)

    bank_rr = [0]

    def next_bank(n=6):
        b = bank_rr[0] % n
        bank_rr[0] += 1
        return b

    rr = {}

    def rot(name, n):
        i = rr.get(name, 0)
        rr[name] = i + 1
        return i % n

    wsrc = {"in": w_in_d, "kv": w_kv_d, "out": w_out_d, "up": w_up_d, "down": w_down_d}
    wnblk = {"in": 10, "kv": 2, "out": 4, "up": 16, "down": 16}

    def convert(name, l, blocks=None, pace=()):
        pairs = []
        for b in (range(wnblk[name]) if blocks is None else blocks):
            if name == "down":
                half, cb = b // 8, b % 8
                src = w_down_d[l, half * 4096:(half + 1) * 4096, cb * 256:(cb + 1) * 256].rearrange(
                    "(kc p) n -> p kc n", p=P)
                dst = wb[name][l, b].rearrange("p (kc n) -> p kc n", n=256)
            else:
                src = wsrc[name][l, :, b * 512:(b + 1) * 512].rearrange("(kc p) n -> p kc n", p=P)
                dst = wb[name][l, b].rearrange("p (kc n) -> p kc n", n=512)
            pairs.append((dst, src))
        tr.dma("pool", pairs, dsem(("cv", name, l)), reads=pace, writes=[("wb", name, l)])

    wseq = []
    wphase = []
    for l in range(L + 1):
        n0 = len(wseq)
        if l < L:
            wseq += [("kv", l, 0), ("kv", l, 1)]
        for t in range(NT):
            if l > 0:
                wseq += [("out", l - 1, b) for b in range(4)]
                for half in range(2):
                    wseq += [("up", l - 1, half * 8 + b) for b in range(8)]
                    wseq += [("down", l - 1, half * 8 + b) for b in range(8)]
            if l < L:
                wseq += [("in", l, b) for b in range(10)]
        wphase += [l] * (len(wseq) - n0)
    wpos = [0]
    wld = [0]

    def wnext(expect):
        i = wpos[0]
        assert wseq[i] == expect, (wseq[i], expect)
        while wld[0] < len(wseq) and wld[0] <= i + 2 and wphase[wld[0]] == wphase[i]:
            j = wld[0]
            name, l, b = wseq[j]
            tr.dma("sp", [(wbuf[j % 3][:, :], wb[name][l, b])], dsem(("wl", j % 3)),
                   reads=[("wb", name, l)], writes=[("wbuf", j % 3)])
            wld[0] += 1
        wpos[0] += 1
        return wbuf[i % 3], ("wbuf", i % 3)

    tr.dma("sp", [(ident_f[:, :], ident_d)], dsem("c0"), writes=["ident_f"])
    tr.dma("pool", [(ident_b[:, :], ident_d)], dsem("c1"), writes=["ident_b"])
    tr.dma("pool", [(diagB[:, :, :], diagb_d.rearrange("h i j -> i h j"))], dsem("c2"), writes=["diagB"])
    tr.dma("pool", [(qaug_b, qaug_d)], dsem("c3"), writes=["qaug_b"])
    convert("in", 0)
    convert("kv", 0)
    memset("dve", ones_b[:, :], 1.0, ["ones_b"])
    memset("dve", bd_b[:, :], 0.0, ["bd_b"])
    memset("dve", bd_b[0:64, 0:64], 1.0, ["bd_b"])
    memset("dve", bd_b[64:128, 64:128], 1.0, ["bd_b"])
    memset("dve", eps_t[:, :], EPS, ["eps_t"])
    tr.dma("sp", [(vecs_t[:, :], vecs_d)], dsem("c4"), writes=["small"])
    tr.dma("sp", [(lamv_t[i][:, :, :], lam_d[i].partition_broadcast(P)) for i in range(4)], dsem("c5"),
           writes=["lamv"])
    for j in range(2):
        tt("dve", lamp_t[:, :, :], lamv_t[2 * j][:, 0, :].rearrange("p (l d) -> p l d", d=64),
           lamv_t[2 * j + 1][:, 0, :].rearrange("p (l d) -> p l d", d=64), ALU.mult, ["lamv"], ["lamp"])
        tr.emit("dve", lambda e, o=lams_t[:, j, :], i=lamp_t[:, :, :]:
                e.tensor_reduce(out=o, in_=i, axis=AX.X, op=ALU.add), ["lamp"], ["lams"])
    act(lams_t[:, :, :], lams_t[:, :, :], AF.Exp, ["lams"], ["lams"])
    for l in range(L):
        stt("dve", neglam_t[:, l:l + 1], lams_t[:, 1, l:l + 1], -lambda_init(l), lams_t[:, 0, l:l + 1],
            ALU.add, ALU.subtract, ["lams"], ["neglam"])
        ts("dve", gsub_t[:, l:l + 1], gsub_t[:, l:l + 1], 1.0 - lambda_init(l), ALU.mult, ["small"], ["small"])
    dbg_dump("neglam", neglam_t[:, :], [P, L], F32, ["neglam"])
    dbg_dump("lams", lams_t[:, :, :], [P, 2, L], F32, ["lams"])
    dbg_dump("lamp", lamp_t[:, :, :], [P, L, 64], F32, ["lamp"])
    dbg_dump("vecs", vecs_t[:, :], [P, NV], F32, ["small"])
    tr.barrier()

    def rms_stats_to_rs(psb, pkey, n, lnv, rsv, lkey, width):
        act(lnv, psb, AF.Ln, [pkey], [lkey + "_ln"], scale=1.0 / n, bias=eps_t[:, 0:1])
        act(rsv, lnv, AF.Exp, [lkey + "_ln"], [lkey], scale=-0.5)

    def phase_kv(l):
        m0 = ar.mark()
        memtok = ar.alloc("memtok", [P, 2, D], F32)
        memT = ar.alloc("memT", [P, KC, MEM], F32)
        memsq = ar.alloc("memsq", [P, KC, MEM], BF16)
        memn = ar.alloc("memn", [P, KC, MEM], BF16)
        lnv = ar.alloc("kv_ln", [P, MEM], F32)
        rsv = ar.alloc("kv_rs", [P, MEM], F32)
        sqk = ar.alloc("kv_sqk", [P, MEM], BF16)
        tr.dma("sp", [(memtok[:, :, :], mem_d.rearrange("(mt p) d -> p mt d", p=P))], dsem("memtok"),
               writes=["memtok"])
        for kc in range(KC):
            b = next_bank()
            for mt in range(2):
                tr.emit("pe", lambda e, o=bank(b)[:, mt * P:(mt + 1) * P], i=memtok[:, mt, kc * P:(kc + 1) * P]:
                        e.transpose(o, i, ident_f[:, :]), ["memtok", "ident_f"], [bkey(b)])
            cp("act" if kc % 2 else "dve", memT[:, kc, :], bank(b)[:, 0:MEM], [bkey(b)], [("memT", kc)])
            act(memsq[:, kc, :], memT[:, kc, :], AF.Square, [("memT", kc)], [("memsq", kc)])
        b = next_bank()
        for kc in range(KC):
            mm(bank(b)[:, 0:MEM], ones_b[:, :], memsq[:, kc, :], kc == 0, kc == KC - 1,
               [("memsq", kc), "ones_b"], [bkey(b)])
        rms_stats_to_rs(bank(b)[:, 0:MEM], bkey(b), D, lnv[:, :], rsv[:, :], "kvrs", MEM)
        for kc in range(KC):
            stt("dve", memn[:, kc, :], memT[:, kc, :], gmem_t[:, l, kc:kc + 1], rsv[:, :], ALU.mult, ALU.mult,
                [("memT", kc), "kvrs"], [("memn", kc)])
        w, wk = wnext(("kv", l, 0))
        w3 = w[:, :].rearrange("p (kc n) -> p kc n", n=512)
        for h in range(4):
            b = next_bank()
            for kc in range(KC):
                mm(bank(b)[:, 0:MEM], w3[:, kc, h * P:(h + 1) * P], memn[:, kc, :], kc == 0, kc == KC - 1,
                   [wk, ("memn", kc)], [bkey(b)])
            act(sqk[:, :], bank(b)[:, 0:MEM], AF.Square, [bkey(b)], ["sqk"])
            b2 = next_bank()
            mm(bank(b2)[:, 0:MEM], ones_b[:, :], sqk[:, :], True, True, ["sqk", "ones_b"], [bkey(b2)])
            rms_stats_to_rs(bank(b2)[:, 0:MEM], bkey(b2), P, lnv[:, :], rsv[:, :], "kvrs2", MEM)
            stt("dve", kmT[:, h, :], bank(b)[:, 0:MEM], gkm_t[:, l:l + 1], rsv[:, :], ALU.mult, ALU.mult,
                [bkey(b), "kvrs2"], ["kmT"])
        w, wk = wnext(("kv", l, 1))
        w3 = w[:, :].rearrange("p (kc n) -> p kc n", n=512)
        for mt in range(2):
            b = next_bank()
            for kc in range(KC):
                mm(bank(b), memn[:, kc, mt * P:(mt + 1) * P], w3[:, kc, :], kc == 0, kc == KC - 1,
                   [wk, ("memn", kc)], [bkey(b)])
            cp("dve", vm[:, mt, :], bank(b), [bkey(b)], ["vm"])
        tr.barrier()
        ar.release(m0)

    def phase_p(l):
        m0 = ar.mark()
        xT = ar.alloc("xT", [P, KC, T], F32)
        hT = ar.alloc("hT", [P, KC, T], BF16)
        mixT = ar.alloc("mixT", [P, KC, T], BF16)
        aT = ar.alloc("aT", [P, 32, T], BF16)
        gcb = ar.alloc("gcb", [P, 4, T], F32)
        sq = [ar.alloc(f"sq{i}", [P, T], BF16) for i in range(4)]
        lnv = [ar.alloc(f"lnv{i}", [P, T], F32) for i in range(2)]
        rsv = [ar.alloc(f"rsv{i}", [P, T], F32) for i in range(2)]
        rl = [ar.alloc(f"rl{i}", [P, T], F32) for i in range(2)]
        qst = [ar.alloc(f"qst{i}", [P, T], BF16) for i in range(2)]
        vst = [ar.alloc(f"vst{i}", [P, T], BF16) for i in range(2)]
        gst = [ar.alloc(f"gst{i}", [P, T], BF16) for i in range(2)]
        most = [ar.alloc(f"most{i}", [P, T], BF16) for i in range(2)]
        qmn = [ar.alloc(f"qmn{i}", [P, T], BF16) for i in range(2)]
        pm = [ar.alloc(f"pm{i}", [P, 2 * T], BF16) for i in range(2)]
        rinv = [ar.alloc(f"rinv{i}", [P, T], F32) for i in range(2)]
        zc = ar.alloc("zc", [P, 4, T + 2], BF16)
        gbc = ar.alloc("gbc", [P, 4, T], BF16)
        cy = ar.alloc("cy", [P, T], F32)
        xin = nc.alloc_sbuf_tensor_at(f"xin_{l}", [P, 4, D], F32, offset=ar.off["aT"])
        xout = xin

        SB = 6

        sq_pending = []

        def sq_accum(c):
            while sq_pending:
                sq_pending.pop(0)()
            i = rot("sq", 4)
            act(sq[i][:, :], xT[:, c, :], AF.Square, [("xT", c)], [("sq", i)])
            sq_pending.append(lambda c=c, i=i: mm(bank(SB), ones_b[:, :], sq[i][:, :], c == 0, c == KC - 1,
                                                  [("sq", i), "ones_b"], [bkey(SB)]))

        def norm_finish(g_t, tagl):
            while sq_pending:
                sq_pending.pop(0)()
            i = rot("lnv", 2)
            act(lnv[i][:, :], bank(SB), AF.Ln, [bkey(SB)], [("lnv", i)], scale=1.0 / D, bias=eps_t[:, 0:1])
            act(rsv[i][:, :], lnv[i][:, :], AF.Exp, [("lnv", i)], [("rsv", i)], scale=-0.5)
            for kc in range(KC):
                stt("dve", hT[:, kc, :], xT[:, kc, :], g_t[:, tagl, kc:kc + 1],
                    rsv[i][:, :], ALU.mult, ALU.mult, [("xT", kc), ("rsv", i)], [("hT", kc)])

        def load_x(tt_):
            s1 = tt_ * T
            if l == 0:
                tr.dma("sp", [(xin[:, :, :], x_d[s1:s1 + T, :].rearrange("(j p) d -> p j d", p=P))], dsem("xin"),
                       writes=["xin"])
            else:
                tr.dma("sp", [(xT[:, :, :], xres[:, :, s1:s1 + T])], dsem("xT"),
                       writes=[("xT", kc) for kc in range(KC)])

        def load_mix(tt_):
            s1 = tt_ * T
            tr.dma("sp", [(mixT[:, 0:8, :], mixT_s[0:8, :, s1:s1 + T].rearrange("c p s -> p c s")),
                          (mixT[:, 12:16, :], mixT_s[12:16, :, s1:s1 + T].rearrange("c p s -> p c s"))],
                   dsem("mixTl"), writes=[("mixT", kc) for kc in range(8)] + [("mixT", kc) for kc in range(12, 16)])
            a = max(s1 - 1, 0)
            b_ = min(s1 + T + 1, S)
            if s1 == 0:
                memset("dve", zc[:, :, 0:1], 0.0, ["zc"])
            if s1 + T == S:
                memset("dve", zc[:, :, T + 1:T + 2], 0.0, ["zc"])
            tr.dma("sp", [(zc[:, :, a - (s1 - 1):b_ - (s1 - 1)], zT_s[:, :, a:b_].rearrange("j p s -> p j s")),
                          (gbc[:, :, :], gbT_s[:, :, s1:s1 + T].rearrange("j p s -> p j s"))],
                   dsem("zcl"), writes=["zc", "gbc"])

        def prep_steps():
            lp = l - 1
            steps = []
            for j in range(4):
                def conv_step(j=j):
                    ts("dve", cy[:, :], zc[:, j, 1:T + 1], convw_t[:, lp, 1, j:j + 1], ALU.mult, ["zc"], ["cy"])
                    stt("dve", cy[:, :], zc[:, j, 0:T], convw_t[:, lp, 0, j:j + 1], cy[:, :], ALU.mult, ALU.add,
                        ["zc", "cy"], ["cy"])
                    stt("dve", cy[:, :], zc[:, j, 2:T + 2], convw_t[:, lp, 2, j:j + 1], cy[:, :], ALU.mult, ALU.add,
                        ["zc", "cy"], ["cy"])
                    tt("dve", mixT[:, 8 + j, :], cy[:, :], gbc[:, j, :], ALU.mult, ["cy", "gbc"], [("mixT", 8 + j)])
                steps.append(conv_step)
            sqi = {}

            def norm_a(kc):
                i = rot("sq", 4)
                sqi[kc] = i
                act(sq[i][:, :], mixT[:, kc, :], AF.Square, [("mixT", kc)], [("sq", i)])

            def norm_b(kc):
                i = sqi[kc]
                b = next_bank()
                mm(bank(b), ones_b[:, :], sq[i][:, :], True, True, [("sq", i), "ones_b"], [bkey(b)])
                k = rot("lnv", 2)
                act(lnv[k][:, :], bank(b), AF.Ln, [bkey(b)], [("lnv", k)], scale=1.0 / P, bias=eps_t[:, 0:1])
                act(rsv[k][:, :], lnv[k][:, :], AF.Exp, [("lnv", k)], [("rsv", k)], scale=-0.5)
                g = gsub_t[:, lp:lp + 1] if kc < 8 else gconv_t[:, lp, kc - 8:kc - 7]
                stt("dve", mixT[:, kc, :], mixT[:, kc, :], g, rsv[k][:, :], ALU.mult, ALU.mult,
                    [("mixT", kc), ("rsv", k)], [("mixT", kc)])

            conv = steps
            steps = [lambda: (conv[0](), conv[1]()), lambda: (conv[2](), conv[3]()), lambda: norm_a(0)]
            for kc in range(11):
                steps.append(lambda kc=kc: (norm_b(kc), norm_a(kc + 1)))
            steps.append(lambda: norm_b(11))
            return steps

        for t in range(NT):
            s0 = t * T
            if t == 0:
                load_x(0)
                if l > 0:
                    load_mix(0)
                    for st_ in prep_steps():
                        st_()
            if l == 0:
                for kc in range(KC):
                    b = next_bank()
                    for j in range(4):
                        tr.emit("pe", lambda e, o=bank(b)[:, j * P:(j + 1) * P], i=xin[:, j, kc * P:(kc + 1) * P]:
                                e.transpose(o, i, ident_f[:, :]), ["xin", "ident_f"], [bkey(b)])
                    cp("act" if kc % 2 else "dve", xT[:, kc, :], bank(b), [bkey(b)], [("xT", kc)])
                    sq_accum(kc)
                if t + 1 < NT:
                    load_x(t + 1)
            else:
                lp = l - 1
                for g in range(4):
                    w, wk = wnext(("out", lp, g))
                    w3 = w[:, :].rearrange("p (kc n) -> p kc n", n=512)
                    for dc in range(4):
                        b = next_bank()
                        for kc in range(KC):
                            mm(bank(b), w3[:, kc, dc * P:(dc + 1) * P], mixT[:, kc, :], kc == 0, kc == KC - 1,
                               [wk, ("mixT", kc)], [bkey(b)])
                        c = 4 * g + dc
                        tt("dve", xT[:, c, :], xT[:, c, :], bank(b), ALU.add, [("xT", c), bkey(b)], [("xT", c)])
                        sq_accum(c)
                norm_finish(gmlp_t, lp)
                pending = []
                if t + 1 < NT:
                    load_mix(t + 1)
                    pending = prep_steps()
                blk_i = 0
                for half in range(2):
                    for g in range(8):
                        if pending and blk_i >= 3:
                            pending.pop(0)()
                        blk_i += 1
                        w, wk = wnext(("up", lp, half * 8 + g))
                        w3 = w[:, :].rearrange("p (kc n) -> p kc n", n=512)
                        for hc in range(4):
                            b = next_bank()
                            for kc in range(KC):
                                mm(bank(b), w3[:, kc, hc * P:(hc + 1) * P], hT[:, kc, :], kc == 0, kc == KC - 1,
                                   [wk, ("hT", kc)], [bkey(b)])
                            i = rot("rl", 2)
                            a = g * 4 + hc
                            act(rl[i][:, :], bank(b), AF.Relu, [bkey(b)], [("rl", i)])
                            tt("dve", aT[:, a, :], rl[i][:, :], rl[i][:, :], ALU.mult,
                               [("rl", i)], [("aT", a)])
                    for g in range(8):
                        if pending and blk_i >= 3:
                            pending.pop(0)()
                        blk_i += 1
                        w, wk = wnext(("down", lp, half * 8 + g))
                        w3 = w[:, :].rearrange("p (kc n) -> p kc n", n=256)
                        for dc in range(2):
                            b = next_bank()
                            for a in range(32):
                                mm(bank(b), w3[:, a, dc * P:(dc + 1) * P], aT[:, a, :], a == 0, a == 31,
                                   [wk, ("aT", a)], [bkey(b)])
                            c = 2 * g + dc
                            tt("dve", xT[:, c, :], xT[:, c, :], bank(b), ALU.add, [("xT", c), bkey(b)], [("xT", c)])
                            if half == 1 and l < L:
                                sq_accum(c)
            if l == L:
                for j in range(4):
                    for c4 in range(4):
                        b = next_bank()
                        for k in range(4):
                            kc = c4 * 4 + k
                            tr.emit("pe", lambda e, o=bank(b)[:, k * P:(k + 1) * P], i=xT[:, kc, j * P:(j + 1) * P]:
                                    e.transpose(o, i, ident_f[:, :]), [("xT", kc), "ident_f"], [bkey(b)])
                        cp("act" if c4 % 2 else "dve", xout[:, j, c4 * 512:(c4 + 1) * 512], bank(b), [bkey(b)],
                           [("aT", a_) for a_ in range(32)])
                tr.dma("sp", [(out_d[s0:s0 + T, :].rearrange("(j p) d -> p j d", p=P), xout[:, :, :])],
                       dsem("xout"), reads=[("aT", a_) for a_ in range(32)])
                if t + 1 < NT:
                    load_x(t + 1)
                continue
            if l > 0:
                tr.dma("sp", [(xres[:, :, s0:s0 + T], xT[:, :, :])], dsem("xres_st"),
                       reads=[("xT", kc) for kc in range(KC)])
            else:
                tr.dma("sp", [(xres[:, :, s0:s0 + T], xT[:, :, :])], dsem("xres_st"),
                       reads=[("xT", kc) for kc in range(KC)])
            norm_finish(gmix_t, l)
            if l > 0 and t + 1 < NT:
                load_x(t + 1)
            pend = None

            def finish_qk(item):
                b, which, h = item
                i = rot("sq", 4)
                act(sq[i][:, :], bank(b), AF.Square, [bkey(b)], [("sq", i)])
                b2 = next_bank()
                mm(bank(b2), bd_b[:, :], sq[i][:, :], True, True, [("sq", i), "bd_b"], [bkey(b2)])
                k = rot("lnv", 2)
                act(lnv[k][:, :], bank(b2), AF.Ln, [bkey(b2)], [("lnv", k)], scale=1.0 / 64, bias=eps_t[:, 0:1])
                act(rsv[k][:, :], lnv[k][:, :], AF.Exp, [("lnv", k)], [("rsv", k)], scale=-0.5)
                s = rot("qst", 2)
                g = gq_t if which == 0 else gk_t
                stt("dve", qst[s][:, :], bank(b), g[:, l:l + 1], rsv[k][:, :], ALU.mult, ALU.mult,
                    [bkey(b), ("rsv", k)], [("qst", s)])
                dst = qT_s if which == 0 else kT_s
                tr.dma("sp", [(dst[h, :, s0:s0 + T], qst[s][:, :])], dsem(("qst", s)), reads=[("qst", s)])

            for blk in range(4):
                w, wk = wnext(("in", l, blk))
                w3 = w[:, :].rearrange("p (kc n) -> p kc n", n=512)
                for oc in range(4):
                    b = next_bank()
                    for kc in range(KC):
                        mm(bank(b), w3[:, kc, oc * P:(oc + 1) * P], hT[:, kc, :], kc == 0, kc == KC - 1,
                           [wk, ("hT", kc)], [bkey(b)])
                    if pend is not None:
                        finish_qk(pend)
                    pend = (b, blk // 2, (blk % 2) * 4 + oc)
            for blk in range(4, 6):
                w, wk = wnext(("in", l, blk))
                w3 = w[:, :].rearrange("p (kc n) -> p kc n", n=512)
                for j in range(4):
                    b = next_bank()
                    for kc in range(KC):
                        mm(bank(b), hT[:, kc, j * P:(j + 1) * P], w3[:, kc, :], kc == 0, kc == KC - 1,
                           [wk, ("hT", kc)], [bkey(b)])
                    if pend is not None:
                        finish_qk(pend)
                        pend = None
                    s = rot("vst", 2)
                    cp("act", vst[s][:, :], bank(b), [bkey(b)], [("vst", s)])
                    hb = (blk - 4) * 4
                    tr.dma("sp", [(v_s[hb:hb + 4, :, t * 4 + j, :].rearrange("h p d -> p h d"),
                                   vst[s][:, :].rearrange("p (h d) -> p h d", d=P))],
                           dsem(("vst", s)), reads=[("vst", s)])
            for blk in range(6, 9):
                w, wk = wnext(("in", l, blk))
                w3 = w[:, :].rearrange("p (kc n) -> p kc n", n=512)
                for oc in range(4):
                    b = next_bank()
                    for kc in range(KC):
                        mm(bank(b), w3[:, kc, oc * P:(oc + 1) * P], hT[:, kc, :], kc == 0, kc == KC - 1,
                           [wk, ("hT", kc)], [bkey(b)])
                    if blk == 6:
                        s = rot("gst", 2)
                        cp("act", gst[s][:, :], bank(b), [bkey(b)], [("gst", s)])
                        tr.dma("sp", [(gbT_s[oc, :, s0:s0 + T], gst[s][:, :])], dsem(("gst", s)), reads=[("gst", s)])
                    elif blk == 7:
                        cp("act", gcb[:, oc, :], bank(b), [bkey(b)], [("gcb", oc)])
                    else:
                        s = rot("gst", 2)
                        tt("dve", gst[s][:, :], bank(b), gcb[:, oc, :], ALU.mult, [bkey(b), ("gcb", oc)],
                           [("gst", s)])
                        tr.dma("sp", [(zT_s[oc, :, s0:s0 + T], gst[s][:, :])], dsem(("gst", s)), reads=[("gst", s)])
            w, wk = wnext(("in", l, 9))
            w3 = w[:, :].rearrange("p (kc n) -> p kc n", n=512)
            st8 = [dict() for _ in range(4)]

            def stA(h):
                b = next_bank()
                for kc in range(KC):
                    mm(bank(b), w3[:, kc, h * P:(h + 1) * P], hT[:, kc, :], kc == 0, kc == KC - 1,
                       [wk, ("hT", kc)], [bkey(b)])
                i = rot("sq", 4)
                act(sq[i][:, :], bank(b), AF.Square, [bkey(b)], [("sq", i)])
                st8[h].update(b=b, i=i)

            def stB(h):
                b, i = st8[h]["b"], st8[h]["i"]
                b2 = next_bank()
                mm(bank(b2), ones_b[:, :], sq[i][:, :], True, True, [("sq", i), "ones_b"], [bkey(b2)])
                k = rot("lnv", 2)
                act(lnv[k][:, :], bank(b2), AF.Ln, [bkey(b2)], [("lnv", k)], scale=1.0 / P, bias=eps_t[:, 0:1])
                act(rsv[k][:, :], lnv[k][:, :], AF.Exp, [("lnv", k)], [("rsv", k)], scale=-0.5)
                qi = rot("qmn", 2)
                stt("dve", qmn[qi][:, :], bank(b), gqm_t[:, l:l + 1], rsv[k][:, :], ALU.mult, ALU.mult,
                    [bkey(b), ("rsv", k)], [("qmn", qi)])
                st8[h].update(qi=qi)

            def stC(h):
                qi = st8[h]["qi"]
                for mt in range(2):
                    mm(bank(6 + mt), kmT[:, h, mt * P:(mt + 1) * P], qmn[qi][:, :], True, True,
                       [("qmn", qi), "kmT"], [bkey(6 + mt)])
                pi = rot("pm", 2)
                act(pm[pi][:, :], pst[3][:, :], AF.Exp, [bkey(6), bkey(7)], [("pm", pi)], scale=float(P) ** -0.5)
                st8[h].update(pi=pi)

            def stD(h):
                pi = st8[h]["pi"]
                bo = next_bank()
                for mt in range(2):
                    mm(bank(bo), vm[:, mt, h * P:(h + 1) * P], pm[pi][:, mt * T:(mt + 1) * T], mt == 0, mt == 1,
                       [("pm", pi), "vm"], [bkey(bo)])
                bs = next_bank()
                for mt in range(2):
                    mm(bank(bs), ones_b[:, :], pm[pi][:, mt * T:(mt + 1) * T], mt == 0, mt == 1,
                       [("pm", pi), "ones_b"], [bkey(bs)])
                ri = rot("rinv", 2)
                recip(rinv[ri][:, :], bank(bs), [bkey(bs)], [("rinv", ri)])
                s_ = rot("most", 2)
                tt("dve", most[s_][:, :], bank(bo), rinv[ri][:, :], ALU.mult, [bkey(bo), ("rinv", ri)],
                   [("most", s_)])
                tr.dma("sp", [(mixT_s[12 + h, :, s0:s0 + T], most[s_][:, :])], dsem(("most", s_)),
                       reads=[("most", s_)])

            stA(0); stA(1); stB(0); stA(2); stB(1); stC(0); stA(3); stB(2); stC(1); stD(0)
            stB(3); stC(2); stD(1); stC(3); stD(2); stD(3)
        tr.barrier()
        ar.release(m0)

    def skip_tile(h, qb, kt):
        if skip_thresh is None:
            return False
        q0, q1 = qb * 512, qb * 512 + 511
        k0, k1 = kt * P, kt * P + P - 1
        if k1 < q0:
            md = q0 - k1
        elif k0 > q1:
            md = k0 - q1
        else:
            return False
        return (2.0 ** -(h + 1)) * md >= skip_thresh

    def phase_m(l):
        m0 = ar.mark()
        sets = []
        set_off = []
        for s in range(2):
            set_off.append((ar.mark() + 31) // 32 * 32)
            d = {}
            d["KT"] = [ar.alloc(f"KT{s}{c}", [68, S], BF16) for c in range(2)]
            d["QA"] = [ar.alloc(f"QA{s}{c}", [68, S], BF16) for c in range(2)]
            d["QB"] = [ar.alloc(f"QB{s}{c}", [68, S], BF16) for c in range(2)]
            d["V"] = ar.alloc(f"V{s}", [P, NKT, P], BF16)
            sets.append(d)
        PT = [ar.alloc(f"PT{i}", [P, 1024], BF16) for i in range(3)]
        r0 = ar.alloc("r0", [P, 512], F32)
        r1 = ar.alloc("r1", [P, 512], F32)
        o_f = ar.alloc("o_f", [P, 512], F32)
        t_f = ar.alloc("t_f", [P, 512], F32)
        lnv = ar.alloc("m_lnv", [P, 512], F32)
        rsv = ar.alloc("m_rsv", [P, 512], F32)
        sqo = ar.alloc("sqo", [P, 512], BF16)
        ost = [ar.alloc(f"ost{i}", [P, 512], BF16) for i in range(2)]

        def load_head(h, s):
            d = sets[s]
            pairs = []
            keys = []
            for c in range(2):
                pairs.append((d["KT"][c][0:64, :], kT_s[h, c * 64:(c + 1) * 64, :]))
                pairs.append((d["QA"][c][0:64, :], qT_s[h, c * 64:(c + 1) * 64, :]))
                pairs.append((d["QB"][c][0:64, :], qT_s[h, c * 64:(c + 1) * 64, :]))
                pairs.append((d["QA"][c][64:68, :], qaug_b[h, 0]))
                pairs.append((d["QB"][c][64:68, :], qaug_b[h, 1]))
            pairs.append((d["V"][:, :, :], v_s[h]))
            tr.dma("sp", pairs, dsem(("head", s)), writes=[("head", s)])

        tr.dma("pool", [(sets[0]["KT"][c][64:68, :], kaug_d) for c in range(2)], dsem("kaug"),
               writes=[("head", 0)])
        load_head(0, 0)

        tr.dma("pool", [(sets[1]["KT"][c][64:68, :], kaug_d) for c in range(2)], dsem("kaug"),
               writes=[("head", 1)])

        for pc in conv_pieces(l):
            convert(pc[0], pc[1], pc[2])

        PV = [4, 5]
        LS = [6, 7]
        for h in range(NH):
            s = h % 2
            d = sets[s]
            hk = ("head", s)
            if h + 1 < NH:
                load_head(h + 1, 1 - s)
            for qb in range(NQB):
                q0 = qb * 512
                kts = [kt for kt in range(NKT) if not skip_tile(h, qb, kt)]
                n = len(kts)

                def qk(i):
                    kt = kts[i]
                    k0 = kt * P
                    pr = i % 2
                    for c in range(2):
                        b = 2 * pr + c
                        KTc, QAc, QBc = d["KT"][c], d["QA"][c], d["QB"][c]
                        if k0 + P <= q0:
                            mm(bank(b), KTc[0:68, k0:k0 + P], QAc[0:68, q0:q0 + 512], True, True, [hk], [bkey(b)])
                        elif k0 >= q0 + 512:
                            mm(bank(b), KTc[0:68, k0:k0 + P], QBc[0:68, q0:q0 + 512], True, True, [hk], [bkey(b)])
                        else:
                            js = (k0 - q0) // P
                            if js > 0:
                                mm(bank(b)[:, 0:js * P], KTc[0:68, k0:k0 + P], QBc[0:68, q0:q0 + js * P],
                                   True, True, [hk], [bkey(b)])
                            mm(bank(b)[:, js * P:(js + 1) * P], KTc[0:64, k0:k0 + P],
                               QAc[0:64, q0 + js * P:q0 + (js + 1) * P], True, False, [hk], [bkey(b)])
                            mm(bank(b)[:, js * P:(js + 1) * P], ident_b[:, :], diagB[:, h, :], False, True,
                               ["ident_b", "diagB"], [bkey(b)])
                            if js < 3:
                                mm(bank(b)[:, (js + 1) * P:512], KTc[0:68, k0:k0 + P],
                                   QAc[0:68, q0 + (js + 1) * P:q0 + 512], True, True, [hk], [bkey(b)])

                def expav(i):
                    kt = kts[i]
                    pr = i % 2
                    pi = i % 3
                    act(PT[pi][:, :], pst[pr][:, :], AF.Exp, [bkey(2 * pr), bkey(2 * pr + 1)], [("PT", pi)],
                        scale=0.125)
                    first, last = (i == 0), (i == n - 1)
                    if h == 0 and qb == 0 and l == 0:
                        dbg_dump(f"PT{i}", PT[pi][:, :], [P, 1024], BF16, [("PT", pi)])
                    for c in range(2):
                        mm(bank(PV[c]), d["V"][:, kt, :], PT[pi][:, c * 512:(c + 1) * 512], first, last,
                           [hk, ("PT", pi)], [bkey(PV[c])])
                    for c in range(2):
                        mm(bank(LS[c]), ones_b[:, :], PT[pi][:, c * 512:(c + 1) * 512], first, last,
                           ["ones_b", ("PT", pi)], [bkey(LS[c])])

                qk(0)
                for i in range(n):
                    if i + 1 < n:
                        qk(i + 1)
                    expav(i)
                cp("act", t_f[:, :], bank(PV[1]), [bkey(PV[1])], ["t_f"])
                cp("dve", o_f[:, :], bank(PV[0]), [bkey(PV[0])], ["o_f"])
                cp("dve", r0[:, :], bank(LS[0]), [bkey(LS[0])], ["r0"])
                cp("dve", r1[:, :], bank(LS[1]), [bkey(LS[1])], ["r1"])
                recip(r0[:, :], r0[:, :], ["r0"], ["r0"])
                recip(r1[:, :], r1[:, :], ["r1"], ["r1"])
                tt("dve", o_f[:, :], o_f[:, :], r0[:, :], ALU.mult, ["o_f", "r0"], ["o_f"])
                tt("dve", t_f[:, :], t_f[:, :], r1[:, :], ALU.mult, ["t_f", "r1"], ["t_f"])
                oi = rot("ost", 2)
                stt("dve", ost[oi][:, :], t_f[:, :], neglam_t[:, l:l + 1], o_f[:, :], ALU.mult, ALU.add,
                    ["t_f", "o_f"], [("ost", oi)])
                tr.dma("sp", [(mixT_s[h, :, q0:q0 + 512], ost[oi][:, :])], dsem(("ost", oi)), reads=[("ost", oi)])
        tr.barrier()
        ar.release(m0)

    def conv_pieces(l):
        pcs = [("out", l, range(0, 4))]
        for q4 in range(4):
            pcs.append(("up", l, range(q4 * 4, q4 * 4 + 4)))
            pcs.append(("down", l, range(q4 * 4, q4 * 4 + 4)))
        if l + 1 < L:
            pcs.append(("in", l + 1, range(0, 5)))
            pcs.append(("in", l + 1, range(5, 10)))
            pcs.append(("kv", l + 1, range(0, 2)))
        return pcs

    for l in range(L):
        phase_kv(l)
        phase_p(l)
        phase_m(l)
    phase_p(L)
    tr.barrier()
    assert wpos[0] == len(wseq), (wpos[0], len(wseq))

    tr.finalize(engsem)
    with stack:
        with nc.Block() as block:
            @block.tensor
            def _(e):
                tr.replay("pe", e)

            @block.scalar
            def _(e):
                tr.replay("act", e)

            @block.vector
            def _(e):
                tr.replay("dve", e)

            @block.gpsimd
            def _(e):
                tr.replay("pool", e)

            @block.sync
            def _(e):
                tr.replay("sp", e)
    return nc


def make_consts(S):
    ident = np.eye(P, dtype=np.float32)
    pos = np.arange(S)
    hi = (pos // P) * P
    lo = pos % P
    kaug = np.stack([np.ones(S), np.ones(S), hi, lo]).astype(np.float32)
    qaug = np.zeros((NH, 2, 4, S), np.float32)
    diagb = np.zeros((NH, P, P), np.float32)
    ii = np.arange(P)
    for h in range(NH):
        m8 = 8.0 * 2.0 ** (-(h + 1))
        qaug[h, 0] = np.stack([-m8 * hi, -m8 * lo, m8 * np.ones(S), m8 * np.ones(S)])
        qaug[h, 1] = -qaug[h, 0]
        diagb[h] = -m8 * np.abs(ii[:, None] - ii[None, :])
    return {"c_ident": ident, "c_kaug": kaug, "c_qaug": qaug, "c_diagb": diagb}


_W_KEYS = ("w_in", "w_mem_kv", "w_out", "w_up", "w_down")


def pack_vecs(inputs, L):
    f = lambda k: np.asarray(inputs[k], dtype=np.float32)[:L]
    cols = []
    for k in ("g_mix", "g_mlp", "g_mem"):
        cols.append(f(k).reshape(L, KC, P).transpose(2, 0, 1).reshape(P, L * KC))
    for k in ("g_q_diff", "g_k_diff"):
        cols.append(np.concatenate([f(k).T, f(k).T], axis=0))
    for k in ("g_subln", "g_q_mem", "g_k_mem"):
        cols.append(f(k).T)
    cols.append(f("conv_w").reshape(L, 3, 4, P).transpose(3, 0, 1, 2).reshape(P, L * 12))
    cols.append(f("g_conv_out").reshape(L, 4, P).transpose(2, 0, 1).reshape(P, L * 4))
    return np.ascontiguousarray(np.concatenate(cols, axis=1))
_LAM_KEYS = ("lam_q1", "lam_k1", "lam_q2", "lam_k2")


def make_in_maps(inputs, S, L, ncores):
    shared = {k: np.ascontiguousarray(np.asarray(inputs[k], dtype=np.float32)[:L]) for k in _W_KEYS}
    for k in _LAM_KEYS:
        shared[k] = np.ascontiguousarray(np.asarray(inputs[k], dtype=np.float32)[:L].reshape(1, L * 64))
    shared["vecs"] = pack_vecs(inputs, L)
    shared.update(make_consts(S))
    x = np.asarray(inputs["x"], dtype=np.float32)
    mem = np.asarray(inputs["mem"], dtype=np.float32)
    maps = []
    for c in range(ncores):
        m = dict(shared)
        m["x"] = np.ascontiguousarray(x[c, :S])
        m["mem"] = np.ascontiguousarray(mem[c])
        maps.append(m)
    return maps


SKIP_THRESH = 60.0


def kernel(**inputs):
    x = inputs["x"]
    B, S, _ = x.shape
    L = inputs["w_in"].shape[0]
    nc = build_program(S, L, SKIP_THRESH)
    in_maps = make_in_maps(inputs, S, L, B)
    res = run_bass_kernel_spmd(nc, in_maps, core_ids=list(range(B)))
    return np.stack([np.asarray(r["out"], dtype=np.float32) for r in res.results], axis=0)
```
